# Optimizing a Trainium2 kernel written in Bass

```python
import jax
import jax.numpy as jnp
from jax import lax
import numpy as np

D_MODEL = 2048
BATCH = 4
SEQ = 2048
DEPTH = 2
DEC_BATCH = 32
DEC_SEQ = 1
PAST_LEN = 16384
PAGE_SIZE = 128

HEAD_DIM = 64
ATTN_WIDTH = D_MODEL // 2
RWKV_WIDTH = D_MODEL - ATTN_WIDTH
N_Q_HEADS = ATTN_WIDTH // HEAD_DIM
N_KV_HEADS = N_Q_HEADS // 4
GQA_REP = N_Q_HEADS // N_KV_HEADS
KV_WIDTH = N_KV_HEADS * HEAD_DIM
N_RWKV_HEADS = RWKV_WIDTH // HEAD_DIM
WINDOW = 128
BLOCK = WINDOW
D_DECAY_LORA = max(32, int(round(1.8 * RWKV_WIDTH ** 0.5 / 32)) * 32)
D_ICLR_LORA = max(32, int(round(1.8 * RWKV_WIDTH ** 0.5 / 32)) * 32)
D_GATE_LORA = max(32, int(round(0.6 * RWKV_WIDTH ** 0.8 / 32)) * 32)
RWKV_PROJ = 3 * RWKV_WIDTH + D_DECAY_LORA + D_ICLR_LORA + D_GATE_LORA
QKV_COLS = ATTN_WIDTH + 2 * KV_WIDTH
IN_COLS = QKV_COLS + RWKV_PROJ
ATTN_SPLITS = (ATTN_WIDTH, ATTN_WIDTH + KV_WIDTH, QKV_COLS)
RWKV_SPLITS = (RWKV_WIDTH, 2 * RWKV_WIDTH, 3 * RWKV_WIDTH, 3 * RWKV_WIDTH + D_DECAY_LORA,
               3 * RWKV_WIDTH + D_DECAY_LORA + D_ICLR_LORA)
FFN_HIDDEN = ((8 * D_MODEL + 3 * 256 - 1) // (3 * 256)) * 256
RMS_EPS = 1e-5
GN_EPS = 64e-5
NEG_INF = -1e30

kernel_name = "hymba_swa_sink_alibi_rwkv7_adaln_decode_step"


def _alibi_slopes(n):
    return jnp.exp2(-8.0 * jnp.arange(1, n + 1, dtype=jnp.float32) / n)


def _rmsnorm(x, g):
    xf = x.astype(jnp.float32)
    y = xf * lax.rsqrt(jnp.mean(xf * xf, axis=-1, keepdims=True) + RMS_EPS)
    return (y * g.astype(jnp.float32)).astype(x.dtype)


def _modulate(h, shift, scale):
    return h * (1.0 + scale[:, None, :]) + shift[:, None, :]


def _banded_attention(q, k, v, q_pos, k_pos, sinks):
    f32 = jnp.float32
    logits = jnp.einsum("bnqgrd,bnkgd->bngrqk", q.astype(f32), k.astype(f32)) * (HEAD_DIM ** -0.5)
    dist = q_pos[:, :, None] - k_pos[:, None, :]
    valid = (dist >= 0) & (dist <= WINDOW) & (k_pos[:, None, :] >= 0)
    slopes = _alibi_slopes(N_Q_HEADS).reshape(N_KV_HEADS, GQA_REP)
    logits = logits - slopes[None, None, :, :, None, None] * dist.astype(f32)[None, :, None, None]
    logits = jnp.where(valid[None, :, None, None], logits, NEG_INF)
    sink = jnp.broadcast_to(sinks.astype(f32).reshape(N_KV_HEADS, GQA_REP)[None, None, :, :, None, None],
                            logits.shape[:-1] + (1,))
    probs = jax.nn.softmax(jnp.concatenate([logits, sink], axis=-1), axis=-1)[..., :-1]
    out = jnp.einsum("bngrqk,bnkgd->bnqgrd", probs, v.astype(f32))
    return out.astype(q.dtype)


def _wkv7_scan(S0, r, w, k, v, a, b):
    def step(S, inp):
        r_t, w_t, k_t, v_t, a_t, b_t = inp
        sa = jnp.einsum("bhij,bhj->bhi", S, a_t)
        S = S * w_t[:, :, None, :] + sa[..., None] * b_t[:, :, None, :] + v_t[..., None] * k_t[:, :, None, :]
        return S, jnp.einsum("bhij,bhj->bhi", S, r_t)
    xs = tuple(jnp.swapaxes(t, 0, 1) for t in (r, w, k, v, a, b))
    S, ys = lax.scan(step, S0, xs)
    return jnp.swapaxes(ys, 0, 1), S


def _rwkv7_mix(p_cur, p_prev_row, S0, mix, w0, decay_up, a0, iclr_up, gate_up, k_k, k_a, r_k, ln_w, ln_b):
    f32 = jnp.float32
    B, T = p_cur.shape[:2]
    pc = p_cur.astype(f32)
    pp = jnp.concatenate([p_prev_row.astype(f32)[:, None], pc[:, :-1]], axis=1)
    p = pc + (pp - pc) * mix.astype(f32)
    r, k, v, wd, ad, gd = jnp.split(p, RWKV_SPLITS, axis=-1)
    w = w0.astype(f32) + jnp.tanh(wd) @ decay_up.astype(f32)
    decay = jnp.exp(-jnp.exp(-jax.nn.softplus(-w) - 0.5))
    a = jax.nn.sigmoid(a0.astype(f32) + ad @ iclr_up.astype(f32))
    g = jax.nn.sigmoid(gd) @ gate_up.astype(f32)
    hs = lambda t: t.reshape(B, T, N_RWKV_HEADS, HEAD_DIM)
    kk = hs(k * k_k.astype(f32))
    kk = kk / jnp.maximum(jnp.sqrt(jnp.sum(kk * kk, axis=-1, keepdims=True)), 1e-12)
    k = k * (1.0 + (a - 1.0) * k_a.astype(f32))
    r_h, k_h, v_h = hs(r), hs(k), hs(v)
    y, S = _wkv7_scan(S0.astype(f32), r_h, hs(decay), k_h, v_h, -kk, kk * hs(a))
    mu = jnp.mean(y, axis=-1, keepdims=True)
    var = jnp.mean(jnp.square(y - mu), axis=-1, keepdims=True)
    y = ((y - mu) * lax.rsqrt(var + GN_EPS)).reshape(B, T, RWKV_WIDTH) * ln_w.astype(f32) + ln_b.astype(f32)
    bonus = jnp.sum(r_h * k_h * r_k.astype(f32), axis=-1, keepdims=True) * v_h
    y = (y + bonus.reshape(B, T, RWKV_WIDTH)) * g
    return y.astype(p_cur.dtype), S, p_cur[:, -1]


def setup_inputs(seed: int = 0) -> dict:
    key = jax.random.key(seed)
    ks = iter(jax.random.split(key, 40))
    f32 = jnp.float32
    nrm = lambda shape, s: jax.random.normal(next(ks), shape, f32) * s
    L = DEPTH
    n_buf = min(WINDOW, PAST_LEN)
    d = {}
    d["x_prompt"] = nrm((BATCH, SEQ, D_MODEL), 1.0)
    d["x_sample"] = nrm((DEC_BATCH, DEC_SEQ, D_MODEL), 1.0)
    d["cache_k"] = nrm((L, DEC_BATCH, n_buf, N_KV_HEADS, HEAD_DIM), 1.0)
    d["cache_v"] = nrm((L, DEC_BATCH, n_buf, N_KV_HEADS, HEAD_DIM), 1.0)
    d["state_wkv"] = nrm((L, DEC_BATCH, N_RWKV_HEADS, HEAD_DIM, HEAD_DIM), 0.5)
    d["state_shift"] = nrm((L, DEC_BATCH, RWKV_PROJ), 1.0)
    d["c_prompt"] = nrm((BATCH, D_MODEL), 1.0)
    d["c_sample"] = nrm((DEC_BATCH, D_MODEL), 1.0)
    d["w_ada"] = nrm((L, D_MODEL, 6 * D_MODEL), 0.5 * D_MODEL ** -0.5)
    d["b_ada"] = nrm((L, 6 * D_MODEL), 0.02)
    d["g_norm_mix"] = 1.0 + nrm((L, D_MODEL), 0.02)
    d["g_norm_ffn"] = 1.0 + nrm((L, D_MODEL), 0.02)
    d["w_in"] = nrm((L, D_MODEL, IN_COLS), D_MODEL ** -0.5)
    d["w_out"] = nrm((L, D_MODEL, D_MODEL), D_MODEL ** -0.5)
    d["attn_sinks"] = nrm((L, N_Q_HEADS), 1.0)
    d["mix_shift"] = jax.random.uniform(next(ks), (L, RWKV_PROJ), f32)
    d["decay_w0"] = jax.random.uniform(next(ks), (L, RWKV_WIDTH), f32, -5.0, 0.0)
    d["decay_up"] = nrm((L, D_DECAY_LORA, RWKV_WIDTH), 0.1)
    d["iclr_a0"] = nrm((L, RWKV_WIDTH), 0.1)
    d["iclr_up"] = nrm((L, D_ICLR_LORA, RWKV_WIDTH), 0.1)
    d["gate_up"] = nrm((L, D_GATE_LORA, RWKV_WIDTH), D_GATE_LORA ** -0.5)
    d["k_k"] = 0.85 + nrm((L, RWKV_WIDTH), 0.02)
    d["k_a"] = 1.0 + nrm((L, RWKV_WIDTH), 0.02)
    d["r_k"] = nrm((L, N_RWKV_HEADS, HEAD_DIM), 0.1)
    d["ln_x_w"] = 1.0 + nrm((L, RWKV_WIDTH), 0.02)
    d["ln_x_b"] = nrm((L, RWKV_WIDTH), 0.02)
    d["w_ffn_in"] = nrm((L, D_MODEL, 2 * FFN_HIDDEN), D_MODEL ** -0.5)
    d["w_ffn_out"] = nrm((L, FFN_HIDDEN, D_MODEL), FFN_HIDDEN ** -0.5)
    d["g_norm_final"] = 1.0 + nrm((D_MODEL,), 0.02)
    return d


def reference(x_prompt, x_sample, cache_k, cache_v, state_wkv, state_shift, c_prompt, c_sample,
              w_ada, b_ada, g_norm_mix, g_norm_ffn, w_in, w_out, attn_sinks, mix_shift, decay_w0,
              decay_up, iclr_a0, iclr_up, gate_up, k_k, k_a, r_k, ln_x_w, ln_x_b, w_ffn_in, w_ffn_out,
              g_norm_final):

    def layer(l, x, c, buf_k, buf_v, S0, shift0):
        B, T = x.shape[:2]
        mod = jax.nn.silu(c) @ w_ada[l] + b_ada[l]
        sh1, sc1, gt1, sh2, sc2, gt2 = jnp.split(mod, 6, axis=-1)
        h = _modulate(_rmsnorm(x, g_norm_mix[l]), sh1, sc1)
        proj = h @ w_in[l]
        q, k, v, p_rwkv = jnp.split(proj, ATTN_SPLITS, axis=-1)
        q = q.reshape(B, T, N_KV_HEADS, GQA_REP, HEAD_DIM)
        k = k.reshape(B, T, N_KV_HEADS, HEAD_DIM)
        v = v.reshape(B, T, N_KV_HEADS, HEAD_DIM)
        if buf_k is None:
            nb = T // BLOCK
            kb = k.reshape(B, nb, BLOCK, N_KV_HEADS, HEAD_DIM)
            vb = v.reshape(B, nb, BLOCK, N_KV_HEADS, HEAD_DIM)
            band = lambda t: jnp.concatenate(
                [jnp.concatenate([jnp.zeros_like(t[:, :1]), t[:, :-1]], axis=1), t], axis=2)
            q_pos = jnp.arange(T).reshape(nb, BLOCK)
            k_pos = (jnp.arange(nb)[:, None] - 1) * BLOCK + jnp.arange(2 * BLOCK)[None, :]
            att = _banded_attention(q.reshape(B, nb, BLOCK, N_KV_HEADS, GQA_REP, HEAD_DIM),
                                    band(kb), band(vb), q_pos, k_pos, attn_sinks[l])
            new_k, new_v = k[:, T - WINDOW:], v[:, T - WINDOW:]
        else:
            n_buf = buf_k.shape[1]
            k_all = jnp.concatenate([buf_k.astype(k.dtype), k], axis=1)
            v_all = jnp.concatenate([buf_v.astype(v.dtype), v], axis=1)
            q_pos = (PAST_LEN + jnp.arange(T))[None]
            k_pos = (PAST_LEN - n_buf + jnp.arange(n_buf + T))[None]
            att = _banded_attention(q[:, None], k_all[:, None], v_all[:, None], q_pos, k_pos, attn_sinks[l])
            new_k, new_v = k_all[:, T:], v_all[:, T:]
        att = att.reshape(B, T, ATTN_WIDTH)
        rw, S, last_row = _rwkv7_mix(p_rwkv, shift0, S0, mix_shift[l], decay_w0[l], decay_up[l], iclr_a0[l],
                                     iclr_up[l], gate_up[l], k_k[l], k_a[l], r_k[l], ln_x_w[l], ln_x_b[l])
        x = x + gt1[:, None, :] * (jnp.concatenate([att, rw], axis=-1) @ w_out[l])
        h2 = _modulate(_rmsnorm(x, g_norm_ffn[l]), sh2, sc2)
        gate, up = jnp.split(h2 @ w_ffn_in[l], 2, axis=-1)
        x = x + gt2[:, None, :] * ((jax.nn.silu(gate) * up) @ w_ffn_out[l])
        return x, new_k, new_v, S, last_row

    xp = x_prompt
    bp = x_prompt.shape[0]
    pk, pv, pS, pR = [], [], [], []
    for l in range(DEPTH):
        xp, nk, nv, S, row = layer(l, xp, c_prompt, None, None,
                                   jnp.zeros((bp, N_RWKV_HEADS, HEAD_DIM, HEAD_DIM), jnp.float32),
                                   jnp.zeros((bp, RWKV_PROJ), x_prompt.dtype))
        pk.append(nk); pv.append(nv); pS.append(S); pR.append(row)
    y_prompt = _rmsnorm(xp, g_norm_final)

    xs = x_sample
    sk, sv, sS, sR = [], [], [], []
    for l in range(DEPTH):
        xs, nk, nv, S, row = layer(l, xs, c_sample, cache_k[l], cache_v[l], state_wkv[l], state_shift[l])
        sk.append(nk); sv.append(nv); sS.append(S); sR.append(row)
    y_sample = _rmsnorm(xs, g_norm_final)

    return (y_prompt, y_sample,
            jnp.stack(pk, 0), jnp.stack(pv, 0), jnp.stack(pS, 0), jnp.stack(pR, 0),
            jnp.stack(sk, 0), jnp.stack(sv, 0), jnp.stack(sS, 0), jnp.stack(sR, 0))
```

```python
import math
from contextlib import ExitStack
import numpy as np
import concourse.bass as bass
import concourse.mybir as mybir
from concourse.bass_utils import run_bass_kernel_spmd

F32 = mybir.dt.float32
BF16 = mybir.dt.bfloat16
ALU = mybir.AluOpType
AF = mybir.ActivationFunctionType
AX = mybir.AxisListType

D = 2048
KC = 16
SEQ = 2048
NT = 512
NCH = SEQ // NT
NS = 4
L = 2
NH = 16
HD = 64
FF = 5632
NJ = FF // 128
RP = 3360
NPC = 27
C0 = math.exp(-0.5)
RMS_EPS = 1e-5
GN_EPS = 64e-5
BIG = 1.0e9
SLOPES = [2.0 ** (-8.0 * (h + 1) / NH) for h in range(NH)]

V_GMIX, V_GFFN, V_MIX, V_W0, V_A0, V_KK, V_KA, V_RK, V_LNW, V_LNB, V_BADA = 0, 16, 32, 59, 67, 75, 83, 91, 99, 107, 115
NV = 115 + 96
C_ID, C_BLK, C_MLT, C_MUT, C_MUE, C_DM, C_DM0, C_DS = 0, 128, 256, 320, 384, 448, 704, 960
NCST = 960 + 129


class _Stop(Exception):
    pass


class Buf:
    __slots__ = ("name", "w", "r", "excl")

    def __init__(self, name, excl=False):
        self.name = name
        self.w = None
        self.r = {}
        self.excl = excl


class Sched:
    CE = ("pe", "act", "dve", "pool")

    def __init__(self, nc, ndma=24):
        self.nc = nc
        self.q = {e: [] for e in ("pe", "act", "dve", "pool", "sp")}
        self.cnt = {e: 0 for e in self.CE}
        self.seen = {e: {} for e in self.q}
        self.floor = {e: {} for e in self.q}
        self.ndma = ndma
        self.dval = [0] * ndma
        self.dnext = {"sp": 0, "pool": ndma // 2, "act": 0}

    def _need(self, r, w):
        need = {}
        for b in r:
            if b.w is not None:
                k, v = b.w
                if need.get(k, 0) < v:
                    need[k] = v
        for b in w:
            if b.w is not None:
                k, v = b.w
                if need.get(k, 0) < v:
                    need[k] = v
            for k, v in b.r.items():
                if need.get(k, 0) < v:
                    need[k] = v
        return need

    def _waits(self, eng, need):
        fl = self.floor[eng]
        if fl:
            for k, v in fl.items():
                if need.get(k, 0) < v:
                    need[k] = v
            self.floor[eng] = {}
        seen = self.seen[eng]
        out = []
        for k, v in need.items():
            if k == eng and eng == "pe":
                continue
            if seen.get(k, 0) >= v:
                continue
            seen[k] = v
            out.append((k, v))
        return out

    def _mark(self, tok, r, w):
        k, v = tok
        for b in r:
            if b.r.get(k, 0) < v:
                b.r[k] = v
        for b in w:
            b.w = tok
            b.r = {}

    def op(self, eng, meth, *args, r=(), w=(), **kw):
        if any(b.excl for b in r):
            w = list(w) + [b for b in r if b.excl]
            r = [b for b in r if not b.excl]
        waits = self._waits(eng, self._need(r, w))
        self.cnt[eng] += 1
        tok = (eng, self.cnt[eng])
        self.q[eng].append((waits, meth, args, kw, eng))
        self._mark(tok, r, w)

    def dma(self, eng, out, in_, r=(), w=(), **kw):
        need = self._need(r, w)
        i = self.dnext[eng]
        half = self.ndma // 2
        base = half if eng == "pool" else 0
        self.dnext[eng] = base + (i - base + 1) % half
        k = ("d", i)
        if self.dval[i] > 0 and need.get(k, 0) < self.dval[i]:
            need[k] = self.dval[i]
        waits = self._waits(eng, need)
        self.dval[i] += 16
        tok = (k, self.dval[i])
        kw = dict(kw)
        kw["out"] = out
        kw["in_"] = in_
        self.q[eng].append((waits, "dma_start", (), kw, k))
        self._mark(tok, r, w)

    def barrier(self):
        for e in self.q:
            fl = self.floor[e]
            for c in self.CE:
                if self.cnt[c] > 0 and fl.get(c, 0) < self.cnt[c]:
                    fl[c] = self.cnt[c]
            for i in range(self.ndma // 2):
                if self.dval[i] > 0:
                    fl[("d", i)] = self.dval[i]

    def emit(self, es):
        nc = self.nc
        sems = {e: es.enter_context(nc.semaphore("s_" + e)) for e in self.CE}
        dsems = [es.enter_context(nc.semaphore("d%d" % i)) for i in range(self.ndma)]
        block = es.enter_context(nc.Block())

        def semof(k):
            return sems[k] if isinstance(k, str) else dsems[k[1]]

        def run(name, final=False):
            def f(e):
                for waits, meth, args, kw, inc in self.q[name]:
                    for k, v in waits:
                        e.wait_ge(semof(k), v)
                    ins = getattr(e, meth)(*args, **kw)
                    if isinstance(inc, str):
                        ins.then_inc(sems[inc], 1)
                    else:
                        ins.then_inc(dsems[inc[1]], 16)
                if final:
                    for i in range(self.ndma):
                        if self.dval[i] > 0:
                            e.wait_ge(dsems[i], self.dval[i])
                    for c in self.CE:
                        if self.cnt[c] > 0:
                            e.wait_ge(sems[c], self.cnt[c])
            return f

        block.tensor(run("pe"))
        block.scalar(run("act"))
        block.vector(run("dve"))
        block.gpsimd(run("pool"))
        block.sync(run("sp", final=True))


def build(cfg=None):
    cfg = cfg or {}
    n_layers = cfg.get("layers", L)
    n_chunks = cfg.get("chunks", NCH)
    do_sample = cfg.get("sample", True)
    taps = cfg.get("taps", ())
    nc = bass.Bass("TRN2", target_bir_lowering=False)
    S = Sched(nc)
    es = ExitStack()

    def din(name, shape, dt=F32):
        return nc.dram_tensor(name, list(shape), dt, kind="ExternalInput").ap()

    def dout(name, shape, dt=F32):
        return nc.dram_tensor(name, list(shape), dt, kind="ExternalOutput").ap()

    xpT = din("xpT", [KC, 128, SEQ])
    xsT = din("xsT", [128, KC, NS])
    c5T = din("c5T", [128, KC, 5])
    vecs_d = din("vecs", [128, L, NV])
    sinks_d = din("sinks", [128, L, NH])
    cst_d = din("cst", [128, NCST])
    ckT_d = din("ckT", [L, NS, 128, 8, 128])
    cv_d = din("cv", [L, NS, 128, 256])
    ck_d = din("ck", [L, NS, 128, 256])
    stT_d = din("stT", [L, NS, 128, 8, 64])
    shT_d = din("shT", [L, 128, NPC, NS])
    wada_d = din("wada", [L, 48, 128, 4096])
    win_d = din("win", [L, 22, 128, 4096])
    wkvt_d = din("wkvt", [L, 2, 128, 4096])
    wout_d = din("wout", [L, 8, 128, 4096])
    wffi_d = din("wffi", [L, 44, 128, 4096])
    wffo_d = din("wffo", [L, 32, 128, 2816])
    lup_d = din("lup", [L, 128, 4096])
    gfin_d = din("gfin", [128, KC])

    y_p = dout("y_p", [SEQ, D])
    y_s = dout("y_s", [NS, D])
    nk_p = dout("nk_p", [L, 128, 256])
    nv_p = dout("nv_p", [L, 128, 256])
    nwkv_p = dout("nwkv_p", [L, NH, 64, 64])
    nsh_p = dout("nsh_p", [L, NPC, 128])
    nk_s = dout("nk_s", [L, NS, 128, 256])
    nv_s = dout("nv_s", [L, NS, 128, 256])
    nwkv_s = dout("nwkv_s", [L, NS, NH, 64, 64])
    nsh_s = dout("nsh_s", [L, NS, NPC, 128])
    dram_out_buf = Buf("dram_out")

    def sb(name, shape, dt=F32):
        return es.enter_context(nc.sbuf_tensor("sb_" + name, list(shape), dt))

    def carve(reg, off, shape, dt=F32, parts=128):
        n = 1
        for d_ in shape[1:]:
            n *= d_
        nf = n if dt == F32 else (n + 1) // 2
        v = reg[0:parts, off:off + nf]
        if dt != F32:
            v = v.bitcast(dt)[:, 0:n]
        if len(shape) == 3:
            v = v.rearrange("p (a b) -> p a b", a=shape[1])
        return v, off + nf

    xT = sb("xT", [128, KC, NT])
    xTb = [Buf("xT%d" % k) for k in range(KC)]
    R2 = sb("R2", [128, 4096])
    hT, _ = carve(R2, 0, [128, KC, NT], BF16)
    hTb = [Buf("hT%d" % k) for k in range(KC)]
    NWS = 3
    wsl = [sb("wsl%d" % i, [128, 4096], BF16) for i in range(NWS)]
    wslb = [Buf("wsl%d" % i) for i in range(NWS)]
    lupb = sb("lupb", [128, 4096], BF16)
    lupB = Buf("lup")
    cst = sb("cst", [128, NCST])
    cstB = Buf("cst")
    cbf = sb("cbf", [128, 384], BF16)
    ones_f = sb("ones_f", [128, 64])
    vecs = sb("vecs", [128, L, NV])
    vecB = Buf("vecs")
    omk = sb("omk", [128, L, 8])
    sinks = sb("sinks", [128, L, NH])
    modT = sb("modT", [128, L, 96, 5])
    modB = Buf("mod")
    gfin = sb("gfin", [128, KC])
    c5 = sb("c5", [128, KC, 5])
    scT = sb("scT", [128, KC, 5], BF16)
    smallc = sb("smallc", [128, 4])
    Sst = sb("Sst", [128, L, 8, 64])
    SstB = [Buf("Sst%d" % l) for l in range(L)]
    Scur = sb("Scur", [128, 8, 64])
    Sbf = sb("Sbf", [128, 8, 64], BF16)
    ScurB = Buf("Scur")
    SbfB = Buf("Sbf")
    prevrow = sb("prevrow", [128, L, NPC], BF16)
    prevrowB = [Buf("prow%d" % l) for l in range(L)]
    prevK = sb("prevK", [128, L, 8, 128], BF16)
    prevKB = [Buf("pK%d" % l) for l in range(L)]
    prevV = sb("prevV", [128, L, 256], BF16)
    prevVB = [Buf("pV%d" % l) for l in range(L)]
    lastrow = sb("lastrow", [128, NPC, NS])
    lastrowB = Buf("lastrow")
    shs = sb("shs", [128, NPC, NS])
    shsB = Buf("shs")
    vnew = sb("vnew", [1, NS, 256], BF16)
    vnewB = Buf("vnew")
    kdTs = sb("kdTs", [128, 8, NS], BF16)
    kdTsB = Buf("kdTs")
    sqb = sb("sqb", [128, 2, NT], BF16)
    sqB = [Buf("sq0"), Buf("sq1")]
    rstd = sb("rstd", [128, NT])
    rstdB = Buf("rstd")
    tmpn = sb("tmpn", [128, 2, NT])
    tmpnB = [Buf("tmpn0"), Buf("tmpn1")]
    ftmp = tmpn
    ftmpB = tmpnB
    otok = sb("otok", [128, D])
    otokB = Buf("otok")
    kvout = otok[:, 0:512].rearrange("p (a b) -> p a b", a=2)
    kvoutB = otokB
    kvnew_f = otok[0:1, 0:2048].rearrange("p (a s c) -> p a s c", a=2, s=NS)
    kvnewB = otokB
    R1 = sb("R1", [128, 11264])
    pc, o1 = carve(R1, 0, [128, NPC, NT + 2], BF16)
    pcB = [Buf("pc%d" % c) for c in range(NPC)]
    yT, o1 = carve(R1, o1, [128, KC, NT], BF16)
    yTb = [Buf("yT%d" % k) for k in range(KC)]
    assert o1 <= 11264
    aT, _ = carve(R1, 0, [128, NJ, NT], BF16)
    aTb = [Buf("aT%d" % j) for j in range(NJ)]
    R3N = 9632
    R3 = sb("R3", [128, R3N])
    o = 0
    qT, o = carve(R3, o, [128, 8, NT], BF16)
    qTb = [Buf("qT%d" % k) for k in range(8)]
    kdT, o = carve(R3, o, [128, 8, 128 + NT], BF16)
    kdTb = [Buf("kdT%d" % k) for k in range(8)]
    vtok, o = carve(R3, o, [128, 5, 256], BF16)
    vtokB = [Buf("vtok%d" % k) for k in range(5)]
    lg, o = carve(R3, o, [128, 4, 256])
    lgB = Buf("lg")
    pbt2, pbB2, pTt2, pTtf2, pTB2 = [], [], [], [], []
    for i_ in range(2):
        t_, o = carve(R3, o, [128, 4, 256], BF16)
        pbt2.append(t_)
        pbB2.append(Buf("pb%d" % i_))
        tf_, _ = carve(R3, o, [128, 512])
        t_, o = carve(R3, o, [128, 8, 128], BF16)
        pTt2.append(t_)
        pTtf2.append(tf_)
        pTB2.append(Buf("pT%d" % i_))
    attn_state = {"i": 0}
    ast, o = carve(R3, o, [128, 6, 4])
    astB = Buf("ast")
    ksT, o = carve(R3, o, [128, 8, 130], BF16)
    ksTB = Buf("ksT")
    cvb, o = carve(R3, o, [128, 256], BF16)
    cvbB = Buf("cvb")
    assert o <= R3N, o
    o = 0
    psb, o = carve(R3, o, [128, NPC, 64])
    psbB = Buf("psb")
    o_psb_end = o
    TT = []
    for i in range(8):
        t_, _ = carve(R2, i * 512, [128, 8, 64])
        TT.append(t_)
    TTB = [Buf("TT%d" % i) for i in range(8)]
    OPN = ["rte", "rto", "ate", "ato", "bt", "kt", "bh", "kh", "vb"]
    opb, opB = {}, {}
    ar2f, o = carve(R3, o, [128, 2048], BF16)
    ar2 = ar2f.rearrange("p (m q w t) -> p m q w t", m=8, q=2, w=2)
    opb["ate"], opb["ato"] = ar2[:, :, 0, 0, :], ar2[:, :, 1, 0, :]
    opb["rte"], opb["rto"] = ar2[:, :, 0, 1, :], ar2[:, :, 1, 1, :]
    for n in OPN:
        if n not in opb:
            opb[n], o = carve(R3, o, [128, 8, 64], BF16)
        opB[n] = Buf("o_" + n)
    tkm, tkm_f, tkmB = {}, {}, {}
    for n in ("v", "bh", "kh"):
        tkm_f[n], _ = carve(R3, o, [64, 512], F32, parts=64)
        tkm[n], o = carve(R3, o, [64, 8, 128], BF16, parts=64)
        tkmB[n] = Buf("k_" + n)
    ysb, o = carve(R3, o, [64, NH, 64], F32, parts=64)
    ysbB = Buf("ysb")
    stout, stoutB = ysb, ysbB
    ytmp, o = carve(R3, o, [64, NH, 64], F32, parts=64)
    ytmpB = Buf("ytmp")
    zt, o = carve(R3, o, [128, 8, 64])
    ztB = Buf("zt")
    bon, o = carve(R3, o, [128, 8, 64])
    bonB = Buf("bon")
    gg, o = carve(R3, o, [128, 8, 64])
    ggB = Buf("gg")
    tl, o = carve(R3, o, [128, 64], BF16)
    sg, o = carve(R3, o, [128, 2, 64], BF16)
    tmpb, o = carve(R3, o, [128, 8, 64], BF16)
    tlB, sgB, tmpbB = Buf("tl"), Buf("sg"), Buf("tmpb")
    pcol, o = carve(R3, o, [128, 8])
    pcolS, o = carve(R3, o, [128, 8, NS])
    pcolB = Buf("pcol")
    yst, o = carve(R3, o, [64, 4, NH], F32, parts=64)
    ystB = Buf("yst")
    assert o <= R3N, o
    MATN = ["M0", "N0", "M1", "N1", "IM", "Ak", "Rb", "Rk", "Pa", "Pb", "RH"]
    mat, matB = {}, {}
    for i, n in enumerate(MATN):
        if i < 8:
            mat[n], _ = carve(R2, i * 512, [64, NH, 64], BF16, parts=64)
        else:
            mat[n], _ = carve(R3, (i - 8) * 512, [64, NH, 64], BF16, parts=64)
        matB[n] = Buf("m_" + n)
    assert 3 * 512 <= o_psb_end
    mat["U"], matB["U"] = mat["Ak"], matB["Ak"]
    ps = es.enter_context(nc.psum_tensor("ps", [128, 8, 512], F32))
    psB = [Buf("ps%d" % b, excl=True) for b in range(8)]
    st = {"p1": 0, "p2": 0, "ws": 0}

    def PS1():
        b = st["p1"]
        st["p1"] = (b + 1) % 4
        return ps[:, b, :], psB[b]

    def PS2():
        k = st["p2"]
        st["p2"] = (k + 1) % 2
        b = 4 + 2 * k
        return ps[:, b:b + 2, :].rearrange("p a b -> p (a b)"), [psB[b], psB[b + 1]]

    ident_f = cst[:, C_ID:C_ID + 128]
    ident_b = cbf[:, 0:128]
    blk_b = cbf[:, 128:256]
    ones_b = cbf[:, 256:384]

    op = S.op

    def bc(ap, shape):
        return ap.to_broadcast(list(shape))

    def wload(src, ncols):
        i = st["ws"]
        st["ws"] = (i + 1) % NWS
        S.dma("pool", wsl[i][:, 0:ncols], src, r=(), w=[wslb[i]], max_dma_last_dim=8192)
        return wsl[i], wslb[i]

    class WStream:
        def __init__(self, items, depth=2):
            self.items = items
            self.depth = depth
            self.issued = []
            self.pos = 0

        def _issue(self):
            k = len(self.issued)
            if k < len(self.items):
                src, ncols, _ = self.items[k]
                self.issued.append(wload(src, ncols))

        def get(self, tag):
            while len(self.issued) < min(len(self.items), self.pos + 1 + self.depth):
                self._issue()
            assert self.items[self.pos][2] == tag, (self.items[self.pos][2], tag)
            t = self.issued[self.pos]
            self.pos += 1
            return t

    passes = []
    for ci in range(n_chunks):
        for l in range(n_layers):
            passes.append(("p", ci, l))
    if do_sample:
        for l in range(n_layers):
            passes.append(("s", 0, l))

    items = []
    for l in range(n_layers):
        for g in range(48):
            items.append((wada_d[l, g], 4096, ("ada", l, g)))
    for (kind, ci, l) in passes:
        for g in range(22):
            items.append((win_d[l, g], 4096, ("in", l, g)))
        for g in range(2):
            items.append((wkvt_d[l, g], 4096, ("kv", l, g)))
        for g in range(8):
            items.append((wout_d[l, g], 4096, ("out", l, g)))
        for g in range(44):
            items.append((wffi_d[l, g], 4096, ("ffi", l, g)))
        for g in range(32):
            items.append((wffo_d[l, g], 2816, ("ffo", l, g)))
    WS = WStream(items)

    def checkpoint(name):
        if cfg.get("stop") == name:
            raise _Stop()

    def tap(name, ap, bufs, dt=F32):
        if name in taps:
            d = dout("tap_" + name, list(ap.shape), dt)
            S.dma("sp", d, ap, r=bufs, w=[dram_out_buf])

    S.dma("sp", cst[:], cst_d, w=[cstB])
    S.dma("sp", vecs[:], vecs_d, w=[vecB])
    S.dma("sp", sinks[:], sinks_d, w=[vecB])
    S.dma("sp", c5[:], c5T, w=[modB])
    S.dma("sp", gfin[:], gfin_d, w=[vecB])
    op("dve", "tensor_copy", cbf[:, 0:256], cst[:, C_ID:C_ID + 256], r=[cstB], w=[cstB])
    op("dve", "memset", cbf[:, 256:384], 1.0, w=[cstB])
    op("dve", "memset", smallc[:, 0:1], RMS_EPS, w=[cstB])
    op("dve", "memset", smallc[:, 1:2], GN_EPS, w=[cstB])
    op("dve", "memset", smallc[:, 2:3], 0.0, w=[cstB])
    op("dve", "memset", smallc[:, 3:4], 1.0, w=[cstB])
    op("dve", "memset", ones_f[:], 1.0, w=[cstB])
    for l in range(L):
        op("dve", "tensor_scalar", omk[:, l, :], vecs[:, l, V_KA:V_KA + 8], -1.0, 1.0, ALU.mult, ALU.add,
           r=[vecB], w=[vecB])
        op("dve", "memset", Sst[:, l], 0.0, w=[SstB[l]])
        op("dve", "memset", prevrow[:, l], 0.0, w=[prevrowB[l]])
        op("dve", "memset", prevK[:, l], 0.0, w=[prevKB[l]])
        op("dve", "memset", prevV[:, l], 0.0, w=[prevVB[l]])

    stopped = False
    try:
        tap("scT", c5[:], [modB])
        checkpoint("consts")
        op("act", "activation", scT[:], c5[:], AF.Silu, r=[modB], w=[modB])
        for l in range(n_layers):
            pm, pmB = PS1()
            for g2 in range(48):
                if cfg.get("stop") == "ada1" and g2 == cfg.get("ngrp", 1):
                    raise _Stop()
                wt, wb = WS.get(("ada", l, g2))
                wv = wt[:, :].rearrange("p (g k c) -> p g k c", g=2, k=KC)
                for g in range(2):
                    m = g2 * 2 + g
                    for kc in range(KC):
                        op("pe", "matmul", pm[:, m * 5:m * 5 + 5], wv[:, g, kc, :], scT[:, kc, :],
                           start=(kc == 0), stop=(kc == KC - 1), r=[wb, modB], w=[pmB])
            checkpoint("ada_mm")
            pm3 = pm[:, 0:480].rearrange("p (m s) -> p m s", s=5)
            op("dve", "tensor_tensor", modT[:, l], pm3, bc(vecs[:, l, V_BADA:V_BADA + 96].unsqueeze(2), [128, 96, 5]),
               ALU.add, r=[pmB, vecB], w=[modB])
            checkpoint("ada_tt")
            for (lo, voff) in ((16, V_GMIX), (64, V_GFFN)):
                op("dve", "tensor_scalar", modT[:, l, lo:lo + 16, :], modT[:, l, lo:lo + 16, :], 1.0, None, ALU.add,
                   r=[modB], w=[modB])
                op("dve", "tensor_tensor", modT[:, l, lo:lo + 16, :], modT[:, l, lo:lo + 16, :],
                   bc(vecs[:, l, voff:voff + 16].unsqueeze(2), [128, 16, 5]), ALU.mult, r=[vecB, modB], w=[modB])
    except _Stop:
        stopped = True
    tap("modT", modT[:, 0:n_layers], [modB])


    def modap(l, sec, kc, N, sample):
        if sample:
            return modT[:, l, sec * 16 + kc, 1:1 + NS]
        return bc(modT[:, l, sec * 16 + kc, 0:1], [128, N])

    def norm_phase(N, sample, A_of, B_of, out_of, outB_of, out_dt_bf16=True):
        pss, pssB = PS1()
        for kc in range(KC):
            i = kc % 2
            op("act", "activation", sqb[:, i, :N], xT[:, kc, :N], AF.Square, r=[xTb[kc]], w=[sqB[i]])
            op("pe", "matmul", pss[:, :N], ones_b, sqb[:, i, :N], start=(kc == 0), stop=(kc == KC - 1),
               r=[sqB[i], cstB], w=[pssB])
        op("act", "activation", rstd[:, :N], pss[:, :N], AF.Sqrt, bias=smallc[:, 0:1], scale=1.0 / D,
           r=[pssB, cstB], w=[rstdB])
        op("dve", "reciprocal", rstd[:, :N], rstd[:, :N], r=[rstdB], w=[rstdB])
        checkpoint("norm_a")
        for kc in range(KC):
            i = kc % 2
            op("dve", "tensor_tensor", tmpn[:, i, :N], xT[:, kc, :N], rstd[:, :N], ALU.mult,
               r=[xTb[kc], rstdB], w=[tmpnB[i]])
            A, B = A_of(kc), B_of(kc)
            if B is None:
                op("dve", "tensor_tensor", out_of(kc), tmpn[:, i, :N], A, ALU.mult,
                   r=[tmpnB[i], modB, vecB], w=[outB_of(kc)])
            else:
                op("pool", "tensor_tensor", tmpn[:, i, :N], tmpn[:, i, :N], A, ALU.mult,
                   r=[tmpnB[i], modB], w=[tmpnB[i]])
                op("dve", "tensor_tensor", out_of(kc), tmpn[:, i, :N], B, ALU.add,
                   r=[tmpnB[i], modB], w=[outB_of(kc)])

    def resid_add(l, sec, N, sample, mo, psap, pB):
        G = modap(l, sec, mo, N, sample)
        i = mo % 2
        op("dve", "tensor_tensor", ftmp[:, i, :N], psap, G, ALU.mult, r=[pB, modB], w=[ftmpB[i]])
        op("pool", "tensor_tensor", xT[:, mo, :N], xT[:, mo, :N], ftmp[:, i, :N], ALU.add,
           r=[ftmpB[i], xTb[mo]], w=[xTb[mo]])

    def attn_group(l, g, Q, nk, q_of, k_of, dm, pv_terms, out_ap, out_bufs, rbufs):
        ai = attn_state["i"]
        attn_state["i"] = 1 - ai
        pbt, pbB, pTt, pTt_f, pTB = pbt2[ai], pbB2[ai], pTt2[ai], pTtf2[ai], pTB2[ai]
        p2, p2B = PS2()
        lgv = p2[:, :].rearrange("p (r k) -> p r k", r=4)
        for r in range(4):
            h = 4 * g + r
            op("pe", "matmul", lgv[0:Q, r, 0:nk], q_of(h), k_of(h), start=True, stop=True, r=rbufs, w=p2B)
        for r in range(4):
            h = 4 * g + r
            op("dve", "tensor_scalar", lg[0:Q, r, 0:nk], dm, -SLOPES[h], None, ALU.mult, r=[cstB], w=[lgB])
        checkpoint("att_mm")
        op("dve", "tensor_tensor", lg[0:Q, :, 0:nk], lg[0:Q, :, 0:nk], lgv[0:Q, :, 0:nk], ALU.add, r=p2B + [lgB], w=[lgB])
        checkpoint("att_lg")
        mx, rs, sd, den = ast[0:Q, 0, :], ast[0:Q, 1, :], ast[0:Q, 2, :], ast[0:Q, 3, :]
        sk = sinks[0:Q, l, 4 * g:4 * g + 4]
        op("dve", "tensor_reduce", mx, lg[0:Q, :, 0:nk], AX.X, ALU.max, r=[lgB], w=[astB])
        op("dve", "tensor_tensor", mx, mx, sk, ALU.max, r=[astB, vecB], w=[astB])
        op("dve", "tensor_tensor", lg[0:Q, :, 0:nk], lg[0:Q, :, 0:nk], bc(mx.unsqueeze(2), [Q, 4, nk]), ALU.subtract,
           r=[lgB, astB], w=[lgB])
        op("act", "activation", lg[0:Q, :, 0:nk], lg[0:Q, :, 0:nk], AF.Exp, r=[lgB], w=[lgB])
        op("dve", "tensor_reduce", rs, lg[0:Q, :, 0:nk], AX.X, ALU.add, r=[lgB], w=[astB])
        op("dve", "tensor_tensor", sd, sk, mx, ALU.subtract, r=[astB, vecB], w=[astB])
        op("act", "activation", sd, sd, AF.Exp, r=[astB], w=[astB])
        op("dve", "tensor_tensor", den, rs, sd, ALU.add, r=[astB], w=[astB])
        op("dve", "reciprocal", den, den, r=[astB], w=[astB])
        op("dve", "tensor_tensor", pbt[0:Q, :, 0:nk], lg[0:Q, :, 0:nk], bc(den.unsqueeze(2), [Q, 4, nk]), ALU.mult,
           r=[lgB, astB], w=[pbB])
        checkpoint("att_sm")
        pt1, pt1B = PS1()
        ptb = pt1.bitcast(BF16)
        if Q == 128:
            ptv = ptb.rearrange("p (a q) -> p a q", q=128)
            for r in range(4):
                for kb in range(2):
                    op("pe", "transpose", ptv[:, r * 2 + kb, :], pbt[:, r, kb * 128:(kb + 1) * 128], ident_b,
                       r=[pbB, cstB], w=[pt1B])
            op("dve", "tensor_copy", pTt_f[:, :], pt1[:, :], r=[pt1B], w=[pTB])
        else:
            for r in range(4):
                op("pe", "transpose", ptb[:, 2 * r:2 * r + 1], pbt[0:1, r, 0:128], ident_b[0:1, 0:1],
                   r=[pbB, cstB], w=[pt1B])
            op("dve", "tensor_copy", pTt_f[:, 0:4], pt1[:, 0:4], r=[pt1B], w=[pTB])
        checkpoint("att_T")
        po, poB = PS1()
        pov = po[:, 0:2 * Q].rearrange("p (c q) -> p c q", c=2)
        for r in range(4):
            h = 4 * g + r
            half = h % 2
            c = (h // 2) - 2 * g
            terms = pv_terms(r, pTt, pTB, pbt, pbB)
            for ti, (lt, rh, rb) in enumerate(terms):
                op("pe", "matmul", pov[half * 64:half * 64 + 64, c, :], lt, rh, start=(ti == 0),
                   stop=(ti == len(terms) - 1), r=rb, w=[poB])
        op("act", "activation", out_ap, pov, AF.Copy, r=[poB], w=out_bufs)

    def rwkv_prep(l, nt, prev_ap, cur_ap, rbufs):
        P3 = [128, NPC, nt]
        mixb = bc(vecs[:, l, V_MIX:V_MIX + NPC].unsqueeze(2), P3)
        op("dve", "tensor_tensor", psb[:, :, 0:nt], prev_ap, cur_ap, ALU.subtract, r=rbufs, w=[psbB])
        op("dve", "tensor_tensor", psb[:, :, 0:nt], psb[:, :, 0:nt], mixb, ALU.mult, r=[psbB, vecB], w=[psbB])
        op("dve", "tensor_tensor", psb[:, :, 0:nt], psb[:, :, 0:nt], cur_ap, ALU.add, r=[psbB] + rbufs, w=[psbB])
        r_, k_, v_ = psb[:, 0:8, 0:nt], psb[:, 8:16, 0:nt], psb[:, 16:24, 0:nt]
        op("act", "activation", tl[0:64, 0:nt], psb[0:64, 24, 0:nt], AF.Tanh, r=[psbB], w=[tlB])
        op("act", "activation", tl[64:128, 0:nt], psb[64:128, 24, 0:nt], AF.Copy, r=[psbB], w=[tlB])
        op("act", "activation", sg[:, :, 0:nt], psb[:, 25:27, 0:nt], AF.Sigmoid, r=[psbB], w=[sgB])
        up1w = lupb[:, 0:1024]
        up1a = lupb[:, 1024:2048]
        gup = lupb[:, 2048:4096].rearrange("p (k c) -> p k c", k=2)
        T = [t[:, :, 0:nt] for t in TT]
        B8 = [128, 8, nt]
        pw, pwB = PS1()
        pa, paB = PS1()
        pg, pgB = PS1()
        pwv = pw[:, 0:8 * nt].rearrange("p (m t) -> p m t", m=8)
        pav = pa[:, 0:8 * nt].rearrange("p (m t) -> p m t", m=8)
        pgv = pg[:, 0:8 * nt].rearrange("p (m t) -> p m t", m=8)
        for m in range(8):
            cs_ = slice(m * 128, (m + 1) * 128)
            op("pe", "matmul", pwv[:, m, :], up1w[:, cs_], tl[:, 0:nt], start=True, stop=True, r=[tlB, lupB], w=[pwB])
            op("pe", "matmul", pav[:, m, :], up1a[:, cs_], tl[:, 0:nt], start=True, stop=True, r=[tlB, lupB], w=[paB])
            op("pe", "matmul", pgv[:, m, :], gup[:, 0, cs_], sg[:, 0, 0:nt], start=True, stop=False, r=[sgB, lupB], w=[pgB])
            op("pe", "matmul", pgv[:, m, :], gup[:, 1, cs_], sg[:, 1, 0:nt], start=False, stop=True, r=[sgB, lupB], w=[pgB])

        def vb(off):
            return bc(vecs[:, l, off:off + 8].unsqueeze(2), B8)

        op("dve", "tensor_tensor", T[0], pwv, vb(V_W0), ALU.add, r=[pwB, vecB], w=[TTB[0]])
        op("act", "activation", T[0], T[0], AF.Sigmoid, r=[TTB[0]], w=[TTB[0]])
        op("dve", "tensor_tensor", T[1], pav, vb(V_A0), ALU.add, r=[paB, vecB], w=[TTB[1]])
        op("act", "activation", T[1], T[1], AF.Sigmoid, r=[TTB[1]], w=[TTB[1]])
        op("act", "activation", gg[:, :, 0:nt], pgv, AF.Copy, r=[pgB], w=[ggB])
        if nt == 64:
            for m in range(8):
                op("dve", "tensor_tensor_scan", TT[3][:, m, :], ones_f[:, :], TT[0][:, m, :], 0.0,
                   ALU.mult, ALU.add, r=[TTB[0], cstB], w=[TTB[3]])
            lastb = bc(TT[3][:, :, 63:64], B8)
        else:
            op("dve", "tensor_copy", T[3], T[0], r=[TTB[0]], w=[TTB[3]])
            lastb = T[3]
        op("act", "activation", T[4], T[3], AF.Exp, scale=-C0, r=[TTB[3]], w=[TTB[4]])
        op("dve", "tensor_tensor", T[0], T[3], T[0], ALU.subtract, r=[TTB[3], TTB[0]], w=[TTB[0]])
        op("act", "activation", T[0], T[0], AF.Exp, scale=-C0, r=[TTB[0]], w=[TTB[0]])
        op("act", "activation", T[5], T[3], AF.Exp, scale=C0, r=[TTB[3]], w=[TTB[5]])
        if nt == 64:
            op("dve", "tensor_copy", pcol[:, :], TT[4][:, :, 63], r=[TTB[4]], w=[pcolB])
            op("dve", "tensor_tensor", T[3], lastb, T[3], ALU.subtract, r=[TTB[3]], w=[TTB[3]])
            op("act", "activation", T[3], T[3], AF.Exp, scale=-C0, r=[TTB[3]], w=[TTB[3]])
        else:
            op("dve", "tensor_copy", pcolS[:, :, 0:nt], T[4], r=[TTB[4]], w=[pcolB])
            op("dve", "memset", T[3], 1.0, w=[TTB[3]])
        op("dve", "tensor_tensor", T[6], k_, vb(V_KK), ALU.mult, r=[psbB, vecB], w=[TTB[6]])
        op("dve", "tensor_tensor", tmpb[:, :, 0:nt], T[6], T[6], ALU.mult, r=[TTB[6]], w=[tmpbB])
        pq, pqB = PS1()
        pqv = pq[:, 0:8 * nt].rearrange("p (m t) -> p m t", m=8)
        op("pe", "matmul", pqv, blk_b, tmpb[:, :, 0:nt], start=True, stop=True, r=[tmpbB, cstB], w=[pqB])
        op("act", "activation", T[7], pqv, AF.Sqrt, r=[pqB], w=[TTB[7]])
        op("dve", "tensor_scalar", T[7], T[7], 1e-12, None, ALU.max, r=[TTB[7]], w=[TTB[7]])
        op("dve", "reciprocal", T[7], T[7], r=[TTB[7]], w=[TTB[7]])
        op("dve", "tensor_tensor", T[6], T[6], T[7], ALU.mult, r=[TTB[6], TTB[7]], w=[TTB[6]])
        op("dve", "tensor_tensor", T[7], T[1], vb(V_KA), ALU.mult, r=[TTB[1], vecB], w=[TTB[7]])
        op("dve", "tensor_tensor", T[7], T[7], bc(omk[:, l, :].unsqueeze(2), B8), ALU.add, r=[TTB[7], vecB], w=[TTB[7]])
        op("dve", "tensor_tensor", T[7], T[7], k_, ALU.mult, r=[TTB[7], psbB], w=[TTB[7]])
        op("dve", "tensor_tensor", k_, r_, T[7], ALU.mult, r=[psbB, TTB[7]], w=[psbB])
        op("dve", "tensor_tensor", tmpb[:, :, 0:nt], k_, vb(V_RK), ALU.mult, r=[psbB, vecB], w=[tmpbB])
        pq2, pq2B = PS1()
        pq2v = pq2[:, 0:8 * nt].rearrange("p (m t) -> p m t", m=8)
        op("pe", "matmul", pq2v, blk_b, tmpb[:, :, 0:nt], start=True, stop=True, r=[tmpbB, cstB], w=[pq2B])
        op("dve", "tensor_tensor", bon[:, :, 0:nt], pq2v, v_, ALU.mult, r=[pq2B, psbB], w=[bonB])
        O = {n: opb[n][:, :, 0:nt] for n in OPN}
        for nm_, lo in (("rte", 0), ("rto", 64)):
            op("dve", "tensor_tensor", O[nm_][lo:lo + 64], r_[lo:lo + 64], T[4][lo:lo + 64], ALU.mult,
               r=[psbB, TTB[4]], w=[opB[nm_]])
        op("dve", "tensor_tensor", T[0], T[6], T[0], ALU.mult, r=[TTB[6], TTB[0]], w=[TTB[0]])
        for nm_, lo in (("ate", 0), ("ato", 64)):
            op("dve", "tensor_scalar", O[nm_][lo:lo + 64], T[0][lo:lo + 64], -1.0, None, ALU.mult, r=[TTB[0]], w=[opB[nm_]])
        op("dve", "tensor_tensor", T[1], T[6], T[1], ALU.mult, r=[TTB[6], TTB[1]], w=[TTB[1]])
        op("dve", "tensor_tensor", O["bt"], T[1], T[5], ALU.mult, r=[TTB[1], TTB[5]], w=[opB["bt"]])
        op("dve", "tensor_tensor", O["bh"], T[1], T[3], ALU.mult, r=[TTB[1], TTB[3]], w=[opB["bh"]])
        op("dve", "tensor_tensor", O["kt"], T[7], T[5], ALU.mult, r=[TTB[7], TTB[5]], w=[opB["kt"]])
        op("dve", "tensor_tensor", O["kh"], T[7], T[3], ALU.mult, r=[TTB[7], TTB[3]], w=[opB["kh"]])
        op("act", "activation", O["vb"], v_, AF.Copy, r=[psbB], w=[opB["vb"]])

    def rwkv_tokmajor(t0, C):
        for n, src in (("v", "vb"), ("bh", "bh"), ("kh", "kh")):
            p1, p1B = PS1()
            pb16 = p1.bitcast(BF16).rearrange("p (m f) -> p m f", f=128)
            for m in range(8):
                op("pe", "transpose", pb16[0:C, m, :], opb[src][:, m, t0:t0 + C], ident_b, r=[opB[src], cstB], w=[p1B])
            op("dve", "tensor_copy", tkm_f[n][0:C, :], p1[0:C, :], r=[p1B], w=[tkmB[n]])

    def rwkv_chunk(l, t0, C, pcol_ap, y_tok_out):
        mlt = cst[0:C, C_MLT:C_MLT + C]
        mut = cst[0:C, C_MUT:C_MUT + C]
        mue = cst[0:C, C_MUE:C_MUE + C]
        idb = ident_f[0:C, 0:C]

        def fm(name, h):
            if name in ("at", "rt"):
                name = name + ("e" if h % 2 == 0 else "o")
            return opb[name][:, h // 2, t0:t0 + C]

        def fmB(name, h):
            if name in ("at", "rt"):
                name = name + ("e" if h % 2 == 0 else "o")
            return opB[name]

        def tk(name, h):
            return tkm[name][0:C, h // 2, (h % 2) * 64:(h % 2) * 64 + 64]

        def sbh(h):
            return Sbf[:, h // 2, :]

        def hb(ap):
            return bc(ap.unsqueeze(1), [C, NH, C])

        allop = [opB[n_] for n_ in ("rte", "rto", "ate", "ato", "bt", "kt")]

        def pair(nameL, nameR, outn, mask, rb):
            p2, p2B = PS2()
            pv = p2[:, :].rearrange("p (h s) -> p h s", h=NH)
            for h in range(NH):
                op("pe", "matmul", pv[0:C, h, 0:C], fm(nameL, h), fm(nameR, h), start=True, stop=True, r=rb, w=p2B)
            op("dve", "tensor_tensor", mat[outn][0:C, :, 0:C], pv[0:C, :, 0:C], hb(mask), ALU.mult, r=p2B + [cstB], w=[matB[outn]])

        def pair2(nameL, outA, outB):
            for half in range(2):
                p2, p2B = PS2()
                pv = p2[:, :].rearrange("p (h w s) -> p h w s", h=8, w=2)
                for hh in range(8):
                    h = half * 8 + hh
                    op("pe", "matmul", pv[0:C, hh, :, :], fm(nameL, h), ar2[:, h // 2, h % 2, :, t0:t0 + C],
                       start=True, stop=True, r=allop, w=p2B)
                hs = slice(half * 8, half * 8 + 8)
                op("dve", "tensor_tensor", mat[outA][0:C, hs, 0:C], pv[0:C, :, 0, 0:C], bc(mut.unsqueeze(1), [C, 8, C]),
                   ALU.mult, r=p2B + [cstB], w=[matB[outA]])
                op("dve", "tensor_tensor", mat[outB][0:C, hs, 0:C], pv[0:C, :, 1, 0:C], bc(mue.unsqueeze(1), [C, 8, C]),
                   ALU.mult, r=p2B + [cstB], w=[matB[outB]])

        if C > 1:
            pair("at", "bt", "M0", mlt, allop)
            pair2("bt", "N0", "Rb")
            pair2("kt", "Ak", "Rk")
        else:
            pair("bt", "rt", "Rb", mue, allop)
            pair("kt", "rt", "Rk", mue, allop)
        p2, p2B = PS2()
        pv = p2[:, :].rearrange("p (h s) -> p h s", h=NH)
        for h in range(NH):
            op("pe", "matmul", pv[0:C, h, :], fm("at", h), sbh(h), start=True, stop=(C == 1), r=allop + [SbfB], w=p2B)
            if C > 1:
                op("pe", "matmul", pv[0:C, h, :], mat["Ak"][0:C, h, 0:C], tk("v", h), start=False, stop=True,
                   r=[matB["Ak"], tkmB["v"]], w=p2B)
        op("act", "activation", mat["RH"][0:C], pv[0:C], AF.Copy, r=p2B, w=[matB["RH"]])
        if C > 1:
            op("dve", "tensor_tensor", mat["Pa"][0:C, :, 0:C], mat["N0"][0:C, :, 0:C], hb(idb), ALU.add,
               r=[matB["N0"], cstB], w=[matB["Pa"]])
            curM, curN, curP = "M0", "N0", "Pa"
            for lev in range(1, 6):
                nM = "M1" if curM == "M0" else "M0"
                nN = "N1" if curN == "N0" else "N0"
                nP = "Pb" if curP == "Pa" else "Pa"
                p2, p2B = PS2()
                pm_ = p2[:, :].rearrange("p (h s) -> p h s", h=NH)
                for h in range(NH):
                    op("pe", "matmul", pm_[0:C, h, 0:C], mat[curN][0:C, h, 0:C], mat[curM][0:C, h, 0:C], start=True, stop=True,
                       r=[matB[curN], matB[curM]], w=p2B)
                op("dve", "tensor_tensor", mat["IM"][0:C, :, 0:C], pm_[0:C, :, 0:C], hb(idb), ALU.add, r=p2B + [cstB], w=[matB["IM"]])
                if lev < 5:
                    op("act", "activation", mat[nM][0:C, :, 0:C], pm_[0:C, :, 0:C], AF.Copy, r=p2B, w=[matB[nM]])
                    p3, p3B = PS2()
                    pn_ = p3[:, :].rearrange("p (h s) -> p h s", h=NH)
                    for h in range(NH):
                        op("pe", "matmul", pn_[0:C, h, 0:C], mat[curM][0:C, h, 0:C], mat[curN][0:C, h, 0:C], start=True, stop=True,
                           r=[matB[curN], matB[curM]], w=p3B)
                    op("act", "activation", mat[nN][0:C, :, 0:C], pn_[0:C, :, 0:C], AF.Copy, r=p3B, w=[matB[nN]])
                p4, p4B = PS2()
                pp_ = p4[:, :].rearrange("p (h s) -> p h s", h=NH)
                for h in range(NH):
                    op("pe", "matmul", pp_[0:C, h, 0:C], mat["IM"][0:C, h, 0:C], mat[curP][0:C, h, 0:C], start=True, stop=True,
                       r=[matB["IM"], matB[curP]], w=p4B)
                op("dve", "tensor_copy", mat[nP][0:C, :, 0:C], pp_[0:C, :, 0:C], r=p4B, w=[matB[nP]])
                curM, curN, curP = nM, nN, nP
            p2, p2B = PS2()
            pu = p2[:, :].rearrange("p (h s) -> p h s", h=NH)
            for h in range(NH):
                op("pe", "matmul", pu[0:C, h, :], mat[curP][0:C, h, 0:C], mat["RH"][0:C, h, :], start=True, stop=True,
                   r=[matB[curP], matB["RH"]], w=p2B)
            op("act", "activation", mat["U"][0:C], pu[0:C], AF.Copy, r=p2B, w=[matB["U"]])
            Un = "U"
        else:
            Un = "RH"
        p2, p2B = PS2()
        py = p2[:, :].rearrange("p (h s) -> p h s", h=NH)
        for h in range(NH):
            op("pe", "matmul", py[0:C, h, :], fm("rt", h), sbh(h), start=True, stop=False, r=allop + [SbfB], w=p2B)
            op("pe", "matmul", py[0:C, h, :], mat["Rb"][0:C, h, 0:C], mat[Un][0:C, h, :], start=False, stop=False,
               r=[matB["Rb"], matB[Un]], w=p2B)
            op("pe", "matmul", py[0:C, h, :], mat["Rk"][0:C, h, 0:C], tk("v", h), start=False, stop=True,
               r=[matB["Rk"], tkmB["v"]], w=p2B)
        op("act", "activation", ysb[0:C], py[0:C], AF.Copy, r=p2B, w=[ysbB])
        p1, p1B = PS1()
        pS = p1[:, :].rearrange("p (m i) -> p m i", m=8)
        for h in range(NH):
            hp = (h % 2) * 64
            op("pe", "matmul", pS[hp:hp + 64, h // 2, :], tk("bh", h), mat[Un][0:C, h, :], start=True, stop=False,
               r=[tkmB["bh"], matB[Un]], w=[p1B])
            op("pe", "matmul", pS[hp:hp + 64, h // 2, :], tk("kh", h), tk("v", h), start=False, stop=True,
               r=[tkmB["kh"], tkmB["v"]], w=[p1B])
        op("pool", "tensor_tensor", Scur[:], Scur[:], bc(pcol_ap.unsqueeze(2), [128, 8, 64]), ALU.mult,
           r=[ScurB, pcolB], w=[ScurB])
        op("dve", "tensor_tensor", Scur[:], Scur[:], pS, ALU.add, r=[ScurB, p1B], w=[ScurB])
        op("act", "activation", Sbf[:], Scur[:], AF.Copy, r=[ScurB], w=[SbfB])
        mu, var = yst[0:C, 0, :], yst[0:C, 1, :]
        op("dve", "tensor_reduce", mu, ysb[0:C], AX.X, ALU.add, r=[ysbB], w=[ystB])
        op("dve", "tensor_scalar", mu, mu, 1.0 / 64, None, ALU.mult, r=[ystB], w=[ystB])
        op("dve", "tensor_tensor", ysb[0:C], ysb[0:C], bc(mu.unsqueeze(2), [C, NH, 64]), ALU.subtract, r=[ysbB, ystB], w=[ysbB])
        op("pool", "tensor_tensor", ytmp[0:C], ysb[0:C], ysb[0:C], ALU.mult, r=[ysbB], w=[ytmpB])
        op("dve", "tensor_reduce", var, ytmp[0:C], AX.X, ALU.add, r=[ytmpB], w=[ystB])
        op("act", "activation", var, var, AF.Sqrt, bias=smallc[0:C, 1:2], scale=1.0 / 64, r=[ystB, cstB], w=[ystB])
        op("dve", "reciprocal", var, var, r=[ystB], w=[ystB])
        op("dve", "tensor_tensor", ytmp[0:C], ysb[0:C], bc(var.unsqueeze(2), [C, NH, 64]), ALU.mult, r=[ysbB, ystB], w=[ytmpB])

    def rwkv_post(l, nt, pz, pzB, out_ap, out_bufs):
        B8 = [128, 8, nt]
        z = zt[:, :, 0:nt]
        op("dve", "tensor_tensor", z, pz, bc(vecs[:, l, V_LNW:V_LNW + 8].unsqueeze(2), B8), ALU.mult, r=[pzB, vecB], w=[ztB])
        op("dve", "tensor_tensor", z, z, bc(vecs[:, l, V_LNB:V_LNB + 8].unsqueeze(2), B8), ALU.add, r=[ztB, vecB], w=[ztB])
        op("dve", "tensor_tensor", z, z, bon[:, :, 0:nt], ALU.add, r=[ztB, bonB], w=[ztB])
        op("dve", "tensor_tensor", out_ap, z, gg[:, :, 0:nt], ALU.mult, r=[ztB, ggB], w=out_bufs)

    def state_out(dst):
        p2, p2B = PS2()
        pv = p2[:, :].rearrange("p (m f) -> p m f", m=8)
        for m in range(8):
            op("pe", "transpose", pv[0:64, m, :], Scur[:, m, :], ident_f, r=[ScurB, cstB], w=p2B)
        op("act", "activation", stout[:].rearrange("p h j -> p (h j)"), p2[0:64, :], AF.Copy, r=p2B, w=[stoutB])
        S.dma("sp", dst.rearrange("h i j -> i h j"), stout[:], r=[stoutB], w=[dram_out_buf])

    def run_pass(kind, ci, l):
        sample = (kind == "s")
        N = NS if sample else NT
        first_chunk = (ci == 0)
        last_chunk = sample or (ci == n_chunks - 1)
        if l == 0:
            if sample:
                S.dma("sp", xT[:, :, 0:NS], xsT, w=xTb)
            else:
                for kc in range(KC):
                    S.dma("sp", xT[:, kc, :], xpT[kc, :, ci * NT:(ci + 1) * NT], w=[xTb[kc]])
        S.barrier()
        S.dma("pool", lupb[:], lup_d[l], w=[lupB], max_dma_last_dim=8192)
        norm_phase(N, sample, lambda kc: modap(l, 1, kc, N, sample), lambda kc: modap(l, 0, kc, N, sample),
                   lambda kc: hT[:, kc, :N], lambda kc: hTb[kc])
        checkpoint("norm")
        if not sample:
            op("dve", "tensor_copy", pc[:, :, 0], prevrow[:, l, :], r=[prevrowB[l]], w=pcB)
            op("dve", "tensor_copy", kdT[:, :, 0:128], prevK[:, l], r=[prevKB[l]], w=kdTb)
            op("dve", "tensor_copy", vtok[:, 0, :], prevV[:, l, :], r=[prevVB[l]], w=[vtokB[0]])
        checkpoint("prevcopy")
        ev = 0
        for g2 in range(22):
            wt, wb = WS.get(("in", l, g2))
            wv = wt[:, :].rearrange("p (g k c) -> p g k c", g=2, k=KC)
            for g in range(2):
                m = g2 * 2 + g
                if m >= 43:
                    continue
                if cfg.get("stop") == "projn" and m == cfg.get("nm", 1):
                    raise _Stop()
                p1, p1B = PS1()
                for kc in range(KC):
                    op("pe", "matmul", p1[:, :N], wv[:, g, kc, :], hT[:, kc, :N], start=(kc == 0), stop=(kc == KC - 1),
                       r=[wb, hTb[kc]], w=[p1B])
                if m < 8:
                    op("act", "activation", qT[:, m, :N], p1[:, :N], AF.Copy, scale=0.125, r=[p1B], w=[qTb[m]])
                elif m < 16:
                    gq = m - 8
                    if sample:
                        op("dve", "tensor_copy", kdTs[:, gq, :], p1[:, :N], r=[p1B], w=[kdTsB])
                    else:
                        op("dve", "tensor_copy", kdT[:, gq, 128:128 + N], p1[:, :N], r=[p1B], w=[kdTb[gq]])
                else:
                    c = m - 16
                    off = 0 if sample else 1
                    if ev % 2 == 0:
                        op("act", "activation", pc[:, c, off:off + N], p1[:, :N], AF.Copy, r=[p1B], w=[pcB[c]])
                    else:
                        op("dve", "tensor_copy", pc[:, c, off:off + N], p1[:, :N], r=[p1B], w=[pcB[c]])
                    ev += 1
                    if last_chunk:
                        if sample:
                            op("dve", "tensor_copy", lastrow[:, c, :], p1[:, 0:NS], r=[p1B], w=[lastrowB])
                        else:
                            op("dve", "tensor_copy", lastrow[:, c, 0:1], p1[:, N - 1:N], r=[p1B], w=[lastrowB])
        checkpoint("projmm")
        wkt, wkb = WS.get(("kv", l, 0))
        wk3 = wkt[:, :].rearrange("p (k c) -> p k c", k=KC)
        if not sample:
            if last_chunk:
                p1, p1B = PS1()
                for kc in range(KC):
                    op("pe", "matmul", p1[:, 0:256], hT[:, kc, NT - 128:NT], wk3[:, kc, :], start=(kc == 0),
                       stop=(kc == KC - 1), r=[wkb, hTb[kc]], w=[p1B])
                op("act", "activation", kvout[:, 0, :], p1[:, 0:256], AF.Copy, r=[p1B], w=[kvoutB])
                S.dma("sp", nk_p[l], kvout[:, 0, :], r=[kvoutB], w=[dram_out_buf])
        else:
            p1, p1B = PS1()
            p1b_, p1bB = PS1()
            for s in range(NS):
                pso = (p1 if s < 2 else p1b_)[:, :].rearrange("p (s c) -> p s c", c=256)
                psoB = p1B if s < 2 else p1bB
                for kc in range(KC):
                    op("pe", "matmul", pso[0:1, s % 2, :], hT[:, kc, s:s + 1], wk3[:, kc, :], start=(kc == 0),
                       stop=(kc == KC - 1), r=[wkb, hTb[kc]], w=[psoB])
                op("act", "activation", kvnew_f[0:1, 0, s, :], pso[0:1, s % 2, :], AF.Copy, r=[psoB], w=[kvnewB])
            S.dma("sp", nk_s[l, :, 127:128, :].rearrange("s o d -> o s d"), kvnew_f[0:1, 0], r=[kvnewB], w=[dram_out_buf])
        wvt, wvb = WS.get(("kv", l, 1))
        wv3 = wvt[:, :].rearrange("p (k c) -> p k c", k=KC)
        if not sample:
            for tb in range(NT // 128):
                p1, p1B = PS1()
                for kc in range(KC):
                    op("pe", "matmul", p1[:, 0:256], hT[:, kc, tb * 128:(tb + 1) * 128], wv3[:, kc, :], start=(kc == 0),
                       stop=(kc == KC - 1), r=[wvb, hTb[kc]], w=[p1B])
                op("dve", "tensor_copy", vtok[:, 1 + tb, :], p1[:, 0:256], r=[p1B], w=[vtokB[1 + tb]])
                if last_chunk and tb == NT // 128 - 1:
                    op("act", "activation", kvout[:, 1, :], p1[:, 0:256], AF.Copy, r=[p1B], w=[kvoutB])
            if last_chunk:
                S.dma("sp", nv_p[l], kvout[:, 1, :], r=[kvoutB], w=[dram_out_buf])
        else:
            p1, p1B = PS1()
            p1b_, p1bB = PS1()
            for s in range(NS):
                pso = (p1 if s < 2 else p1b_)[:, :].rearrange("p (s c) -> p s c", c=256)
                psoB = p1B if s < 2 else p1bB
                for kc in range(KC):
                    op("pe", "matmul", pso[0:1, s % 2, :], hT[:, kc, s:s + 1], wv3[:, kc, :], start=(kc == 0),
                       stop=(kc == KC - 1), r=[wvb, hTb[kc]], w=[psoB])
                op("dve", "tensor_copy", vnew[0:1, s, :], pso[0:1, s % 2, :], r=[psoB], w=[vnewB])
                op("act", "activation", kvnew_f[0:1, 1, s, :], pso[0:1, s % 2, :], AF.Copy, r=[psoB], w=[kvnewB])
            S.dma("sp", nv_s[l, :, 127:128, :].rearrange("s o d -> o s d"), kvnew_f[0:1, 1], r=[kvnewB], w=[dram_out_buf])
            S.dma("sp", nk_s[l, :, 0:127, :], ck_d[l, :, 1:128, :], w=[dram_out_buf])
            S.dma("sp", nv_s[l, :, 0:127, :], cv_d[l, :, 1:128, :], w=[dram_out_buf])
        checkpoint("proj")
        tap("pc_%s%d_%d" % (kind, ci, l), pc[:, :, 0:NT + 1], pcB, BF16)
        tap("qT_%s%d_%d" % (kind, ci, l), qT[:], qTb, BF16)
        if not sample:
            for qb in range(NT // 128):
                dm = cst[:, C_DM0:C_DM0 + 256] if (first_chunk and qb == 0) else cst[:, C_DM:C_DM + 256]
                for g in range(4):
                    def q_of(h, qb=qb):
                        return qT[:, h // 2, qb * 128:(qb + 1) * 128]

                    def k_of(h, qb=qb, g=g):
                        return kdT[:, g * 2 + h % 2, qb * 128:qb * 128 + 256]

                    def pv_terms(r, pTt, pTB, pbt, pbB, qb=qb, g=g):
                        return [(vtok[:, qb + kb, g * 64:(g + 1) * 64], pTt[:, r * 2 + kb, :], [vtokB[qb + kb], pTB])
                                for kb in range(2)]

                    attn_group(l, g, 128, 256, q_of, k_of, dm, pv_terms,
                               yT[:, 2 * g:2 * g + 2, qb * 128:(qb + 1) * 128], [yTb[2 * g], yTb[2 * g + 1]],
                               qTb + [kdTb[2 * g], kdTb[2 * g + 1]])
            op("pool", "tensor_copy", prevK[:, l], kdT[:, :, NT:NT + 128], r=kdTb, w=[prevKB[l]])
            op("pool", "tensor_copy", prevV[:, l, :], vtok[:, NT // 128, :], r=[vtokB[NT // 128]], w=[prevVB[l]])
        else:
            for s in range(NS):
                S.dma("pool", ksT[:, :, 0:128], ckT_d[l, s], w=[ksTB])
                S.dma("pool", cvb[:], cv_d[l, s], w=[cvbB])
                op("dve", "tensor_copy", ksT[:, :, 128:129], kdTs[:, :, s:s + 1], r=[kdTsB], w=[ksTB])
                for g in range(4):
                    def q_of(h, s=s):
                        return qT[:, h // 2, s:s + 1]

                    def k_of(h, g=g):
                        return ksT[:, g * 2 + h % 2, 0:129]

                    def pv_terms(r, pTt, pTB, pbt, pbB, s=s, g=g):
                        return [(cvb[:, g * 64:(g + 1) * 64], pTt[:, 0, 2 * r:2 * r + 1], [cvbB, pTB]),
                                (vnew[0:1, s, g * 64:(g + 1) * 64], pbt[0:1, r, 128:129], [vnewB, pbB])]

                    attn_group(l, g, 1, 129, q_of, k_of, cst[0:1, C_DS:C_DS + 129], pv_terms,
                               yT[:, 2 * g:2 * g + 2, s:s + 1], [yTb[2 * g], yTb[2 * g + 1]], qTb + [ksTB])
        checkpoint("attn")
        tap("yTa_%s%d_%d" % (kind, ci, l), yT[:, 0:8, :], yTb[0:8], BF16)
        S.barrier()
        for nm_, lo in (("rte", 64), ("rto", 0), ("ate", 64), ("ato", 0)):
            op("pool", "memset", opb[nm_][lo:lo + 64], 0.0, w=[opB[nm_]])
        if not sample:
            op("dve", "tensor_copy", Scur[:], Sst[:, l], r=[SstB[l]], w=[ScurB])
            op("act", "activation", Sbf[:], Scur[:], AF.Copy, r=[ScurB], w=[SbfB])
            for cb in range(NT // 64):
                t0 = cb * 64
                rwkv_prep(l, 64, pc[:, :, t0:t0 + 64], pc[:, :, t0 + 1:t0 + 65], pcB)
                rwkv_tokmajor(0, 64)
                S.barrier()
                rwkv_chunk(l, 0, 64, pcol[:, :], None)
                pz, pzB = PS1()
                pzv = pz[:, :].rearrange("p (m t) -> p m t", m=8)
                for m in range(8):
                    op("pe", "transpose", pzv[:, m, :], ytmp[:, 2 * m:2 * m + 2, :].rearrange("p a b -> p (a b)"),
                       ident_f[0:64, 0:64], r=[ytmpB, cstB], w=[pzB])
                rwkv_post(l, 64, pzv, pzB, yT[:, 8:16, t0:t0 + 64], yTb[8:16])
                S.barrier()
            op("dve", "tensor_copy", Sst[:, l], Scur[:], r=[ScurB], w=[SstB[l]])
            op("dve", "tensor_copy", prevrow[:, l, :], pc[:, :, NT], r=pcB, w=[prevrowB[l]])
            if last_chunk:
                state_out(nwkv_p[l])
        else:
            S.dma("sp", shs[:], shT_d[l], w=[shsB])
            rwkv_prep(l, NS, shs[:, :, :], pc[:, :, 0:NS], pcB + [shsB])
            S.barrier()
            for s in range(NS):
                S.dma("sp", Scur[:], stT_d[l, s], w=[ScurB])
                op("act", "activation", Sbf[:], Scur[:], AF.Copy, r=[ScurB], w=[SbfB])
                rwkv_tokmajor(s, 1)
                rwkv_chunk(l, s, 1, pcolS[:, :, s], None)
                pz, pzB = PS1()
                for m in range(8):
                    op("pe", "transpose", pz[:, 2 * m:2 * m + 1], ytmp[0:1, 2 * m:2 * m + 2, :].rearrange("p a b -> p (a b)"),
                       ident_f[0:1, 0:1], r=[ytmpB, cstB], w=[pzB])
                op("dve", "tensor_copy", zt[:, :, s:s + 1], pz[:, 0:16].rearrange("p (m t) -> p m t", t=2)[:, :, 0:1],
                   r=[pzB], w=[ztB])
                state_out(nwkv_s[l, s])
            rwkv_post(l, NS, zt[:, :, 0:NS], ztB, yT[:, 8:16, 0:NS], yTb[8:16])
        checkpoint("rwkv")
        tap("yT_%s%d_%d" % (kind, ci, l), yT[:], yTb, BF16)
        if last_chunk:
            ncol = NS if sample else 1
            for gi, c0 in enumerate(range(0, NPC, 4)):
                p1, p1B = PS1()
                pv = p1[:, :].rearrange("p (c f) -> p c f", f=128)
                nn = min(4, NPC - c0)
                for c in range(nn):
                    op("pe", "transpose", pv[0:ncol, c, :], lastrow[:, c0 + c, 0:ncol], ident_f, r=[lastrowB, cstB], w=[p1B])
                so = (gi % 4) * 512
                op("act", "activation", otok[0:ncol, so:so + nn * 128], p1[0:ncol, 0:nn * 128], AF.Copy, r=[p1B], w=[otokB])
                if sample:
                    S.dma("sp", nsh_s[l].rearrange("s c f -> s (c f)")[:, c0 * 128:(c0 + nn) * 128], otok[0:NS, so:so + nn * 128],
                          r=[otokB], w=[dram_out_buf])
                else:
                    S.dma("sp", nsh_p[l:l + 1].rearrange("o c f -> o (c f)")[:, c0 * 128:(c0 + nn) * 128],
                          otok[0:1, so:so + nn * 128], r=[otokB], w=[dram_out_buf])
        S.barrier()
        for g2 in range(8):
            wt, wb = WS.get(("out", l, g2))
            wv = wt[:, :].rearrange("p (g k c) -> p g k c", g=2, k=KC)
            for g in range(2):
                mo = g2 * 2 + g
                p1, p1B = PS1()
                for kc in range(KC):
                    op("pe", "matmul", p1[:, :N], wv[:, g, kc, :], yT[:, kc, :N], start=(kc == 0), stop=(kc == KC - 1),
                       r=[wb, yTb[kc]], w=[p1B])
                resid_add(l, 2, N, sample, mo, p1[:, :N], p1B)
        checkpoint("outproj")
        tap("x1_%s%d_%d" % (kind, ci, l), xT[:], xTb)
        S.barrier()
        norm_phase(N, sample, lambda kc: modap(l, 4, kc, N, sample), lambda kc: modap(l, 3, kc, N, sample),
                   lambda kc: hT[:, kc, :N], lambda kc: hTb[kc])
        for j in range(NJ):
            wt, wb = WS.get(("ffi", l, j))
            wv = wt[:, :].rearrange("p (g k c) -> p g k c", g=2, k=KC)
            pg, pgB = PS1()
            pu, puB = PS1()
            for kc in range(KC):
                op("pe", "matmul", pg[:, :N], wv[:, 0, kc, :], hT[:, kc, :N], start=(kc == 0), stop=(kc == KC - 1),
                   r=[wb, hTb[kc]], w=[pgB])
            for kc in range(KC):
                op("pe", "matmul", pu[:, :N], wv[:, 1, kc, :], hT[:, kc, :N], start=(kc == 0), stop=(kc == KC - 1),
                   r=[wb, hTb[kc]], w=[puB])
            i = j % 2
            op("act", "activation", ftmp[:, i, :N], pg[:, :N], AF.Silu, r=[pgB], w=[ftmpB[i]])
            op("dve", "tensor_tensor", aT[:, j, :N], ftmp[:, i, :N], pu[:, :N], ALU.mult, r=[ftmpB[i], puB], w=[aTb[j]])
        for mo in range(KC):
            p1, p1B = PS1()
            for hf in range(2):
                wt, wb = WS.get(("ffo", l, mo * 2 + hf))
                wv = wt[:, 0:2816].rearrange("p (j c) -> p j c", c=128)
                for jj in range(22):
                    j = hf * 22 + jj
                    op("pe", "matmul", p1[:, :N], wv[:, jj, :], aT[:, j, :N], start=(j == 0), stop=(j == NJ - 1),
                       r=[wb, aTb[j]], w=[p1B])
            resid_add(l, 5, N, sample, mo, p1[:, :N], p1B)
        checkpoint("ffn")
        tap("x2_%s%d_%d" % (kind, ci, l), xT[:], xTb)
        if l == n_layers - 1:
            pss, pssB = PS1()
            for kc in range(KC):
                i = kc % 2
                op("act", "activation", sqb[:, i, :N], xT[:, kc, :N], AF.Square, r=[xTb[kc]], w=[sqB[i]])
                op("pe", "matmul", pss[:, :N], ones_b, sqb[:, i, :N], start=(kc == 0), stop=(kc == KC - 1),
                   r=[sqB[i], cstB], w=[pssB])
            op("act", "activation", rstd[:, :N], pss[:, :N], AF.Sqrt, bias=smallc[:, 0:1], scale=1.0 / D, r=[pssB, cstB], w=[rstdB])
            op("dve", "reciprocal", rstd[:, :N], rstd[:, :N], r=[rstdB], w=[rstdB])
            for kc in range(KC):
                op("dve", "scalar_tensor_tensor", xT[:, kc, :N], xT[:, kc, :N], gfin[:, kc:kc + 1], rstd[:, :N], ALU.mult, ALU.mult,
                   r=[xTb[kc], rstdB, vecB], w=[xTb[kc]])
            nblk = 1 if sample else NT // 128
            for tb in range(nblk):
                nq = NS if sample else 128
                for k0 in range(0, KC, 4):
                    p1, p1B = PS1()
                    pv = p1[:, :].rearrange("p (c f) -> p c f", f=128)
                    for c in range(4):
                        op("pe", "transpose", pv[0:nq, c, :], xT[:, k0 + c, tb * 128:tb * 128 + nq], ident_f,
                           r=[xTb[k0 + c], cstB], w=[p1B])
                    dst_ = otok[0:nq, k0 * 128:(k0 + 4) * 128].rearrange("p (c f) -> p c f", f=128)
                    if (k0 // 4) % 2 == 0:
                        op("act", "activation", dst_, pv[0:nq], AF.Copy, r=[p1B], w=[otokB])
                    else:
                        op("dve", "tensor_copy", dst_, pv[0:nq], r=[p1B], w=[otokB])
                if sample:
                    S.dma("sp", y_s, otok[0:NS, :], r=[otokB], w=[dram_out_buf])
                else:
                    S.dma("sp", y_p[ci * NT + tb * 128:ci * NT + (tb + 1) * 128, :], otok[:, :], r=[otokB], w=[dram_out_buf])

    try:
        if stopped:
            raise _Stop()
        checkpoint("mods")
        for (kind, ci, l) in passes:
            run_pass(kind, ci, l)
    except _Stop:
        pass

    S.emit(es)
    es.close()
    return nc


def _consts():
    cst = np.zeros((128, NCST), np.float32)
    cst[:, C_ID:C_ID + 128] = np.eye(128, dtype=np.float32)
    blk = np.zeros((128, 128), np.float32)
    blk[0:64, 0:64] = 1.0
    blk[64:128, 64:128] = 1.0
    cst[:, C_BLK:C_BLK + 128] = blk
    p = np.arange(128)[:, None] % 64
    f = np.arange(64)[None, :]
    cst[:, C_MLT:C_MLT + 64] = (f < p)
    cst[:, C_MUT:C_MUT + 64] = (p < f)
    cst[:, C_MUE:C_MUE + 64] = (p <= f)
    i = np.arange(128)[:, None]
    j = np.arange(256)[None, :]
    dist = (128 + i - j).astype(np.float32)
    valid = (dist >= 0) & (dist <= 128)
    dm = np.where(valid, dist, BIG).astype(np.float32)
    cst[:, C_DM:C_DM + 256] = dm
    dm0 = dm.copy()
    dm0[:, 0:128] = BIG
    cst[:, C_DM0:C_DM0 + 256] = dm0
    ds = np.concatenate([128.0 - np.arange(128), [0.0]]).astype(np.float32)
    cst[:, C_DS:C_DS + 129] = ds[None, :]
    return cst


def _fm(v, ncol):
    return np.ascontiguousarray(v.reshape(ncol, 128).T)


def _pc_layout(a):
    out = np.zeros(a.shape[:-1] + (NPC * 128,), np.float32)
    out[..., 0:3072] = a[..., 0:3072]
    out[..., 3072:3136] = a[..., 3072:3136]
    out[..., 3136:3200] = a[..., 3136:3200]
    out[..., 3200:3360] = a[..., 3200:3360]
    return out


def _prep_weights(inp):
    w = {}
    L_ = L
    wada = inp["w_ada"].reshape(L_, KC, 128, 48, 2, 128).transpose(0, 3, 2, 4, 1, 5)
    w["wada"] = np.ascontiguousarray(wada).reshape(L_, 48, 128, 4096)
    w_in = inp["w_in"]
    wperm = np.zeros((L_, D, 44 * 128), np.float32)
    wperm[:, :, 0:1024] = w_in[:, :, 0:1024]
    for g in range(4):
        wkg = w_in[:, :, 1024 + g * 64:1024 + (g + 1) * 64]
        base = 1024 + g * 256
        wperm[:, :, base:base + 64] = wkg
        wperm[:, :, base + 128 + 64:base + 256] = wkg
    wperm[:, :, 2048:2048 + 3360] = w_in[:, :, 1536:1536 + 3360]
    win = wperm.reshape(L_, KC, 128, 22, 2, 128).transpose(0, 3, 2, 4, 1, 5)
    w["win"] = np.ascontiguousarray(win).reshape(L_, 22, 128, 4096)
    wk = w_in[:, :, 1024:1280].reshape(L_, KC, 128, 256).transpose(0, 2, 1, 3).reshape(L_, 128, 4096)
    wv = w_in[:, :, 1280:1536].reshape(L_, KC, 128, 256).transpose(0, 2, 1, 3).reshape(L_, 128, 4096)
    w["wkvt"] = np.ascontiguousarray(np.stack([wk, wv], axis=1))
    wout = inp["w_out"].reshape(L_, KC, 128, 8, 2, 128).transpose(0, 3, 2, 4, 1, 5)
    w["wout"] = np.ascontiguousarray(wout).reshape(L_, 8, 128, 4096)
    wfi = inp["w_ffn_in"].reshape(L_, KC, 128, 2, NJ, 128).transpose(0, 4, 2, 3, 1, 5)
    w["wffi"] = np.ascontiguousarray(wfi).reshape(L_, NJ, 128, 4096)
    wfo = inp["w_ffn_out"].reshape(L_, 2, 22, 128, KC, 128).transpose(0, 4, 1, 3, 2, 5)
    w["wffo"] = np.ascontiguousarray(wfo).reshape(L_, 32, 128, 2816)
    lup = np.zeros((L_, 128, 4096), np.float32)
    lup[:, 0:64, 0:1024] = inp["decay_up"]
    lup[:, 64:128, 1024:2048] = inp["iclr_up"]
    lup[:, :, 2048:3072] = inp["gate_up"][:, 0:128]
    lup[:, 0:32, 3072:4096] = inp["gate_up"][:, 128:160]
    w["lup"] = lup
    vecs = np.zeros((128, L_, NV), np.float32)
    for l in range(L_):
        vecs[:, l, V_GMIX:V_GMIX + 16] = _fm(inp["g_norm_mix"][l], 16)
        vecs[:, l, V_GFFN:V_GFFN + 16] = _fm(inp["g_norm_ffn"][l], 16)
        vecs[:, l, V_MIX:V_MIX + NPC] = _fm(_pc_layout(inp["mix_shift"][l]), NPC)
        vecs[:, l, V_W0:V_W0 + 8] = _fm(inp["decay_w0"][l], 8)
        vecs[:, l, V_A0:V_A0 + 8] = _fm(inp["iclr_a0"][l], 8)
        vecs[:, l, V_KK:V_KK + 8] = _fm(inp["k_k"][l], 8)
        vecs[:, l, V_KA:V_KA + 8] = _fm(inp["k_a"][l], 8)
        vecs[:, l, V_RK:V_RK + 8] = _fm(inp["r_k"][l].reshape(-1), 8)
        vecs[:, l, V_LNW:V_LNW + 8] = _fm(inp["ln_x_w"][l], 8)
        vecs[:, l, V_LNB:V_LNB + 8] = _fm(inp["ln_x_b"][l], 8)
        vecs[:, l, V_BADA:V_BADA + 96] = _fm(inp["b_ada"][l], 96)
    w["vecs"] = vecs
    w["sinks"] = np.ascontiguousarray(np.broadcast_to(inp["attn_sinks"][None], (128, L_, NH))).astype(np.float32)
    w["cst"] = _consts()
    w["gfin"] = _fm(inp["g_norm_final"], KC)
    return w


def _core_inputs(inp, shared, core):
    b = core % 4
    ss = slice(core * NS, (core + 1) * NS)
    m = dict(shared)
    m["xpT"] = np.ascontiguousarray(inp["x_prompt"][b].T).reshape(KC, 128, SEQ)
    xs = inp["x_sample"][ss, 0, :]
    m["xsT"] = np.ascontiguousarray(xs.reshape(NS, KC, 128).transpose(2, 1, 0))
    c5 = np.concatenate([inp["c_prompt"][b:b + 1], inp["c_sample"][ss]], axis=0)
    m["c5T"] = np.ascontiguousarray(c5.reshape(5, KC, 128).transpose(2, 1, 0))
    ck = inp["cache_k"][:, ss]
    ckT = ck.transpose(0, 1, 4, 3, 2)
    ckz = np.zeros((L, NS, 128, 4, 2, 128), np.float32)
    ckz[:, :, 0:64, :, 0, :] = ckT
    ckz[:, :, 64:128, :, 1, :] = ckT
    m["ckT"] = ckz.reshape(L, NS, 128, 8, 128)
    m["ck"] = np.ascontiguousarray(ck.reshape(L, NS, 128, 256))
    m["cv"] = np.ascontiguousarray(inp["cache_v"][:, ss].reshape(L, NS, 128, 256))
    stw = inp["state_wkv"][:, ss]
    stT = stw.reshape(L, NS, 8, 2, 64, 64).transpose(0, 1, 3, 5, 2, 4)
    m["stT"] = np.ascontiguousarray(stT).reshape(L, NS, 128, 8, 64)
    sh = _pc_layout(inp["state_shift"][:, ss])
    m["shT"] = np.ascontiguousarray(sh.reshape(L, NS, NPC, 128).transpose(0, 3, 2, 1))
    return m


def _unpc(a):
    return np.concatenate([a[..., 0:3200], a[..., 3200:3360]], axis=-1)


_NC_CACHE = {}


def kernel(**inputs):
    inp = {k: np.asarray(v) for k, v in inputs.items()}
    shared = _prep_weights(inp)
    in_maps = [_core_inputs(inp, shared, c) for c in range(8)]
    if "nc" not in _NC_CACHE:
        _NC_CACHE["nc"] = build()
    nc = _NC_CACHE["nc"]
    res = run_bass_kernel_spmd(nc, in_maps, core_ids=list(range(8)))
    R = res.results
    y_prompt = np.stack([R[b]["y_p"] for b in range(4)], 0).astype(np.float32)
    y_sample = np.concatenate([R[c]["y_s"] for c in range(8)], 0).reshape(32, 1, D).astype(np.float32)
    nkp = np.stack([R[b]["nk_p"] for b in range(4)], 1).reshape(L, 4, 128, 4, 64)
    nvp = np.stack([R[b]["nv_p"] for b in range(4)], 1).reshape(L, 4, 128, 4, 64)
    nwp = np.stack([R[b]["nwkv_p"] for b in range(4)], 1)
    nsp = _unpc(np.stack([R[b]["nsh_p"] for b in range(4)], 1).reshape(L, 4, NPC * 128))
    nks = np.concatenate([R[c]["nk_s"] for c in range(8)], 1).reshape(L, 32, 128, 4, 64)
    nvs = np.concatenate([R[c]["nv_s"] for c in range(8)], 1).reshape(L, 32, 128, 4, 64)
    nws = np.concatenate([R[c]["nwkv_s"] for c in range(8)], 1)
    nss = _unpc(np.concatenate([R[c]["nsh_s"] for c in range(8)], 1).reshape(L, 32, NPC * 128))
    f = lambda a: np.ascontiguousarray(a, dtype=np.float32)
    return (f(y_prompt), f(y_sample), f(nkp), f(nvp), f(nwp), f(nsp), f(nks), f(nvs), f(nws), f(nss))
```

```python
import math
from contextlib import ExitStack
import numpy as np
import concourse.bass as bass
import concourse.mybir as mybir
from concourse.bass_utils import run_bass_kernel_spmd

F32 = mybir.dt.float32
BF16 = mybir.dt.bfloat16
ALU = mybir.AluOpType
AF = mybir.ActivationFunctionType
AX = mybir.AxisListType

D = 2048
KC = 16
SEQ = 2048
NT = 512
NCH = SEQ // NT
NS = 4
L = 2
NH = 16
HD = 64
FF = 5632
NJ = FF // 128
RP = 3360
NPC = 27
C0 = math.exp(-0.5)
RMS_EPS = 1e-5
GN_EPS = 64e-5
BIG = 1.0e9
SLOPES = [2.0 ** (-8.0 * (h + 1) / NH) for h in range(NH)]

V_GMIX, V_GFFN, V_MIX, V_W0, V_A0, V_KK, V_KA, V_RK, V_LNW, V_LNB, V_BADA = 0, 16, 32, 59, 67, 75, 83, 91, 99, 107, 115
NV = 115 + 96
C_ID, C_BLK, C_MLT, C_MUT, C_MUE, C_DM, C_DM0, C_DS = 0, 128, 256, 320, 384, 448, 704, 960
NCST = 960 + 129


class _Stop(Exception):
    pass


class Buf:
    __slots__ = ("name", "w", "r", "excl")

    def __init__(self, name, excl=False):
        self.name = name
        self.w = None
        self.r = {}
        self.excl = excl


class Sched:
    CE = ("pe", "act", "dve", "pool")

    def __init__(self, nc, ndma=24):
        self.nc = nc
        self.q = {e: [] for e in ("pe", "act", "dve", "pool", "sp")}
        self.cnt = {e: 0 for e in self.CE}
        self.seen = {e: {} for e in self.q}
        self.floor = {e: {} for e in self.q}
        self.ndma = ndma
        self.dval = [0] * ndma
        self.dnext = {"sp": 0, "pool": ndma // 2, "act": 0}

    def _need(self, r, w):
        need = {}
        for b in r:
            if b.w is not None:
                k, v = b.w
                if need.get(k, 0) < v:
                    need[k] = v
        for b in w:
            if b.w is not None:
                k, v = b.w
                if need.get(k, 0) < v:
                    need[k] = v
            for k, v in b.r.items():
                if need.get(k, 0) < v:
                    need[k] = v
        return need

    def _waits(self, eng, need):
        fl = self.floor[eng]
        if fl:
            for k, v in fl.items():
                if need.get(k, 0) < v:
                    need[k] = v
            self.floor[eng] = {}
        seen = self.seen[eng]
        out = []
        for k, v in need.items():
            if k == eng and eng == "pe":
                continue
            if seen.get(k, 0) >= v:
                continue
            seen[k] = v
            out.append((k, v))
        return out

    def _mark(self, tok, r, w):
        k, v = tok
        for b in r:
            if b.r.get(k, 0) < v:
                b.r[k] = v
        for b in w:
            b.w = tok
            b.r = {}

    def op(self, eng, meth, *args, r=(), w=(), **kw):
        if any(b.excl for b in r):
            w = list(w) + [b for b in r if b.excl]
            r = [b for b in r if not b.excl]
        waits = self._waits(eng, self._need(r, w))
        self.cnt[eng] += 1
        tok = (eng, self.cnt[eng])
        self.q[eng].append((waits, meth, args, kw, eng))
        self._mark(tok, r, w)

    def dma(self, eng, out, in_, r=(), w=(), **kw):
        need = self._need(r, w)
        i = self.dnext[eng]
        half = self.ndma // 2
        base = half if eng == "pool" else 0
        self.dnext[eng] = base + (i - base + 1) % half
        k = ("d", i)
        if self.dval[i] > 0 and need.get(k, 0) < self.dval[i]:
            need[k] = self.dval[i]
        waits = self._waits(eng, need)
        self.dval[i] += 16
        tok = (k, self.dval[i])
        kw = dict(kw)
        kw["out"] = out
        kw["in_"] = in_
        self.q[eng].append((waits, "dma_start", (), kw, k))
        self._mark(tok, r, w)

    def barrier(self):
        for e in self.q:
            fl = self.floor[e]
            for c in self.CE:
                if self.cnt[c] > 0 and fl.get(c, 0) < self.cnt[c]:
                    fl[c] = self.cnt[c]
            for i in range(self.ndma // 2):
                if self.dval[i] > 0:
                    fl[("d", i)] = self.dval[i]

    def emit(self, es):
        nc = self.nc
        sems = {e: es.enter_context(nc.semaphore("s_" + e)) for e in self.CE}
        dsems = [es.enter_context(nc.semaphore("d%d" % i)) for i in range(self.ndma)]
        block = es.enter_context(nc.Block())

        def semof(k):
            return sems[k] if isinstance(k, str) else dsems[k[1]]

        def run(name, final=False):
            def f(e):
                for waits, meth, args, kw, inc in self.q[name]:
                    for k, v in waits:
                        e.wait_ge(semof(k), v)
                    ins = getattr(e, meth)(*args, **kw)
                    if isinstance(inc, str):
                        ins.then_inc(sems[inc], 1)
                    else:
                        ins.then_inc(dsems[inc[1]], 16)
                if final:
                    for i in range(self.ndma):
                        if self.dval[i] > 0:
                            e.wait_ge(dsems[i], self.dval[i])
                    for c in self.CE:
                        if self.cnt[c] > 0:
                            e.wait_ge(sems[c], self.cnt[c])
            return f

        block.tensor(run("pe"))
        block.scalar(run("act"))
        block.vector(run("dve"))
        block.gpsimd(run("pool"))
        block.sync(run("sp", final=True))


def build(cfg=None):
    cfg = cfg or {}
    n_layers = cfg.get("layers", L)
    n_chunks = cfg.get("chunks", NCH)
    do_sample = cfg.get("sample", True)
    taps = cfg.get("taps", ())
    nc = bass.Bass("TRN2", target_bir_lowering=False)
    S = Sched(nc)
    es = ExitStack()

    def din(name, shape, dt=F32):
        return nc.dram_tensor(name, list(shape), dt, kind="ExternalInput").ap()

    def dout(name, shape, dt=F32):
        return nc.dram_tensor(name, list(shape), dt, kind="ExternalOutput").ap()

    xpT = din("xpT", [KC, 128, SEQ])
    xsT = din("xsT", [128, KC, NS])
    c5T = din("c5T", [128, KC, 5])
    vecs_d = din("vecs", [128, L, NV])
    sinks_d = din("sinks", [128, L, NH])
    cst_d = din("cst", [128, NCST])
    ckT_d = din("ckT", [L, NS, 128, 8, 128])
    cv_d = din("cv", [L, NS, 128, 256])
    ck_d = din("ck", [L, NS, 128, 256])
    stT_d = din("stT", [L, NS, 128, 8, 64])
    shT_d = din("shT", [L, 128, NPC, NS])
    wada_d = din("wada", [L, 48, 128, 4096])
    win_d = din("win", [L, 22, 128, 4096])
    wkvt_d = din("wkvt", [L, 2, 128, 4096])
    wout_d = din("wout", [L, 8, 128, 4096])
    wffi_d = din("wffi", [L, 44, 128, 4096])
    wffo_d = din("wffo", [L, 32, 128, 2816])
    lup_d = din("lup", [L, 128, 4096])
    gfin_d = din("gfin", [128, KC])

    y_p = dout("y_p", [SEQ, D])
    y_s = dout("y_s", [NS, D])
    nk_p = dout("nk_p", [L, 128, 256])
    nv_p = dout("nv_p", [L, 128, 256])
    nwkv_p = dout("nwkv_p", [L, NH, 64, 64])
    nsh_p = dout("nsh_p", [L, NPC, 128])
    nk_s = dout("nk_s", [L, NS, 128, 256])
    nv_s = dout("nv_s", [L, NS, 128, 256])
    nwkv_s = dout("nwkv_s", [L, NS, NH, 64, 64])
    nsh_s = dout("nsh_s", [L, NS, NPC, 128])
    dram_out_buf = Buf("dram_out")

    def sb(name, shape, dt=F32):
        return es.enter_context(nc.sbuf_tensor("sb_" + name, list(shape), dt))

    def carve(reg, off, shape, dt=F32, parts=128):
        n = 1
        for d_ in shape[1:]:
            n *= d_
        nf = n if dt == F32 else (n + 1) // 2
        v = reg[0:parts, off:off + nf]
        if dt != F32:
            v = v.bitcast(dt)[:, 0:n]
        if len(shape) == 3:
            v = v.rearrange("p (a b) -> p a b", a=shape[1])
        return v, off + nf

    xT = sb("xT", [128, KC, NT])
    xTb = [Buf("xT%d" % k) for k in range(KC)]
    R2 = sb("R2", [128, 4096])
    hT, _ = carve(R2, 0, [128, KC, NT], BF16)
    hTb = [Buf("hT%d" % k) for k in range(KC)]
    NWS = 3
    wsl = [sb("wsl%d" % i, [128, 4096], BF16) for i in range(NWS)]
    wslb = [Buf("wsl%d" % i) for i in range(NWS)]
    lupb = sb("lupb", [128, 4096], BF16)
    lupB = Buf("lup")
    cst = sb("cst", [128, NCST])
    cstB = Buf("cst")
    cbf = sb("cbf", [128, 384], BF16)
    ones_f = sb("ones_f", [128, 64])
    vecs = sb("vecs", [128, L, NV])
    vecB = Buf("vecs")
    omk = sb("omk", [128, L, 8])
    sinks = sb("sinks", [128, L, NH])
    modT = sb("modT", [128, L, 96, 5])
    modB = Buf("mod")
    gfin = sb("gfin", [128, KC])
    c5 = sb("c5", [128, KC, 5])
    scT = sb("scT", [128, KC, 5], BF16)
    smallc = sb("smallc", [128, 4])
    Sst = sb("Sst", [128, L, 8, 64])
    SstB = [Buf("Sst%d" % l) for l in range(L)]
    Scur = sb("Scur", [128, 8, 64])
    Sbf = sb("Sbf", [128, 8, 64], BF16)
    ScurB = Buf("Scur")
    SbfB = Buf("Sbf")
    prevrow = sb("prevrow", [128, L, NPC], BF16)
    prevrowB = [Buf("prow%d" % l) for l in range(L)]
    prevK = sb("prevK", [128, L, 8, 128], BF16)
    prevKB = [Buf("pK%d" % l) for l in range(L)]
    prevV = sb("prevV", [128, L, 256], BF16)
    prevVB = [Buf("pV%d" % l) for l in range(L)]
    lastrow = sb("lastrow", [128, NPC, NS])
    lastrowB = Buf("lastrow")
    shs = sb("shs", [128, NPC, NS])
    shsB = Buf("shs")
    vnew = sb("vnew", [1, NS, 256], BF16)
    vnewB = Buf("vnew")
    kdTs = sb("kdTs", [128, 8, NS], BF16)
    kdTsB = Buf("kdTs")
    sqb = sb("sqb", [128, 2, NT], BF16)
    sqB = [Buf("sq0"), Buf("sq1")]
    rstd = sb("rstd", [128, NT])
    rstdB = Buf("rstd")
    tmpn = sb("tmpn", [128, 2, NT])
    tmpnB = [Buf("tmpn0"), Buf("tmpn1")]
    ftmp = tmpn
    ftmpB = tmpnB
    otok = sb("otok", [128, D])
    otokB = Buf("otok")
    kvout = otok[:, 0:512].rearrange("p (a b) -> p a b", a=2)
    kvoutB = otokB
    kvnew_f = otok[0:1, 0:2048].rearrange("p (a s c) -> p a s c", a=2, s=NS)
    kvnewB = otokB
    R1 = sb("R1", [128, 11264])
    pc, o1 = carve(R1, 0, [128, NPC, NT + 2], BF16)
    pcB = [Buf("pc%d" % c) for c in range(NPC)]
    yT, o1 = carve(R1, o1, [128, KC, NT], BF16)
    yTb = [Buf("yT%d" % k) for k in range(KC)]
    assert o1 <= 11264
    aT, _ = carve(R1, 0, [128, NJ, NT], BF16)
    aTb = [Buf("aT%d" % j) for j in range(NJ)]
    R3N = 9632
    R3 = sb("R3", [128, R3N])
    o = 0
    qT, o = carve(R3, o, [128, 8, NT], BF16)
    qTb = [Buf("qT%d" % k) for k in range(8)]
    kdT, o = carve(R3, o, [128, 8, 128 + NT], BF16)
    kdTb = [Buf("kdT%d" % k) for k in range(8)]
    vtok, o = carve(R3, o, [128, 5, 256], BF16)
    vtokB = [Buf("vtok%d" % k) for k in range(5)]
    lg, o = carve(R3, o, [128, 4, 256])
    lgB = Buf("lg")
    pbt2, pbB2, pTt2, pTtf2, pTB2 = [], [], [], [], []
    for i_ in range(2):
        t_, o = carve(R3, o, [128, 4, 256], BF16)
        pbt2.append(t_)
        pbB2.append(Buf("pb%d" % i_))
        tf_, _ = carve(R3, o, [128, 512])
        t_, o = carve(R3, o, [128, 8, 128], BF16)
        pTt2.append(t_)
        pTtf2.append(tf_)
        pTB2.append(Buf("pT%d" % i_))
    attn_state = {"i": 0}
    ast, o = carve(R3, o, [128, 6, 4])
    astB = Buf("ast")
    ksT, o = carve(R3, o, [128, 8, 130], BF16)
    ksTB = Buf("ksT")
    cvb, o = carve(R3, o, [128, 256], BF16)
    cvbB = Buf("cvb")
    assert o <= R3N, o
    o = 0
    psb, o = carve(R3, o, [128, NPC, 64])
    psbB = Buf("psb")
    o_psb_end = o
    TT = []
    for i in range(8):
        t_, _ = carve(R2, i * 512, [128, 8, 64])
        TT.append(t_)
    TTB = [Buf("TT%d" % i) for i in range(8)]
    OPN = ["rte", "rto", "ate", "ato", "bt", "kt", "bh", "kh", "vb"]
    opb, opB = {}, {}
    ar2f, o = carve(R3, o, [128, 2048], BF16)
    ar2 = ar2f.rearrange("p (m q w t) -> p m q w t", m=8, q=2, w=2)
    opb["ate"], opb["ato"] = ar2[:, :, 0, 0, :], ar2[:, :, 1, 0, :]
    opb["rte"], opb["rto"] = ar2[:, :, 0, 1, :], ar2[:, :, 1, 1, :]
    for n in OPN:
        if n not in opb:
            opb[n], o = carve(R3, o, [128, 8, 64], BF16)
        opB[n] = Buf("o_" + n)
    tkm, tkm_f, tkmB = {}, {}, {}
    for n in ("v", "bh", "kh"):
        tkm_f[n], _ = carve(R3, o, [64, 512], F32, parts=64)
        tkm[n], o = carve(R3, o, [64, 8, 128], BF16, parts=64)
        tkmB[n] = Buf("k_" + n)
    ysb, o = carve(R3, o, [64, NH, 64], F32, parts=64)
    ysbB = Buf("ysb")
    stout, stoutB = ysb, ysbB
    ytmp, o = carve(R3, o, [64, NH, 64], F32, parts=64)
    ytmpB = Buf("ytmp")
    zt, o = carve(R3, o, [128, 8, 64])
    ztB = Buf("zt")
    bon, o = carve(R3, o, [128, 8, 64])
    bonB = Buf("bon")
    gg, o = carve(R3, o, [128, 8, 64])
    ggB = Buf("gg")
    tl, o = carve(R3, o, [128, 64], BF16)
    sg, o = carve(R3, o, [128, 2, 64], BF16)
    tmpb, o = carve(R3, o, [128, 8, 64], BF16)
    tlB, sgB, tmpbB = Buf("tl"), Buf("sg"), Buf("tmpb")
    pcol, o = carve(R3, o, [128, 8])
    pcolS, o = carve(R3, o, [128, 8, NS])
    pcolB = Buf("pcol")
    yst, o = carve(R3, o, [64, 4, NH], F32, parts=64)
    ystB = Buf("yst")
    assert o <= R3N, o
    MATN = ["M0", "M1", "IM", "Ak", "Rb", "Rk"]
    mat, matB = {}, {}
    for i, n in enumerate(MATN):
        mat[n], _ = carve(R2, i * 512, [64, NH, 64], BF16, parts=64)
        matB[n] = Buf("m_" + n)
    xa_, _ = carve(R2, 3072, [64, 2048], BF16, parts=64)
    xb_, _ = carve(R3, 0, [64, 2048], BF16, parts=64)
    XT = [xa_.rearrange("p (h w s) -> p h w s", h=NH, w=2), xb_.rearrange("p (h w s) -> p h w s", h=NH, w=2)]
    XTB = [Buf("Xa"), Buf("Xb")]
    mat["RH"], _ = carve(R3, 1024, [64, NH, 64], BF16, parts=64)
    matB["RH"] = Buf("m_RH")
    assert 3 * 512 <= o_psb_end
    mat["U"], matB["U"] = mat["Ak"], matB["Ak"]
    mat["N0"], matB["N0"] = XT[1][:, :, 1, :], XTB[1]
    ps = es.enter_context(nc.psum_tensor("ps", [128, 8, 512], F32))
    psB = [Buf("ps%d" % b, excl=True) for b in range(8)]
    st = {"p1": 0, "p2": 0, "ws": 0}

    def PS1():
        b = st["p1"]
        st["p1"] = (b + 1) % 4
        return ps[:, b, :], psB[b]

    def PS2():
        k = st["p2"]
        st["p2"] = (k + 1) % 2
        b = 4 + 2 * k
        return ps[:, b:b + 2, :].rearrange("p a b -> p (a b)"), [psB[b], psB[b + 1]]

    ident_f = cst[:, C_ID:C_ID + 128]
    ident_b = cbf[:, 0:128]
    blk_b = cbf[:, 128:256]
    ones_b = cbf[:, 256:384]

    op = S.op

    def bc(ap, shape):
        return ap.to_broadcast(list(shape))

    def wload(src, ncols):
        i = st["ws"]
        st["ws"] = (i + 1) % NWS
        S.dma("pool", wsl[i][:, 0:ncols], src, r=(), w=[wslb[i]], max_dma_last_dim=8192)
        return wsl[i], wslb[i]

    class WStream:
        def __init__(self, items, depth=2):
            self.items = items
            self.depth = depth
            self.issued = []
            self.pos = 0

        def _issue(self):
            k = len(self.issued)
            if k < len(self.items):
                src, ncols, _ = self.items[k]
                self.issued.append(wload(src, ncols))

        def get(self, tag):
            while len(self.issued) < min(len(self.items), self.pos + 1 + self.depth):
                self._issue()
            assert self.items[self.pos][2] == tag, (self.items[self.pos][2], tag)
            t = self.issued[self.pos]
            self.pos += 1
            return t

    passes = []
    for ci in range(n_chunks):
        for l in range(n_layers):
            passes.append(("p", ci, l))
    if do_sample:
        for l in range(n_layers):
            passes.append(("s", 0, l))

    items = []
    for l in range(n_layers):
        for g in range(48):
            items.append((wada_d[l, g], 4096, ("ada", l, g)))
    for (kind, ci, l) in passes:
        for g in range(22):
            items.append((win_d[l, g], 4096, ("in", l, g)))
        for g in range(2):
            items.append((wkvt_d[l, g], 4096, ("kv", l, g)))
        for g in range(8):
            items.append((wout_d[l, g], 4096, ("out", l, g)))
        for g in range(44):
            items.append((wffi_d[l, g], 4096, ("ffi", l, g)))
        for g in range(32):
            items.append((wffo_d[l, g], 2816, ("ffo", l, g)))
    WS = WStream(items)

    def checkpoint(name):
        if cfg.get("stop") == name:
            raise _Stop()

    def tap(name, ap, bufs, dt=F32):
        if name in taps:
            d = dout("tap_" + name, list(ap.shape), dt)
            S.dma("sp", d, ap, r=bufs, w=[dram_out_buf])

    S.dma("sp", cst[:], cst_d, w=[cstB])
    S.dma("sp", vecs[:], vecs_d, w=[vecB])
    S.dma("sp", sinks[:], sinks_d, w=[vecB])
    S.dma("sp", c5[:], c5T, w=[modB])
    S.dma("sp", gfin[:], gfin_d, w=[vecB])
    op("dve", "tensor_copy", cbf[:, 0:256], cst[:, C_ID:C_ID + 256], r=[cstB], w=[cstB])
    op("dve", "memset", cbf[:, 256:384], 1.0, w=[cstB])
    op("dve", "memset", smallc[:, 0:1], RMS_EPS, w=[cstB])
    op("dve", "memset", smallc[:, 1:2], GN_EPS, w=[cstB])
    op("dve", "memset", smallc[:, 2:3], 0.0, w=[cstB])
    op("dve", "memset", smallc[:, 3:4], 1.0, w=[cstB])
    op("dve", "memset", ones_f[:], 1.0, w=[cstB])
    for l in range(L):
        op("dve", "tensor_scalar", omk[:, l, :], vecs[:, l, V_KA:V_KA + 8], -1.0, 1.0, ALU.mult, ALU.add,
           r=[vecB], w=[vecB])
        op("dve", "memset", Sst[:, l], 0.0, w=[SstB[l]])
        op("dve", "memset", prevrow[:, l], 0.0, w=[prevrowB[l]])
        op("dve", "memset", prevK[:, l], 0.0, w=[prevKB[l]])
        op("dve", "memset", prevV[:, l], 0.0, w=[prevVB[l]])

    stopped = False
    try:
        tap("scT", c5[:], [modB])
        checkpoint("consts")
        op("act", "activation", scT[:], c5[:], AF.Silu, r=[modB], w=[modB])
        for l in range(n_layers):
            pm, pmB = PS1()
            for g2 in range(48):
                if cfg.get("stop") == "ada1" and g2 == cfg.get("ngrp", 1):
                    raise _Stop()
                wt, wb = WS.get(("ada", l, g2))
                wv = wt[:, :].rearrange("p (g k c) -> p g k c", g=2, k=KC)
                for g in range(2):
                    m = g2 * 2 + g
                    for kc in range(KC):
                        op("pe", "matmul", pm[:, m * 5:m * 5 + 5], wv[:, g, kc, :], scT[:, kc, :],
                           start=(kc == 0), stop=(kc == KC - 1), r=[wb, modB], w=[pmB])
            checkpoint("ada_mm")
            pm3 = pm[:, 0:480].rearrange("p (m s) -> p m s", s=5)
            op("dve", "tensor_tensor", modT[:, l], pm3, bc(vecs[:, l, V_BADA:V_BADA + 96].unsqueeze(2), [128, 96, 5]),
               ALU.add, r=[pmB, vecB], w=[modB])
            checkpoint("ada_tt")
            for (lo, voff) in ((16, V_GMIX), (64, V_GFFN)):
                op("dve", "tensor_scalar", modT[:, l, lo:lo + 16, :], modT[:, l, lo:lo + 16, :], 1.0, None, ALU.add,
                   r=[modB], w=[modB])
                op("dve", "tensor_tensor", modT[:, l, lo:lo + 16, :], modT[:, l, lo:lo + 16, :],
                   bc(vecs[:, l, voff:voff + 16].unsqueeze(2), [128, 16, 5]), ALU.mult, r=[vecB, modB], w=[modB])
    except _Stop:
        stopped = True
    tap("modT", modT[:, 0:n_layers], [modB])


    def modap(l, sec, kc, N, sample):
        if sample:
            return modT[:, l, sec * 16 + kc, 1:1 + NS]
        return bc(modT[:, l, sec * 16 + kc, 0:1], [128, N])

    def norm_phase(N, sample, A_of, B_of, out_of, outB_of, out_dt_bf16=True):
        pss, pssB = PS1()
        for kc in range(KC):
            i = kc % 2
            op("act", "activation", sqb[:, i, :N], xT[:, kc, :N], AF.Square, r=[xTb[kc]], w=[sqB[i]])
            op("pe", "matmul", pss[:, :N], ones_b, sqb[:, i, :N], start=(kc == 0), stop=(kc == KC - 1),
               r=[sqB[i], cstB], w=[pssB])
        op("act", "activation", rstd[:, :N], pss[:, :N], AF.Sqrt, bias=smallc[:, 0:1], scale=1.0 / D,
           r=[pssB, cstB], w=[rstdB])
        op("dve", "reciprocal", rstd[:, :N], rstd[:, :N], r=[rstdB], w=[rstdB])
        checkpoint("norm_a")
        for kc in range(KC):
            i = kc % 2
            op("dve", "tensor_tensor", tmpn[:, i, :N], xT[:, kc, :N], rstd[:, :N], ALU.mult,
               r=[xTb[kc], rstdB], w=[tmpnB[i]])
            A, B = A_of(kc), B_of(kc)
            if B is None:
                op("dve", "tensor_tensor", out_of(kc), tmpn[:, i, :N], A, ALU.mult,
                   r=[tmpnB[i], modB, vecB], w=[outB_of(kc)])
            else:
                op("pool", "tensor_tensor", tmpn[:, i, :N], tmpn[:, i, :N], A, ALU.mult,
                   r=[tmpnB[i], modB], w=[tmpnB[i]])
                op("dve", "tensor_tensor", out_of(kc), tmpn[:, i, :N], B, ALU.add,
                   r=[tmpnB[i], modB], w=[outB_of(kc)])

    def resid_add(l, sec, N, sample, mo, psap, pB):
        G = modap(l, sec, mo, N, sample)
        i = mo % 2
        op("dve", "tensor_tensor", ftmp[:, i, :N], psap, G, ALU.mult, r=[pB, modB], w=[ftmpB[i]])
        op("pool", "tensor_tensor", xT[:, mo, :N], xT[:, mo, :N], ftmp[:, i, :N], ALU.add,
           r=[ftmpB[i], xTb[mo]], w=[xTb[mo]])

    def attn_group(l, g, Q, nk, q_of, k_of, dm, pv_terms, out_ap, out_bufs, rbufs):
        ai = attn_state["i"]
        attn_state["i"] = 1 - ai
        pbt, pbB, pTt, pTt_f, pTB = pbt2[ai], pbB2[ai], pTt2[ai], pTtf2[ai], pTB2[ai]
        p2, p2B = PS2()
        lgv = p2[:, :].rearrange("p (r k) -> p r k", r=4)
        for r in range(4):
            h = 4 * g + r
            op("pe", "matmul", lgv[0:Q, r, 0:nk], q_of(h), k_of(h), start=True, stop=True, r=rbufs, w=p2B)
        for r in range(4):
            h = 4 * g + r
            op("dve", "tensor_scalar", lg[0:Q, r, 0:nk], dm, -SLOPES[h], None, ALU.mult, r=[cstB], w=[lgB])
        checkpoint("att_mm")
        op("dve", "tensor_tensor", lg[0:Q, :, 0:nk], lg[0:Q, :, 0:nk], lgv[0:Q, :, 0:nk], ALU.add, r=p2B + [lgB], w=[lgB])
        checkpoint("att_lg")
        mx, rs, sd, den = ast[0:Q, 0, :], ast[0:Q, 1, :], ast[0:Q, 2, :], ast[0:Q, 3, :]
        sk = sinks[0:Q, l, 4 * g:4 * g + 4]
        op("dve", "tensor_reduce", mx, lg[0:Q, :, 0:nk], AX.X, ALU.max, r=[lgB], w=[astB])
        op("dve", "tensor_tensor", mx, mx, sk, ALU.max, r=[astB, vecB], w=[astB])
        op("dve", "tensor_tensor", lg[0:Q, :, 0:nk], lg[0:Q, :, 0:nk], bc(mx.unsqueeze(2), [Q, 4, nk]), ALU.subtract,
           r=[lgB, astB], w=[lgB])
        op("act", "activation", lg[0:Q, :, 0:nk], lg[0:Q, :, 0:nk], AF.Exp, r=[lgB], w=[lgB])
        op("dve", "tensor_reduce", rs, lg[0:Q, :, 0:nk], AX.X, ALU.add, r=[lgB], w=[astB])
        op("dve", "tensor_tensor", sd, sk, mx, ALU.subtract, r=[astB, vecB], w=[astB])
        op("act", "activation", sd, sd, AF.Exp, r=[astB], w=[astB])
        op("dve", "tensor_tensor", den, rs, sd, ALU.add, r=[astB], w=[astB])
        op("dve", "reciprocal", den, den, r=[astB], w=[astB])
        op("dve", "tensor_tensor", pbt[0:Q, :, 0:nk], lg[0:Q, :, 0:nk], bc(den.unsqueeze(2), [Q, 4, nk]), ALU.mult,
           r=[lgB, astB], w=[pbB])
        checkpoint("att_sm")
        pt1, pt1B = PS1()
        ptb = pt1.bitcast(BF16)
        if Q == 128:
            ptv = ptb.rearrange("p (a q) -> p a q", q=128)
            for r in range(4):
                for kb in range(2):
                    op("pe", "transpose", ptv[:, r * 2 + kb, :], pbt[:, r, kb * 128:(kb + 1) * 128], ident_b,
                       r=[pbB, cstB], w=[pt1B])
            op("dve", "tensor_copy", pTt_f[:, :], pt1[:, :], r=[pt1B], w=[pTB])
        else:
            for r in range(4):
                op("pe", "transpose", ptb[:, 2 * r:2 * r + 1], pbt[0:1, r, 0:128], ident_b[0:1, 0:1],
                   r=[pbB, cstB], w=[pt1B])
            op("dve", "tensor_copy", pTt_f[:, 0:4], pt1[:, 0:4], r=[pt1B], w=[pTB])
        checkpoint("att_T")
        po, poB = PS1()
        pov = po[:, 0:2 * Q].rearrange("p (c q) -> p c q", c=2)
        for r in range(4):
            h = 4 * g + r
            half = h % 2
            c = (h // 2) - 2 * g
            terms = pv_terms(r, pTt, pTB, pbt, pbB)
            for ti, (lt, rh, rb) in enumerate(terms):
                op("pe", "matmul", pov[half * 64:half * 64 + 64, c, :], lt, rh, start=(ti == 0),
                   stop=(ti == len(terms) - 1), r=rb, w=[poB])
        op("act", "activation", out_ap, pov, AF.Copy, r=[poB], w=out_bufs)

    def rwkv_prep(l, nt, prev_ap, cur_ap, rbufs):
        P3 = [128, NPC, nt]
        mixb = bc(vecs[:, l, V_MIX:V_MIX + NPC].unsqueeze(2), P3)
        op("dve", "tensor_tensor", psb[:, :, 0:nt], prev_ap, cur_ap, ALU.subtract, r=rbufs, w=[psbB])
        op("dve", "tensor_tensor", psb[:, :, 0:nt], psb[:, :, 0:nt], mixb, ALU.mult, r=[psbB, vecB], w=[psbB])
        op("dve", "tensor_tensor", psb[:, :, 0:nt], psb[:, :, 0:nt], cur_ap, ALU.add, r=[psbB] + rbufs, w=[psbB])
        r_, k_, v_ = psb[:, 0:8, 0:nt], psb[:, 8:16, 0:nt], psb[:, 16:24, 0:nt]
        op("act", "activation", tl[0:64, 0:nt], psb[0:64, 24, 0:nt], AF.Tanh, r=[psbB], w=[tlB])
        op("act", "activation", tl[64:128, 0:nt], psb[64:128, 24, 0:nt], AF.Copy, r=[psbB], w=[tlB])
        op("act", "activation", sg[:, :, 0:nt], psb[:, 25:27, 0:nt], AF.Sigmoid, r=[psbB], w=[sgB])
        up1w = lupb[:, 0:1024]
        up1a = lupb[:, 1024:2048]
        gup = lupb[:, 2048:4096].rearrange("p (k c) -> p k c", k=2)
        T = [t[:, :, 0:nt] for t in TT]
        B8 = [128, 8, nt]
        pw, pwB = PS1()
        pa, paB = PS1()
        pg, pgB = PS1()
        pwv = pw[:, 0:8 * nt].rearrange("p (m t) -> p m t", m=8)
        pav = pa[:, 0:8 * nt].rearrange("p (m t) -> p m t", m=8)
        pgv = pg[:, 0:8 * nt].rearrange("p (m t) -> p m t", m=8)
        for m in range(8):
            cs_ = slice(m * 128, (m + 1) * 128)
            op("pe", "matmul", pwv[:, m, :], up1w[:, cs_], tl[:, 0:nt], start=True, stop=True, r=[tlB, lupB], w=[pwB])
            op("pe", "matmul", pav[:, m, :], up1a[:, cs_], tl[:, 0:nt], start=True, stop=True, r=[tlB, lupB], w=[paB])
            op("pe", "matmul", pgv[:, m, :], gup[:, 0, cs_], sg[:, 0, 0:nt], start=True, stop=False, r=[sgB, lupB], w=[pgB])
            op("pe", "matmul", pgv[:, m, :], gup[:, 1, cs_], sg[:, 1, 0:nt], start=False, stop=True, r=[sgB, lupB], w=[pgB])

        def vb(off):
            return bc(vecs[:, l, off:off + 8].unsqueeze(2), B8)

        op("dve", "tensor_tensor", T[0], pwv, vb(V_W0), ALU.add, r=[pwB, vecB], w=[TTB[0]])
        op("act", "activation", T[0], T[0], AF.Sigmoid, r=[TTB[0]], w=[TTB[0]])
        op("dve", "tensor_tensor", T[1], pav, vb(V_A0), ALU.add, r=[paB, vecB], w=[TTB[1]])
        op("act", "activation", T[1], T[1], AF.Sigmoid, r=[TTB[1]], w=[TTB[1]])
        op("act", "activation", gg[:, :, 0:nt], pgv, AF.Copy, r=[pgB], w=[ggB])
        if nt == 64:
            for m in range(8):
                op("dve", "tensor_tensor_scan", TT[3][:, m, :], ones_f[:, :], TT[0][:, m, :], 0.0,
                   ALU.mult, ALU.add, r=[TTB[0], cstB], w=[TTB[3]])
            lastb = bc(TT[3][:, :, 63:64], B8)
        else:
            op("dve", "tensor_copy", T[3], T[0], r=[TTB[0]], w=[TTB[3]])
            lastb = T[3]
        op("act", "activation", T[4], T[3], AF.Exp, scale=-C0, r=[TTB[3]], w=[TTB[4]])
        op("dve", "tensor_tensor", T[0], T[3], T[0], ALU.subtract, r=[TTB[3], TTB[0]], w=[TTB[0]])
        op("act", "activation", T[0], T[0], AF.Exp, scale=-C0, r=[TTB[0]], w=[TTB[0]])
        op("act", "activation", T[5], T[3], AF.Exp, scale=C0, r=[TTB[3]], w=[TTB[5]])
        if nt == 64:
            op("dve", "tensor_copy", pcol[:, :], TT[4][:, :, 63], r=[TTB[4]], w=[pcolB])
            op("dve", "tensor_tensor", T[3], lastb, T[3], ALU.subtract, r=[TTB[3]], w=[TTB[3]])
            op("act", "activation", T[3], T[3], AF.Exp, scale=-C0, r=[TTB[3]], w=[TTB[3]])
        else:
            op("dve", "tensor_copy", pcolS[:, :, 0:nt], T[4], r=[TTB[4]], w=[pcolB])
            op("dve", "memset", T[3], 1.0, w=[TTB[3]])
        op("dve", "tensor_tensor", T[6], k_, vb(V_KK), ALU.mult, r=[psbB, vecB], w=[TTB[6]])
        op("dve", "tensor_tensor", tmpb[:, :, 0:nt], T[6], T[6], ALU.mult, r=[TTB[6]], w=[tmpbB])
        pq, pqB = PS1()
        pqv = pq[:, 0:8 * nt].rearrange("p (m t) -> p m t", m=8)
        op("pe", "matmul", pqv, blk_b, tmpb[:, :, 0:nt], start=True, stop=True, r=[tmpbB, cstB], w=[pqB])
        op("act", "activation", T[7], pqv, AF.Sqrt, r=[pqB], w=[TTB[7]])
        op("dve", "tensor_scalar", T[7], T[7], 1e-12, None, ALU.max, r=[TTB[7]], w=[TTB[7]])
        op("dve", "reciprocal", T[7], T[7], r=[TTB[7]], w=[TTB[7]])
        op("dve", "tensor_tensor", T[6], T[6], T[7], ALU.mult, r=[TTB[6], TTB[7]], w=[TTB[6]])
        op("dve", "tensor_tensor", T[7], T[1], vb(V_KA), ALU.mult, r=[TTB[1], vecB], w=[TTB[7]])
        op("dve", "tensor_tensor", T[7], T[7], bc(omk[:, l, :].unsqueeze(2), B8), ALU.add, r=[TTB[7], vecB], w=[TTB[7]])
        op("dve", "tensor_tensor", T[7], T[7], k_, ALU.mult, r=[TTB[7], psbB], w=[TTB[7]])
        op("dve", "tensor_tensor", k_, r_, T[7], ALU.mult, r=[psbB, TTB[7]], w=[psbB])
        op("dve", "tensor_tensor", tmpb[:, :, 0:nt], k_, vb(V_RK), ALU.mult, r=[psbB, vecB], w=[tmpbB])
        pq2, pq2B = PS1()
        pq2v = pq2[:, 0:8 * nt].rearrange("p (m t) -> p m t", m=8)
        op("pe", "matmul", pq2v, blk_b, tmpb[:, :, 0:nt], start=True, stop=True, r=[tmpbB, cstB], w=[pq2B])
        op("dve", "tensor_tensor", bon[:, :, 0:nt], pq2v, v_, ALU.mult, r=[pq2B, psbB], w=[bonB])
        O = {n: opb[n][:, :, 0:nt] for n in OPN}
        for nm_, lo in (("rte", 0), ("rto", 64)):
            op("dve", "tensor_tensor", O[nm_][lo:lo + 64], r_[lo:lo + 64], T[4][lo:lo + 64], ALU.mult,
               r=[psbB, TTB[4]], w=[opB[nm_]])
        op("dve", "tensor_tensor", T[0], T[6], T[0], ALU.mult, r=[TTB[6], TTB[0]], w=[TTB[0]])
        for nm_, lo in (("ate", 0), ("ato", 64)):
            op("dve", "tensor_scalar", O[nm_][lo:lo + 64], T[0][lo:lo + 64], -1.0, None, ALU.mult, r=[TTB[0]], w=[opB[nm_]])
        op("dve", "tensor_tensor", T[1], T[6], T[1], ALU.mult, r=[TTB[6], TTB[1]], w=[TTB[1]])
        op("dve", "tensor_tensor", O["bt"], T[1], T[5], ALU.mult, r=[TTB[1], TTB[5]], w=[opB["bt"]])
        op("dve", "tensor_tensor", O["bh"], T[1], T[3], ALU.mult, r=[TTB[1], TTB[3]], w=[opB["bh"]])
        op("dve", "tensor_tensor", O["kt"], T[7], T[5], ALU.mult, r=[TTB[7], TTB[5]], w=[opB["kt"]])
        op("dve", "tensor_tensor", O["kh"], T[7], T[3], ALU.mult, r=[TTB[7], TTB[3]], w=[opB["kh"]])
        op("act", "activation", O["vb"], v_, AF.Copy, r=[psbB], w=[opB["vb"]])

    def rwkv_tokmajor(t0, C):
        for n, src in (("v", "vb"), ("bh", "bh"), ("kh", "kh")):
            p1, p1B = PS1()
            pb16 = p1.bitcast(BF16).rearrange("p (m f) -> p m f", f=128)
            for m in range(8):
                op("pe", "transpose", pb16[0:C, m, :], opb[src][:, m, t0:t0 + C], ident_b, r=[opB[src], cstB], w=[p1B])
            op("dve", "tensor_copy", tkm_f[n][0:C, :], p1[0:C, :], r=[p1B], w=[tkmB[n]])

    def rwkv_chunk(l, t0, C, pcol_ap, y_tok_out):
        mlt = cst[0:C, C_MLT:C_MLT + C]
        mut = cst[0:C, C_MUT:C_MUT + C]
        mue = cst[0:C, C_MUE:C_MUE + C]
        idb = ident_f[0:C, 0:C]

        def fm(name, h):
            if name in ("at", "rt"):
                name = name + ("e" if h % 2 == 0 else "o")
            return opb[name][:, h // 2, t0:t0 + C]

        def fmB(name, h):
            if name in ("at", "rt"):
                name = name + ("e" if h % 2 == 0 else "o")
            return opB[name]

        def tk(name, h):
            return tkm[name][0:C, h // 2, (h % 2) * 64:(h % 2) * 64 + 64]

        def sbh(h):
            return Sbf[:, h // 2, :]

        def hb(ap):
            return bc(ap.unsqueeze(1), [C, NH, C])

        allop = [opB[n_] for n_ in ("rte", "rto", "ate", "ato", "bt", "kt")]

        def pair(nameL, nameR, outn, mask, rb):
            p2, p2B = PS2()
            pv = p2[:, :].rearrange("p (h s) -> p h s", h=NH)
            for h in range(NH):
                op("pe", "matmul", pv[0:C, h, 0:C], fm(nameL, h), fm(nameR, h), start=True, stop=True, r=rb, w=p2B)
            op("dve", "tensor_tensor", mat[outn][0:C, :, 0:C], pv[0:C, :, 0:C], hb(mask), ALU.mult, r=p2B + [cstB], w=[matB[outn]])

        def pair2(nameL, outA, outB):
            for half in range(2):
                p2, p2B = PS2()
                pv = p2[:, :].rearrange("p (h w s) -> p h w s", h=8, w=2)
                for hh in range(8):
                    h = half * 8 + hh
                    op("pe", "matmul", pv[0:C, hh, :, :], fm(nameL, h), ar2[:, h // 2, h % 2, :, t0:t0 + C],
                       start=True, stop=True, r=allop, w=p2B)
                hs = slice(half * 8, half * 8 + 8)
                op("dve", "tensor_tensor", mat[outA][0:C, hs, 0:C], pv[0:C, :, 0, 0:C], bc(mut.unsqueeze(1), [C, 8, C]),
                   ALU.mult, r=p2B + [cstB], w=[matB[outA]])
                op("dve", "tensor_tensor", mat[outB][0:C, hs, 0:C], pv[0:C, :, 1, 0:C], bc(mue.unsqueeze(1), [C, 8, C]),
                   ALU.mult, r=p2B + [cstB], w=[matB[outB]])

        if C > 1:
            pair("at", "bt", "M0", mlt, allop)
            pair2("bt", "N0", "Rb")
            pair2("kt", "Ak", "Rk")
        else:
            pair("bt", "rt", "Rb", mue, allop)
            pair("kt", "rt", "Rk", mue, allop)
        p2, p2B = PS2()
        pv = p2[:, :].rearrange("p (h s) -> p h s", h=NH)
        for h in range(NH):
            op("pe", "matmul", pv[0:C, h, :], fm("at", h), sbh(h), start=True, stop=(C == 1), r=allop + [SbfB], w=p2B)
            if C > 1:
                op("pe", "matmul", pv[0:C, h, :], mat["Ak"][0:C, h, 0:C], tk("v", h), start=False, stop=True,
                   r=[matB["Ak"], tkmB["v"]], w=p2B)
        op("act", "activation", mat["RH"][0:C], pv[0:C], AF.Copy, r=p2B, w=[matB["RH"]])
        if C > 1:
            XA, XB = XT[0], XT[1]
            op("dve", "tensor_tensor", XA[0:C, :, 0, 0:C], XB[0:C, :, 1, 0:C], hb(idb), ALU.add,
               r=[XTB[1], cstB], w=[XTB[0]])
            p2, p2B = PS2()
            pm_ = p2[:, :].rearrange("p (h s) -> p h s", h=NH)
            for h in range(NH):
                op("pe", "matmul", pm_[0:C, h, 0:C], mat["M0"][0:C, h, 0:C], XB[0:C, h, 1, 0:C], start=True, stop=True,
                   r=[matB["M0"], XTB[1]], w=p2B)
            op("act", "activation", XA[0:C, :, 1, 0:C], pm_[0:C, :, 0:C], AF.Copy, r=p2B, w=[XTB[0]])
            curM = "M0"
            xp, xn = 0, 1
            for lev in range(1, 6):
                nM = "M1" if curM == "M0" else "M0"
                p2, p2B = PS2()
                pm_ = p2[:, :].rearrange("p (h s) -> p h s", h=NH)
                for h in range(NH):
                    op("pe", "matmul", pm_[0:C, h, 0:C], XT[xn][0:C, h, 1, 0:C], mat[curM][0:C, h, 0:C], start=True, stop=True,
                       r=[XTB[xn], matB[curM]], w=p2B)
                op("dve", "tensor_tensor", mat["IM"][0:C, :, 0:C], pm_[0:C, :, 0:C], hb(idb), ALU.add, r=p2B + [cstB], w=[matB["IM"]])
                if lev < 5:
                    op("act", "activation", mat[nM][0:C, :, 0:C], pm_[0:C, :, 0:C], AF.Copy, r=p2B, w=[matB[nM]])
                    for half in range(2):
                        p3, p3B = PS2()
                        pv3 = p3[:, :].rearrange("p (h w s) -> p h w s", h=8, w=2)
                        hs = slice(half * 8, half * 8 + 8)
                        for hh in range(8):
                            h = half * 8 + hh
                            op("pe", "matmul", pv3[0:C, hh, :, :], mat["IM"][0:C, h, 0:C], XT[xp][0:C, h, :, 0:C],
                               start=True, stop=True, r=[matB["IM"], XTB[xp]], w=p3B)
                        op("act", "activation", XT[xn][0:C, hs, 0, 0:C], pv3[0:C, :, 0, 0:C], AF.Copy, r=p3B, w=[XTB[xn]])
                        op("dve", "tensor_tensor", XT[xn][0:C, hs, 1, 0:C], pv3[0:C, :, 1, 0:C], XT[xp][0:C, hs, 1, 0:C],
                           ALU.subtract, r=p3B + [XTB[xp]], w=[XTB[xn]])
                    xp, xn = xn, xp
                else:
                    p4, p4B = PS2()
                    pp_ = p4[:, :].rearrange("p (h s) -> p h s", h=NH)
                    for h in range(NH):
                        op("pe", "matmul", pp_[0:C, h, 0:C], mat["IM"][0:C, h, 0:C], XT[xp][0:C, h, 0, 0:C], start=True, stop=True,
                           r=[matB["IM"], XTB[xp]], w=p4B)
                    op("dve", "tensor_copy", XT[xn][0:C, :, 0, 0:C], pp_[0:C, :, 0:C], r=p4B, w=[XTB[xn]])
                curM = nM
            Pfin, PfinB = XT[xn], XTB[xn]
            p2, p2B = PS2()
            pu = p2[:, :].rearrange("p (h s) -> p h s", h=NH)
            for h in range(NH):
                op("pe", "matmul", pu[0:C, h, :], Pfin[0:C, h, 0, 0:C], mat["RH"][0:C, h, :], start=True, stop=True,
                   r=[PfinB, matB["RH"]], w=p2B)
            op("act", "activation", mat["U"][0:C], pu[0:C], AF.Copy, r=p2B, w=[matB["U"]])
            Un = "U"
        else:
            Un = "RH"
        p2, p2B = PS2()
        py = p2[:, :].rearrange("p (h s) -> p h s", h=NH)
        for h in range(NH):
            op("pe", "matmul", py[0:C, h, :], fm("rt", h), sbh(h), start=True, stop=False, r=allop + [SbfB], w=p2B)
            op("pe", "matmul", py[0:C, h, :], mat["Rb"][0:C, h, 0:C], mat[Un][0:C, h, :], start=False, stop=False,
               r=[matB["Rb"], matB[Un]], w=p2B)
            op("pe", "matmul", py[0:C, h, :], mat["Rk"][0:C, h, 0:C], tk("v", h), start=False, stop=True,
               r=[matB["Rk"], tkmB["v"]], w=p2B)
        op("act", "activation", ysb[0:C], py[0:C], AF.Copy, r=p2B, w=[ysbB])
        p1, p1B = PS1()
        pS = p1[:, :].rearrange("p (m i) -> p m i", m=8)
        for h in range(NH):
            hp = (h % 2) * 64
            op("pe", "matmul", pS[hp:hp + 64, h // 2, :], tk("bh", h), mat[Un][0:C, h, :], start=True, stop=False,
               r=[tkmB["bh"], matB[Un]], w=[p1B])
            op("pe", "matmul", pS[hp:hp + 64, h // 2, :], tk("kh", h), tk("v", h), start=False, stop=True,
               r=[tkmB["kh"], tkmB["v"]], w=[p1B])
        op("pool", "tensor_tensor", Scur[:], Scur[:], bc(pcol_ap.unsqueeze(2), [128, 8, 64]), ALU.mult,
           r=[ScurB, pcolB], w=[ScurB])
        op("dve", "tensor_tensor", Scur[:], Scur[:], pS, ALU.add, r=[ScurB, p1B], w=[ScurB])
        op("act", "activation", Sbf[:], Scur[:], AF.Copy, r=[ScurB], w=[SbfB])
        mu, var = yst[0:C, 0, :], yst[0:C, 1, :]
        op("dve", "tensor_reduce", mu, ysb[0:C], AX.X, ALU.add, r=[ysbB], w=[ystB])
        op("dve", "tensor_scalar", mu, mu, 1.0 / 64, None, ALU.mult, r=[ystB], w=[ystB])
        op("dve", "tensor_tensor", ysb[0:C], ysb[0:C], bc(mu.unsqueeze(2), [C, NH, 64]), ALU.subtract, r=[ysbB, ystB], w=[ysbB])
        op("pool", "tensor_tensor", ytmp[0:C], ysb[0:C], ysb[0:C], ALU.mult, r=[ysbB], w=[ytmpB])
        op("dve", "tensor_reduce", var, ytmp[0:C], AX.X, ALU.add, r=[ytmpB], w=[ystB])
        op("act", "activation", var, var, AF.Sqrt, bias=smallc[0:C, 1:2], scale=1.0 / 64, r=[ystB, cstB], w=[ystB])
        op("dve", "reciprocal", var, var, r=[ystB], w=[ystB])
        op("dve", "tensor_tensor", ytmp[0:C], ysb[0:C], bc(var.unsqueeze(2), [C, NH, 64]), ALU.mult, r=[ysbB, ystB], w=[ytmpB])

    def rwkv_post(l, nt, pz, pzB, out_ap, out_bufs):
        B8 = [128, 8, nt]
        z = zt[:, :, 0:nt]
        op("dve", "tensor_tensor", z, pz, bc(vecs[:, l, V_LNW:V_LNW + 8].unsqueeze(2), B8), ALU.mult, r=[pzB, vecB], w=[ztB])
        op("dve", "tensor_tensor", z, z, bc(vecs[:, l, V_LNB:V_LNB + 8].unsqueeze(2), B8), ALU.add, r=[ztB, vecB], w=[ztB])
        op("dve", "tensor_tensor", z, z, bon[:, :, 0:nt], ALU.add, r=[ztB, bonB], w=[ztB])
        op("dve", "tensor_tensor", out_ap, z, gg[:, :, 0:nt], ALU.mult, r=[ztB, ggB], w=out_bufs)

    def state_out(dst):
        p2, p2B = PS2()
        pv = p2[:, :].rearrange("p (m f) -> p m f", m=8)
        for m in range(8):
            op("pe", "transpose", pv[0:64, m, :], Scur[:, m, :], ident_f, r=[ScurB, cstB], w=p2B)
        op("act", "activation", stout[:].rearrange("p h j -> p (h j)"), p2[0:64, :], AF.Copy, r=p2B, w=[stoutB])
        S.dma("sp", dst.rearrange("h i j -> i h j"), stout[:], r=[stoutB], w=[dram_out_buf])

    def run_pass(kind, ci, l):
        sample = (kind == "s")
        N = NS if sample else NT
        first_chunk = (ci == 0)
        last_chunk = sample or (ci == n_chunks - 1)
        if l == 0:
            if sample:
                S.dma("sp", xT[:, :, 0:NS], xsT, w=xTb)
            else:
                for kc in range(KC):
                    S.dma("sp", xT[:, kc, :], xpT[kc, :, ci * NT:(ci + 1) * NT], w=[xTb[kc]])
        S.barrier()
        S.dma("pool", lupb[:], lup_d[l], w=[lupB], max_dma_last_dim=8192)
        norm_phase(N, sample, lambda kc: modap(l, 1, kc, N, sample), lambda kc: modap(l, 0, kc, N, sample),
                   lambda kc: hT[:, kc, :N], lambda kc: hTb[kc])
        checkpoint("norm")
        if not sample:
            op("dve", "tensor_copy", pc[:, :, 0], prevrow[:, l, :], r=[prevrowB[l]], w=pcB)
            op("dve", "tensor_copy", kdT[:, :, 0:128], prevK[:, l], r=[prevKB[l]], w=kdTb)
            op("dve", "tensor_copy", vtok[:, 0, :], prevV[:, l, :], r=[prevVB[l]], w=[vtokB[0]])
        checkpoint("prevcopy")
        ev = 0
        for g2 in range(22):
            wt, wb = WS.get(("in", l, g2))
            wv = wt[:, :].rearrange("p (g k c) -> p g k c", g=2, k=KC)
            for g in range(2):
                m = g2 * 2 + g
                if m >= 43:
                    continue
                if cfg.get("stop") == "projn" and m == cfg.get("nm", 1):
                    raise _Stop()
                p1, p1B = PS1()
                for kc in range(KC):
                    op("pe", "matmul", p1[:, :N], wv[:, g, kc, :], hT[:, kc, :N], start=(kc == 0), stop=(kc == KC - 1),
                       r=[wb, hTb[kc]], w=[p1B])
                if m < 8:
                    op("act", "activation", qT[:, m, :N], p1[:, :N], AF.Copy, scale=0.125, r=[p1B], w=[qTb[m]])
                elif m < 16:
                    gq = m - 8
                    if sample:
                        op("dve", "tensor_copy", kdTs[:, gq, :], p1[:, :N], r=[p1B], w=[kdTsB])
                    else:
                        op("dve", "tensor_copy", kdT[:, gq, 128:128 + N], p1[:, :N], r=[p1B], w=[kdTb[gq]])
                else:
                    c = m - 16
                    off = 0 if sample else 1
                    if ev % 2 == 0:
                        op("act", "activation", pc[:, c, off:off + N], p1[:, :N], AF.Copy, r=[p1B], w=[pcB[c]])
                    else:
                        op("dve", "tensor_copy", pc[:, c, off:off + N], p1[:, :N], r=[p1B], w=[pcB[c]])
                    ev += 1
                    if last_chunk:
                        if sample:
                            op("dve", "tensor_copy", lastrow[:, c, :], p1[:, 0:NS], r=[p1B], w=[lastrowB])
                        else:
                            op("dve", "tensor_copy", lastrow[:, c, 0:1], p1[:, N - 1:N], r=[p1B], w=[lastrowB])
        checkpoint("projmm")
        wkt, wkb = WS.get(("kv", l, 0))
        wk3 = wkt[:, :].rearrange("p (k c) -> p k c", k=KC)
        if not sample:
            if last_chunk:
                p1, p1B = PS1()
                for kc in range(KC):
                    op("pe", "matmul", p1[:, 0:256], hT[:, kc, NT - 128:NT], wk3[:, kc, :], start=(kc == 0),
                       stop=(kc == KC - 1), r=[wkb, hTb[kc]], w=[p1B])
                op("act", "activation", kvout[:, 0, :], p1[:, 0:256], AF.Copy, r=[p1B], w=[kvoutB])
                S.dma("sp", nk_p[l], kvout[:, 0, :], r=[kvoutB], w=[dram_out_buf])
        else:
            p1, p1B = PS1()
            p1b_, p1bB = PS1()
            for s in range(NS):
                pso = (p1 if s < 2 else p1b_)[:, :].rearrange("p (s c) -> p s c", c=256)
                psoB = p1B if s < 2 else p1bB
                for kc in range(KC):
                    op("pe", "matmul", pso[0:1, s % 2, :], hT[:, kc, s:s + 1], wk3[:, kc, :], start=(kc == 0),
                       stop=(kc == KC - 1), r=[wkb, hTb[kc]], w=[psoB])
                op("act", "activation", kvnew_f[0:1, 0, s, :], pso[0:1, s % 2, :], AF.Copy, r=[psoB], w=[kvnewB])
            S.dma("sp", nk_s[l, :, 127:128, :].rearrange("s o d -> o s d"), kvnew_f[0:1, 0], r=[kvnewB], w=[dram_out_buf])
        wvt, wvb = WS.get(("kv", l, 1))
        wv3 = wvt[:, :].rearrange("p (k c) -> p k c", k=KC)
        if not sample:
            for tb in range(NT // 128):
                p1, p1B = PS1()
                for kc in range(KC):
                    op("pe", "matmul", p1[:, 0:256], hT[:, kc, tb * 128:(tb + 1) * 128], wv3[:, kc, :], start=(kc == 0),
                       stop=(kc == KC - 1), r=[wvb, hTb[kc]], w=[p1B])
                op("dve", "tensor_copy", vtok[:, 1 + tb, :], p1[:, 0:256], r=[p1B], w=[vtokB[1 + tb]])
                if last_chunk and tb == NT // 128 - 1:
                    op("act", "activation", kvout[:, 1, :], p1[:, 0:256], AF.Copy, r=[p1B], w=[kvoutB])
            if last_chunk:
                S.dma("sp", nv_p[l], kvout[:, 1, :], r=[kvoutB], w=[dram_out_buf])
        else:
            p1, p1B = PS1()
            p1b_, p1bB = PS1()
            for s in range(NS):
                pso = (p1 if s < 2 else p1b_)[:, :].rearrange("p (s c) -> p s c", c=256)
                psoB = p1B if s < 2 else p1bB
                for kc in range(KC):
                    op("pe", "matmul", pso[0:1, s % 2, :], hT[:, kc, s:s + 1], wv3[:, kc, :], start=(kc == 0),
                       stop=(kc == KC - 1), r=[wvb, hTb[kc]], w=[psoB])
                op("dve", "tensor_copy", vnew[0:1, s, :], pso[0:1, s % 2, :], r=[psoB], w=[vnewB])
                op("act", "activation", kvnew_f[0:1, 1, s, :], pso[0:1, s % 2, :], AF.Copy, r=[psoB], w=[kvnewB])
            S.dma("sp", nv_s[l, :, 127:128, :].rearrange("s o d -> o s d"), kvnew_f[0:1, 1], r=[kvnewB], w=[dram_out_buf])
            S.dma("sp", nk_s[l, :, 0:127, :], ck_d[l, :, 1:128, :], w=[dram_out_buf])
            S.dma("sp", nv_s[l, :, 0:127, :], cv_d[l, :, 1:128, :], w=[dram_out_buf])
        checkpoint("proj")
        tap("pc_%s%d_%d" % (kind, ci, l), pc[:, :, 0:NT + 1], pcB, BF16)
        tap("qT_%s%d_%d" % (kind, ci, l), qT[:], qTb, BF16)
        if not sample:
            for qb in range(NT // 128):
                dm = cst[:, C_DM0:C_DM0 + 256] if (first_chunk and qb == 0) else cst[:, C_DM:C_DM + 256]
                for g in range(4):
                    def q_of(h, qb=qb):
                        return qT[:, h // 2, qb * 128:(qb + 1) * 128]

                    def k_of(h, qb=qb, g=g):
                        return kdT[:, g * 2 + h % 2, qb * 128:qb * 128 + 256]

                    def pv_terms(r, pTt, pTB, pbt, pbB, qb=qb, g=g):
                        return [(vtok[:, qb + kb, g * 64:(g + 1) * 64], pTt[:, r * 2 + kb, :], [vtokB[qb + kb], pTB])
                                for kb in range(2)]

                    attn_group(l, g, 128, 256, q_of, k_of, dm, pv_terms,
                               yT[:, 2 * g:2 * g + 2, qb * 128:(qb + 1) * 128], [yTb[2 * g], yTb[2 * g + 1]],
                               qTb + [kdTb[2 * g], kdTb[2 * g + 1]])
            op("pool", "tensor_copy", prevK[:, l], kdT[:, :, NT:NT + 128], r=kdTb, w=[prevKB[l]])
            op("pool", "tensor_copy", prevV[:, l, :], vtok[:, NT // 128, :], r=[vtokB[NT // 128]], w=[prevVB[l]])
        else:
            for s in range(NS):
                S.dma("pool", ksT[:, :, 0:128], ckT_d[l, s], w=[ksTB])
                S.dma("pool", cvb[:], cv_d[l, s], w=[cvbB])
                op("dve", "tensor_copy", ksT[:, :, 128:129], kdTs[:, :, s:s + 1], r=[kdTsB], w=[ksTB])
                for g in range(4):
                    def q_of(h, s=s):
                        return qT[:, h // 2, s:s + 1]

                    def k_of(h, g=g):
                        return ksT[:, g * 2 + h % 2, 0:129]

                    def pv_terms(r, pTt, pTB, pbt, pbB, s=s, g=g):
                        return [(cvb[:, g * 64:(g + 1) * 64], pTt[:, 0, 2 * r:2 * r + 1], [cvbB, pTB]),
                                (vnew[0:1, s, g * 64:(g + 1) * 64], pbt[0:1, r, 128:129], [vnewB, pbB])]

                    attn_group(l, g, 1, 129, q_of, k_of, cst[0:1, C_DS:C_DS + 129], pv_terms,
                               yT[:, 2 * g:2 * g + 2, s:s + 1], [yTb[2 * g], yTb[2 * g + 1]], qTb + [ksTB])
        checkpoint("attn")
        tap("yTa_%s%d_%d" % (kind, ci, l), yT[:, 0:8, :], yTb[0:8], BF16)
        S.barrier()
        for nm_, lo in (("rte", 64), ("rto", 0), ("ate", 64), ("ato", 0)):
            op("pool", "memset", opb[nm_][lo:lo + 64], 0.0, w=[opB[nm_]])
        if not sample:
            op("dve", "tensor_copy", Scur[:], Sst[:, l], r=[SstB[l]], w=[ScurB])
            op("act", "activation", Sbf[:], Scur[:], AF.Copy, r=[ScurB], w=[SbfB])
            for cb in range(NT // 64):
                t0 = cb * 64
                rwkv_prep(l, 64, pc[:, :, t0:t0 + 64], pc[:, :, t0 + 1:t0 + 65], pcB)
                rwkv_tokmajor(0, 64)
                S.barrier()
                rwkv_chunk(l, 0, 64, pcol[:, :], None)
                pz, pzB = PS1()
                pzv = pz[:, :].rearrange("p (m t) -> p m t", m=8)
                for m in range(8):
                    op("pe", "transpose", pzv[:, m, :], ytmp[:, 2 * m:2 * m + 2, :].rearrange("p a b -> p (a b)"),
                       ident_f[0:64, 0:64], r=[ytmpB, cstB], w=[pzB])
                rwkv_post(l, 64, pzv, pzB, yT[:, 8:16, t0:t0 + 64], yTb[8:16])
                S.barrier()
            op("dve", "tensor_copy", Sst[:, l], Scur[:], r=[ScurB], w=[SstB[l]])
            op("dve", "tensor_copy", prevrow[:, l, :], pc[:, :, NT], r=pcB, w=[prevrowB[l]])
            if last_chunk:
                state_out(nwkv_p[l])
        else:
            S.dma("sp", shs[:], shT_d[l], w=[shsB])
            rwkv_prep(l, NS, shs[:, :, :], pc[:, :, 0:NS], pcB + [shsB])
            S.barrier()
            for s in range(NS):
                S.dma("sp", Scur[:], stT_d[l, s], w=[ScurB])
                op("act", "activation", Sbf[:], Scur[:], AF.Copy, r=[ScurB], w=[SbfB])
                rwkv_tokmajor(s, 1)
                rwkv_chunk(l, s, 1, pcolS[:, :, s], None)
                pz, pzB = PS1()
                for m in range(8):
                    op("pe", "transpose", pz[:, 2 * m:2 * m + 1], ytmp[0:1, 2 * m:2 * m + 2, :].rearrange("p a b -> p (a b)"),
                       ident_f[0:1, 0:1], r=[ytmpB, cstB], w=[pzB])
                op("dve", "tensor_copy", zt[:, :, s:s + 1], pz[:, 0:16].rearrange("p (m t) -> p m t", t=2)[:, :, 0:1],
                   r=[pzB], w=[ztB])
                state_out(nwkv_s[l, s])
            rwkv_post(l, NS, zt[:, :, 0:NS], ztB, yT[:, 8:16, 0:NS], yTb[8:16])
        checkpoint("rwkv")
        tap("yT_%s%d_%d" % (kind, ci, l), yT[:], yTb, BF16)
        if last_chunk:
            ncol = NS if sample else 1
            for gi, c0 in enumerate(range(0, NPC, 4)):
                p1, p1B = PS1()
                pv = p1[:, :].rearrange("p (c f) -> p c f", f=128)
                nn = min(4, NPC - c0)
                for c in range(nn):
                    op("pe", "transpose", pv[0:ncol, c, :], lastrow[:, c0 + c, 0:ncol], ident_f, r=[lastrowB, cstB], w=[p1B])
                so = (gi % 4) * 512
                op("act", "activation", otok[0:ncol, so:so + nn * 128], p1[0:ncol, 0:nn * 128], AF.Copy, r=[p1B], w=[otokB])
                if sample:
                    S.dma("sp", nsh_s[l].rearrange("s c f -> s (c f)")[:, c0 * 128:(c0 + nn) * 128], otok[0:NS, so:so + nn * 128],
                          r=[otokB], w=[dram_out_buf])
                else:
                    S.dma("sp", nsh_p[l:l + 1].rearrange("o c f -> o (c f)")[:, c0 * 128:(c0 + nn) * 128],
                          otok[0:1, so:so + nn * 128], r=[otokB], w=[dram_out_buf])
        S.barrier()
        for g2 in range(8):
            wt, wb = WS.get(("out", l, g2))
            wv = wt[:, :].rearrange("p (g k c) -> p g k c", g=2, k=KC)
            for g in range(2):
                mo = g2 * 2 + g
                p1, p1B = PS1()
                for kc in range(KC):
                    op("pe", "matmul", p1[:, :N], wv[:, g, kc, :], yT[:, kc, :N], start=(kc == 0), stop=(kc == KC - 1),
                       r=[wb, yTb[kc]], w=[p1B])
                resid_add(l, 2, N, sample, mo, p1[:, :N], p1B)
        checkpoint("outproj")
        tap("x1_%s%d_%d" % (kind, ci, l), xT[:], xTb)
        S.barrier()
        norm_phase(N, sample, lambda kc: modap(l, 4, kc, N, sample), lambda kc: modap(l, 3, kc, N, sample),
                   lambda kc: hT[:, kc, :N], lambda kc: hTb[kc])
        for j in range(NJ):
            wt, wb = WS.get(("ffi", l, j))
            wv = wt[:, :].rearrange("p (g k c) -> p g k c", g=2, k=KC)
            pg, pgB = PS1()
            pu, puB = PS1()
            for kc in range(KC):
                op("pe", "matmul", pg[:, :N], wv[:, 0, kc, :], hT[:, kc, :N], start=(kc == 0), stop=(kc == KC - 1),
                   r=[wb, hTb[kc]], w=[pgB])
            for kc in range(KC):
                op("pe", "matmul", pu[:, :N], wv[:, 1, kc, :], hT[:, kc, :N], start=(kc == 0), stop=(kc == KC - 1),
                   r=[wb, hTb[kc]], w=[puB])
            i = j % 2
            op("act", "activation", ftmp[:, i, :N], pg[:, :N], AF.Silu, r=[pgB], w=[ftmpB[i]])
            op("dve", "tensor_tensor", aT[:, j, :N], ftmp[:, i, :N], pu[:, :N], ALU.mult, r=[ftmpB[i], puB], w=[aTb[j]])
        for mo in range(KC):
            p1, p1B = PS1()
            for hf in range(2):
                wt, wb = WS.get(("ffo", l, mo * 2 + hf))
                wv = wt[:, 0:2816].rearrange("p (j c) -> p j c", c=128)
                for jj in range(22):
                    j = hf * 22 + jj
                    op("pe", "matmul", p1[:, :N], wv[:, jj, :], aT[:, j, :N], start=(j == 0), stop=(j == NJ - 1),
                       r=[wb, aTb[j]], w=[p1B])
            resid_add(l, 5, N, sample, mo, p1[:, :N], p1B)
        checkpoint("ffn")
        tap("x2_%s%d_%d" % (kind, ci, l), xT[:], xTb)
        if l == n_layers - 1:
            pss, pssB = PS1()
            for kc in range(KC):
                i = kc % 2
                op("act", "activation", sqb[:, i, :N], xT[:, kc, :N], AF.Square, r=[xTb[kc]], w=[sqB[i]])
                op("pe", "matmul", pss[:, :N], ones_b, sqb[:, i, :N], start=(kc == 0), stop=(kc == KC - 1),
                   r=[sqB[i], cstB], w=[pssB])
            op("act", "activation", rstd[:, :N], pss[:, :N], AF.Sqrt, bias=smallc[:, 0:1], scale=1.0 / D, r=[pssB, cstB], w=[rstdB])
            op("dve", "reciprocal", rstd[:, :N], rstd[:, :N], r=[rstdB], w=[rstdB])
            for kc in range(KC):
                op("dve", "scalar_tensor_tensor", xT[:, kc, :N], xT[:, kc, :N], gfin[:, kc:kc + 1], rstd[:, :N], ALU.mult, ALU.mult,
                   r=[xTb[kc], rstdB, vecB], w=[xTb[kc]])
            nblk = 1 if sample else NT // 128
            for tb in range(nblk):
                nq = NS if sample else 128
                for k0 in range(0, KC, 4):
                    p1, p1B = PS1()
                    pv = p1[:, :].rearrange("p (c f) -> p c f", f=128)
                    for c in range(4):
                        op("pe", "transpose", pv[0:nq, c, :], xT[:, k0 + c, tb * 128:tb * 128 + nq], ident_f,
                           r=[xTb[k0 + c], cstB], w=[p1B])
                    dst_ = otok[0:nq, k0 * 128:(k0 + 4) * 128].rearrange("p (c f) -> p c f", f=128)
                    if (k0 // 4) % 2 == 0:
                        op("act", "activation", dst_, pv[0:nq], AF.Copy, r=[p1B], w=[otokB])
                    else:
                        op("dve", "tensor_copy", dst_, pv[0:nq], r=[p1B], w=[otokB])
                if sample:
                    S.dma("sp", y_s, otok[0:NS, :], r=[otokB], w=[dram_out_buf])
                else:
                    S.dma("sp", y_p[ci * NT + tb * 128:ci * NT + (tb + 1) * 128, :], otok[:, :], r=[otokB], w=[dram_out_buf])

    try:
        if stopped:
            raise _Stop()
        checkpoint("mods")
        for (kind, ci, l) in passes:
            run_pass(kind, ci, l)
    except _Stop:
        pass

    S.emit(es)
    es.close()
    return nc


def _consts():
    cst = np.zeros((128, NCST), np.float32)
    cst[:, C_ID:C_ID + 128] = np.eye(128, dtype=np.float32)
    blk = np.zeros((128, 128), np.float32)
    blk[0:64, 0:64] = 1.0
    blk[64:128, 64:128] = 1.0
    cst[:, C_BLK:C_BLK + 128] = blk
    p = np.arange(128)[:, None] % 64
    f = np.arange(64)[None, :]
    cst[:, C_MLT:C_MLT + 64] = (f < p)
    cst[:, C_MUT:C_MUT + 64] = (p < f)
    cst[:, C_MUE:C_MUE + 64] = (p <= f)
    i = np.arange(128)[:, None]
    j = np.arange(256)[None, :]
    dist = (128 + i - j).astype(np.float32)
    valid = (dist >= 0) & (dist <= 128)
    dm = np.where(valid, dist, BIG).astype(np.float32)
    cst[:, C_DM:C_DM + 256] = dm
    dm0 = dm.copy()
    dm0[:, 0:128] = BIG
    cst[:, C_DM0:C_DM0 + 256] = dm0
    ds = np.concatenate([128.0 - np.arange(128), [0.0]]).astype(np.float32)
    cst[:, C_DS:C_DS + 129] = ds[None, :]
    return cst


def _fm(v, ncol):
    return np.ascontiguousarray(v.reshape(ncol, 128).T)


def _pc_layout(a):
    out = np.zeros(a.shape[:-1] + (NPC * 128,), np.float32)
    out[..., 0:3072] = a[..., 0:3072]
    out[..., 3072:3136] = a[..., 3072:3136]
    out[..., 3136:3200] = a[..., 3136:3200]
    out[..., 3200:3360] = a[..., 3200:3360]
    return out


def _prep_weights(inp):
    w = {}
    L_ = L
    wada = inp["w_ada"].reshape(L_, KC, 128, 48, 2, 128).transpose(0, 3, 2, 4, 1, 5)
    w["wada"] = np.ascontiguousarray(wada).reshape(L_, 48, 128, 4096)
    w_in = inp["w_in"]
    wperm = np.zeros((L_, D, 44 * 128), np.float32)
    wperm[:, :, 0:1024] = w_in[:, :, 0:1024]
    for g in range(4):
        wkg = w_in[:, :, 1024 + g * 64:1024 + (g + 1) * 64]
        base = 1024 + g * 256
        wperm[:, :, base:base + 64] = wkg
        wperm[:, :, base + 128 + 64:base + 256] = wkg
    wperm[:, :, 2048:2048 + 3360] = w_in[:, :, 1536:1536 + 3360]
    win = wperm.reshape(L_, KC, 128, 22, 2, 128).transpose(0, 3, 2, 4, 1, 5)
    w["win"] = np.ascontiguousarray(win).reshape(L_, 22, 128, 4096)
    wk = w_in[:, :, 1024:1280].reshape(L_, KC, 128, 256).transpose(0, 2, 1, 3).reshape(L_, 128, 4096)
    wv = w_in[:, :, 1280:1536].reshape(L_, KC, 128, 256).transpose(0, 2, 1, 3).reshape(L_, 128, 4096)
    w["wkvt"] = np.ascontiguousarray(np.stack([wk, wv], axis=1))
    wout = inp["w_out"].reshape(L_, KC, 128, 8, 2, 128).transpose(0, 3, 2, 4, 1, 5)
    w["wout"] = np.ascontiguousarray(wout).reshape(L_, 8, 128, 4096)
    wfi = inp["w_ffn_in"].reshape(L_, KC, 128, 2, NJ, 128).transpose(0, 4, 2, 3, 1, 5)
    w["wffi"] = np.ascontiguousarray(wfi).reshape(L_, NJ, 128, 4096)
    wfo = inp["w_ffn_out"].reshape(L_, 2, 22, 128, KC, 128).transpose(0, 4, 1, 3, 2, 5)
    w["wffo"] = np.ascontiguousarray(wfo).reshape(L_, 32, 128, 2816)
    lup = np.zeros((L_, 128, 4096), np.float32)
    lup[:, 0:64, 0:1024] = inp["decay_up"]
    lup[:, 64:128, 1024:2048] = inp["iclr_up"]
    lup[:, :, 2048:3072] = inp["gate_up"][:, 0:128]
    lup[:, 0:32, 3072:4096] = inp["gate_up"][:, 128:160]
    w["lup"] = lup
    vecs = np.zeros((128, L_, NV), np.float32)
    for l in range(L_):
        vecs[:, l, V_GMIX:V_GMIX + 16] = _fm(inp["g_norm_mix"][l], 16)
        vecs[:, l, V_GFFN:V_GFFN + 16] = _fm(inp["g_norm_ffn"][l], 16)
        vecs[:, l, V_MIX:V_MIX + NPC] = _fm(_pc_layout(inp["mix_shift"][l]), NPC)
        vecs[:, l, V_W0:V_W0 + 8] = _fm(inp["decay_w0"][l], 8)
        vecs[:, l, V_A0:V_A0 + 8] = _fm(inp["iclr_a0"][l], 8)
        vecs[:, l, V_KK:V_KK + 8] = _fm(inp["k_k"][l], 8)
        vecs[:, l, V_KA:V_KA + 8] = _fm(inp["k_a"][l], 8)
        vecs[:, l, V_RK:V_RK + 8] = _fm(inp["r_k"][l].reshape(-1), 8)
        vecs[:, l, V_LNW:V_LNW + 8] = _fm(inp["ln_x_w"][l], 8)
        vecs[:, l, V_LNB:V_LNB + 8] = _fm(inp["ln_x_b"][l], 8)
        vecs[:, l, V_BADA:V_BADA + 96] = _fm(inp["b_ada"][l], 96)
    w["vecs"] = vecs
    w["sinks"] = np.ascontiguousarray(np.broadcast_to(inp["attn_sinks"][None], (128, L_, NH))).astype(np.float32)
    w["cst"] = _consts()
    w["gfin"] = _fm(inp["g_norm_final"], KC)
    return w


def _core_inputs(inp, shared, core):
    b = core % 4
    ss = slice(core * NS, (core + 1) * NS)
    m = dict(shared)
    m["xpT"] = np.ascontiguousarray(inp["x_prompt"][b].T).reshape(KC, 128, SEQ)
    xs = inp["x_sample"][ss, 0, :]
    m["xsT"] = np.ascontiguousarray(xs.reshape(NS, KC, 128).transpose(2, 1, 0))
    c5 = np.concatenate([inp["c_prompt"][b:b + 1], inp["c_sample"][ss]], axis=0)
    m["c5T"] = np.ascontiguousarray(c5.reshape(5, KC, 128).transpose(2, 1, 0))
    ck = inp["cache_k"][:, ss]
    ckT = ck.transpose(0, 1, 4, 3, 2)
    ckz = np.zeros((L, NS, 128, 4, 2, 128), np.float32)
    ckz[:, :, 0:64, :, 0, :] = ckT
    ckz[:, :, 64:128, :, 1, :] = ckT
    m["ckT"] = ckz.reshape(L, NS, 128, 8, 128)
    m["ck"] = np.ascontiguousarray(ck.reshape(L, NS, 128, 256))
    m["cv"] = np.ascontiguousarray(inp["cache_v"][:, ss].reshape(L, NS, 128, 256))
    stw = inp["state_wkv"][:, ss]
    stT = stw.reshape(L, NS, 8, 2, 64, 64).transpose(0, 1, 3, 5, 2, 4)
    m["stT"] = np.ascontiguousarray(stT).reshape(L, NS, 128, 8, 64)
    sh = _pc_layout(inp["state_shift"][:, ss])
    m["shT"] = np.ascontiguousarray(sh.reshape(L, NS, NPC, 128).transpose(0, 3, 2, 1))
    return m


def _unpc(a):
    return np.concatenate([a[..., 0:3200], a[..., 3200:3360]], axis=-1)


_NC_CACHE = {}


def kernel(**inputs):
    inp = {k: np.asarray(v) for k, v in inputs.items()}
    shared = _prep_weights(inp)
    in_maps = [_core_inputs(inp, shared, c) for c in range(8)]
    if "nc" not in _NC_CACHE:
        _NC_CACHE["nc"] = build()
    nc = _NC_CACHE["nc"]
    res = run_bass_kernel_spmd(nc, in_maps, core_ids=list(range(8)))
    R = res.results
    y_prompt = np.stack([R[b]["y_p"] for b in range(4)], 0).astype(np.float32)
    y_sample = np.concatenate([R[c]["y_s"] for c in range(8)], 0).reshape(32, 1, D).astype(np.float32)
    nkp = np.stack([R[b]["nk_p"] for b in range(4)], 1).reshape(L, 4, 128, 4, 64)
    nvp = np.stack([R[b]["nv_p"] for b in range(4)], 1).reshape(L, 4, 128, 4, 64)
    nwp = np.stack([R[b]["nwkv_p"] for b in range(4)], 1)
    nsp = _unpc(np.stack([R[b]["nsh_p"] for b in range(4)], 1).reshape(L, 4, NPC * 128))
    nks = np.concatenate([R[c]["nk_s"] for c in range(8)], 1).reshape(L, 32, 128, 4, 64)
    nvs = np.concatenate([R[c]["nv_s"] for c in range(8)], 1).reshape(L, 32, 128, 4, 64)
    nws = np.concatenate([R[c]["nwkv_s"] for c in range(8)], 1)
    nss = _unpc(np.concatenate([R[c]["nsh_s"] for c in range(8)], 1).reshape(L, 32, NPC * 128))
    f = lambda a: np.ascontiguousarray(a, dtype=np.float32)
    return (f(y_prompt), f(y_sample), f(nkp), f(nvp), f(nwp), f(nsp), f(nks), f(nvs), f(nws), f(nss))
```

```python
import math
from contextlib import ExitStack
import numpy as np
import concourse.bass as bass
import concourse.mybir as mybir
from concourse.bass_utils import run_bass_kernel_spmd

F32 = mybir.dt.float32
BF16 = mybir.dt.bfloat16
ALU = mybir.AluOpType
AF = mybir.ActivationFunctionType
AX = mybir.AxisListType

D = 2048
KC = 16
SEQ = 2048
NT = 512
NCH = SEQ // NT
NS = 4
L = 2
NH = 16
HD = 64
FF = 5632
NJ = FF // 128
RP = 3360
NPC = 27
C0 = math.exp(-0.5)
RMS_EPS = 1e-5
GN_EPS = 64e-5
BIG = 1.0e9
SLOPES = [2.0 ** (-8.0 * (h + 1) / NH) for h in range(NH)]

V_GMIX, V_GFFN, V_MIX, V_W0, V_A0, V_KK, V_KA, V_RK, V_LNW, V_LNB, V_BADA = 0, 16, 32, 59, 67, 75, 83, 91, 99, 107, 115
NV = 115 + 96
C_ID, C_BLK, C_MLT, C_MUT, C_MUE, C_DM, C_DM0, C_DS = 0, 128, 256, 320, 384, 448, 704, 960
NCST = 960 + 129


class _Stop(Exception):
    pass


class Buf:
    __slots__ = ("name", "w", "r", "excl")

    def __init__(self, name, excl=False):
        self.name = name
        self.w = None
        self.r = {}
        self.excl = excl


class Sched:
    CE = ("pe", "act", "dve", "pool")

    def __init__(self, nc, ndma=24):
        self.nc = nc
        self.q = {e: [] for e in ("pe", "act", "dve", "pool", "sp")}
        self.cnt = {e: 0 for e in self.CE}
        self.seen = {e: {} for e in self.q}
        self.floor = {e: {} for e in self.q}
        self.ndma = ndma
        self.dval = [0] * ndma
        self.dnext = {"sp": 0, "pool": ndma // 2, "act": 0}

    def _need(self, r, w):
        need = {}
        for b in r:
            if b.w is not None:
                k, v = b.w
                if need.get(k, 0) < v:
                    need[k] = v
        for b in w:
            if b.w is not None:
                k, v = b.w
                if need.get(k, 0) < v:
                    need[k] = v
            for k, v in b.r.items():
                if need.get(k, 0) < v:
                    need[k] = v
        return need

    def _waits(self, eng, need):
        fl = self.floor[eng]
        if fl:
            for k, v in fl.items():
                if need.get(k, 0) < v:
                    need[k] = v
            self.floor[eng] = {}
        seen = self.seen[eng]
        out = []
        for k, v in need.items():
            if k == eng and eng == "pe":
                continue
            if seen.get(k, 0) >= v:
                continue
            seen[k] = v
            out.append((k, v))
        return out

    def _mark(self, tok, r, w):
        k, v = tok
        for b in r:
            if b.r.get(k, 0) < v:
                b.r[k] = v
        for b in w:
            b.w = tok
            b.r = {}

    def op(self, eng, meth, *args, r=(), w=(), **kw):
        if any(b.excl for b in r):
            w = list(w) + [b for b in r if b.excl]
            r = [b for b in r if not b.excl]
        waits = self._waits(eng, self._need(r, w))
        self.cnt[eng] += 1
        tok = (eng, self.cnt[eng])
        self.q[eng].append((waits, meth, args, kw, eng))
        self._mark(tok, r, w)

    def dma(self, eng, out, in_, r=(), w=(), **kw):
        need = self._need(r, w)
        i = self.dnext[eng]
        half = self.ndma // 2
        base = half if eng == "pool" else 0
        self.dnext[eng] = base + (i - base + 1) % half
        k = ("d", i)
        if self.dval[i] > 0 and need.get(k, 0) < self.dval[i]:
            need[k] = self.dval[i]
        waits = self._waits(eng, need)
        self.dval[i] += 16
        tok = (k, self.dval[i])
        kw = dict(kw)
        kw["out"] = out
        kw["in_"] = in_
        self.q[eng].append((waits, "dma_start", (), kw, k))
        self._mark(tok, r, w)

    def barrier(self):
        for e in self.q:
            fl = self.floor[e]
            for c in self.CE:
                if self.cnt[c] > 0 and fl.get(c, 0) < self.cnt[c]:
                    fl[c] = self.cnt[c]
            for i in range(self.ndma // 2):
                if self.dval[i] > 0:
                    fl[("d", i)] = self.dval[i]

    def emit(self, es):
        nc = self.nc
        sems = {e: es.enter_context(nc.semaphore("s_" + e)) for e in self.CE}
        dsems = [es.enter_context(nc.semaphore("d%d" % i)) for i in range(self.ndma)]
        block = es.enter_context(nc.Block())

        def semof(k):
            return sems[k] if isinstance(k, str) else dsems[k[1]]

        def run(name, final=False):
            def f(e):
                for waits, meth, args, kw, inc in self.q[name]:
                    for k, v in waits:
                        e.wait_ge(semof(k), v)
                    ins = getattr(e, meth)(*args, **kw)
                    if isinstance(inc, str):
                        ins.then_inc(sems[inc], 1)
                    else:
                        ins.then_inc(dsems[inc[1]], 16)
                if final:
                    for i in range(self.ndma):
                        if self.dval[i] > 0:
                            e.wait_ge(dsems[i], self.dval[i])
                    for c in self.CE:
                        if self.cnt[c] > 0:
                            e.wait_ge(sems[c], self.cnt[c])
            return f

        block.tensor(run("pe"))
        block.scalar(run("act"))
        block.vector(run("dve"))
        block.gpsimd(run("pool"))
        block.sync(run("sp", final=True))


def build(cfg=None):
    cfg = cfg or {}
    n_layers = cfg.get("layers", L)
    n_chunks = cfg.get("chunks", NCH)
    do_sample = cfg.get("sample", True)
    taps = cfg.get("taps", ())
    nc = bass.Bass("TRN2", target_bir_lowering=False)
    S = Sched(nc)
    es = ExitStack()

    def din(name, shape, dt=F32):
        return nc.dram_tensor(name, list(shape), dt, kind="ExternalInput").ap()

    def dout(name, shape, dt=F32):
        return nc.dram_tensor(name, list(shape), dt, kind="ExternalOutput").ap()

    xpT = din("xpT", [KC, 128, SEQ])
    xsT = din("xsT", [128, KC, NS])
    c5T = din("c5T", [128, KC, 5])
    vecs_d = din("vecs", [128, L, NV])
    sinks_d = din("sinks", [128, L, NH])
    cst_d = din("cst", [128, NCST])
    ckT_d = din("ckT", [L, NS, 128, 8, 128])
    cv_d = din("cv", [L, NS, 128, 256])
    ck_d = din("ck", [L, NS, 128, 256])
    stT_d = din("stT", [L, NS, 128, 8, 64])
    shT_d = din("shT", [L, 128, NPC, NS])
    wada_d = din("wada", [L, 48, 128, 4096])
    win_d = din("win", [L, 22, 128, 4096])
    wkvt_d = din("wkvt", [L, 2, 128, 4096])
    wout_d = din("wout", [L, 8, 128, 4096])
    wffi_d = din("wffi", [L, 44, 128, 4096])
    wffo_d = din("wffo", [L, 32, 128, 2816])
    lup_d = din("lup", [L, 128, 4096])
    gfin_d = din("gfin", [128, KC])

    y_p = dout("y_p", [SEQ, D])
    y_s = dout("y_s", [NS, D])
    nk_p = dout("nk_p", [L, 128, 256])
    nv_p = dout("nv_p", [L, 128, 256])
    nwkv_p = dout("nwkv_p", [L, NH, 64, 64])
    nsh_p = dout("nsh_p", [L, NPC, 128])
    nk_s = dout("nk_s", [L, NS, 128, 256])
    nv_s = dout("nv_s", [L, NS, 128, 256])
    nwkv_s = dout("nwkv_s", [L, NS, NH, 64, 64])
    nsh_s = dout("nsh_s", [L, NS, NPC, 128])
    dram_out_buf = Buf("dram_out")

    def sb(name, shape, dt=F32):
        return es.enter_context(nc.sbuf_tensor("sb_" + name, list(shape), dt))

    def carve(reg, off, shape, dt=F32, parts=128):
        n = 1
        for d_ in shape[1:]:
            n *= d_
        nf = n if dt == F32 else (n + 1) // 2
        v = reg[0:parts, off:off + nf]
        if dt != F32:
            v = v.bitcast(dt)[:, 0:n]
        if len(shape) == 3:
            v = v.rearrange("p (a b) -> p a b", a=shape[1])
        return v, off + nf

    xT = sb("xT", [128, KC, NT])
    xTb = [Buf("xT%d" % k) for k in range(KC)]
    R2 = sb("R2", [128, 4096])
    hT, _ = carve(R2, 0, [128, KC, NT], BF16)
    hTb = [Buf("hT%d" % k) for k in range(KC)]
    NWS = 3
    wsl = [sb("wsl%d" % i, [128, 4096], BF16) for i in range(NWS)]
    wslb = [Buf("wsl%d" % i) for i in range(NWS)]
    lupb = sb("lupb", [128, 4096], BF16)
    lupB = Buf("lup")
    cst = sb("cst", [128, NCST])
    cstB = Buf("cst")
    cbf = sb("cbf", [128, 384], BF16)
    ones_f = sb("ones_f", [128, 64])
    vecs = sb("vecs", [128, L, NV])
    vecB = Buf("vecs")
    omk = sb("omk", [128, L, 8])
    sinks = sb("sinks", [128, L, NH])
    modT = sb("modT", [128, L, 96, 5])
    modB = Buf("mod")
    gfin = sb("gfin", [128, KC])
    c5 = sb("c5", [128, KC, 5])
    scT = sb("scT", [128, KC, 5], BF16)
    smallc = sb("smallc", [128, 4])
    Sst = sb("Sst", [128, L, 8, 64])
    SstB = [Buf("Sst%d" % l) for l in range(L)]
    Scur = sb("Scur", [128, 8, 64])
    Sbf = sb("Sbf", [128, 8, 64], BF16)
    ScurB = Buf("Scur")
    SbfB = Buf("Sbf")
    prevrow = sb("prevrow", [128, L, NPC], BF16)
    prevrowB = [Buf("prow%d" % l) for l in range(L)]
    prevK = sb("prevK", [128, L, 8, 128], BF16)
    prevKB = [Buf("pK%d" % l) for l in range(L)]
    prevV = sb("prevV", [128, L, 256], BF16)
    prevVB = [Buf("pV%d" % l) for l in range(L)]
    lastrow = sb("lastrow", [128, NPC, NS])
    lastrowB = Buf("lastrow")
    shs = sb("shs", [128, NPC, NS])
    shsB = Buf("shs")
    vnew = sb("vnew", [1, NS, 256], BF16)
    vnewB = Buf("vnew")
    kdTs = sb("kdTs", [128, 8, NS], BF16)
    kdTsB = Buf("kdTs")
    sqb = sb("sqb", [128, 2, NT], BF16)
    sqB = [Buf("sq0"), Buf("sq1")]
    rstd = sb("rstd", [128, NT])
    rstdB = Buf("rstd")
    tmpn = sb("tmpn", [128, 2, NT])
    tmpnB = [Buf("tmpn0"), Buf("tmpn1")]
    ftmp = tmpn
    ftmpB = tmpnB
    otok = sb("otok", [128, D])
    otokB = Buf("otok")
    kvout = otok[:, 0:512].rearrange("p (a b) -> p a b", a=2)
    kvoutB = otokB
    kvnew_f = otok[0:1, 0:2048].rearrange("p (a s c) -> p a s c", a=2, s=NS)
    kvnewB = otokB
    R1 = sb("R1", [128, 11264])
    pc, o1 = carve(R1, 0, [128, NPC, NT + 2], BF16)
    pcB = [Buf("pc%d" % c) for c in range(NPC)]
    yT, o1 = carve(R1, o1, [128, KC, NT], BF16)
    yTb = [Buf("yT%d" % k) for k in range(KC)]
    assert o1 <= 11264
    aT, _ = carve(R1, 0, [128, NJ, NT], BF16)
    aTb = [Buf("aT%d" % j) for j in range(NJ)]
    R3N = 9632
    R3 = sb("R3", [128, R3N])
    o = 0
    qT, o = carve(R3, o, [128, 8, NT], BF16)
    qTb = [Buf("qT%d" % k) for k in range(8)]
    kdT, o = carve(R3, o, [128, 8, 128 + NT], BF16)
    kdTb = [Buf("kdT%d" % k) for k in range(8)]
    vtok, o = carve(R3, o, [128, 5, 256], BF16)
    vtokB = [Buf("vtok%d" % k) for k in range(5)]
    lg, o = carve(R3, o, [128, 4, 256])
    lgB = Buf("lg")
    pbt2, pbB2, pTt2, pTtf2, pTB2 = [], [], [], [], []
    for i_ in range(2):
        t_, o = carve(R3, o, [128, 4, 256], BF16)
        pbt2.append(t_)
        pbB2.append(Buf("pb%d" % i_))
        tf_, _ = carve(R3, o, [128, 512])
        t_, o = carve(R3, o, [128, 8, 128], BF16)
        pTt2.append(t_)
        pTtf2.append(tf_)
        pTB2.append(Buf("pT%d" % i_))
    attn_state = {"i": 0}
    ast, o = carve(R3, o, [128, 6, 4])
    astB = Buf("ast")
    ksT, o = carve(R3, o, [128, 8, 130], BF16)
    ksTB = Buf("ksT")
    cvb, o = carve(R3, o, [128, 256], BF16)
    cvbB = Buf("cvb")
    assert o <= R3N, o
    o = 0
    psb, o = carve(R3, o, [128, NPC, 64])
    psbB = Buf("psb")
    o_psb_end = o
    TT = []
    for i in range(8):
        t_, _ = carve(R2, i * 512, [128, 8, 64])
        TT.append(t_)
    TTB = [Buf("TT%d" % i) for i in range(8)]
    OPN = ["rte", "rto", "ate", "ato", "bt", "kt", "bh", "kh", "vb"]
    opb, opB = {}, {}
    ar2f, o = carve(R3, o, [128, 2048], BF16)
    ar2 = ar2f.rearrange("p (m q w t) -> p m q w t", m=8, q=2, w=2)
    opb["ate"], opb["ato"] = ar2[:, :, 0, 0, :], ar2[:, :, 1, 0, :]
    opb["rte"], opb["rto"] = ar2[:, :, 0, 1, :], ar2[:, :, 1, 1, :]
    for n in OPN:
        if n not in opb:
            opb[n], o = carve(R3, o, [128, 8, 64], BF16)
        opB[n] = Buf("o_" + n)
    tkm, tkm_f, tkmB = {}, {}, {}
    for n in ("v", "bh", "kh"):
        tkm_f[n], _ = carve(R3, o, [64, 512], F32, parts=64)
        tkm[n], o = carve(R3, o, [64, 8, 128], BF16, parts=64)
        tkmB[n] = Buf("k_" + n)
    ysb, o = carve(R3, o, [64, NH, 64], F32, parts=64)
    ysbB = Buf("ysb")
    stout, stoutB = ysb, ysbB
    ytmp, o = carve(R3, o, [64, NH, 64], F32, parts=64)
    ytmpB = Buf("ytmp")
    zt, o = carve(R3, o, [128, 8, 64])
    ztB = Buf("zt")
    bon, o = carve(R3, o, [128, 8, 64])
    bonB = Buf("bon")
    gg, o = carve(R3, o, [128, 8, 64])
    ggB = Buf("gg")
    tl, o = carve(R3, o, [128, 64], BF16)
    sg, o = carve(R3, o, [128, 2, 64], BF16)
    tmpb, o = carve(R3, o, [128, 8, 64], BF16)
    tlB, sgB, tmpbB = Buf("tl"), Buf("sg"), Buf("tmpb")
    pcol, o = carve(R3, o, [128, 8])
    pcolS, o = carve(R3, o, [128, 8, NS])
    pcolB = Buf("pcol")
    yst, o = carve(R3, o, [64, 4, NH], F32, parts=64)
    ystB = Buf("yst")
    assert o <= R3N, o
    MATN = ["M0", "N0", "M1", "N1", "IM", "Ak", "Rb", "Rk", "Pa", "Pb", "RH"]
    mat, matB = {}, {}
    for i, n in enumerate(MATN):
        if i < 8:
            mat[n], _ = carve(R2, i * 512, [64, NH, 64], BF16, parts=64)
        else:
            mat[n], _ = carve(R3, (i - 8) * 512, [64, NH, 64], BF16, parts=64)
        matB[n] = Buf("m_" + n)
    assert 3 * 512 <= o_psb_end
    mat["U"], matB["U"] = mat["Ak"], matB["Ak"]
    ps = es.enter_context(nc.psum_tensor("ps", [128, 8, 512], F32))
    psB = [Buf("ps%d" % b, excl=True) for b in range(8)]
    st = {"p1": 0, "p2": 0, "ws": 0}

    def PS1():
        b = st["p1"]
        st["p1"] = (b + 1) % 4
        return ps[:, b, :], psB[b]

    def PS2():
        k = st["p2"]
        st["p2"] = (k + 1) % 2
        b = 4 + 2 * k
        return ps[:, b:b + 2, :].rearrange("p a b -> p (a b)"), [psB[b], psB[b + 1]]

    ident_f = cst[:, C_ID:C_ID + 128]
    ident_b = cbf[:, 0:128]
    blk_b = cbf[:, 128:256]
    ones_b = cbf[:, 256:384]

    op = S.op

    def bc(ap, shape):
        return ap.to_broadcast(list(shape))

    def wload(src, ncols):
        i = st["ws"]
        st["ws"] = (i + 1) % NWS
        S.dma("pool", wsl[i][:, 0:ncols], src, r=(), w=[wslb[i]], max_dma_last_dim=8192)
        return wsl[i], wslb[i]

    class WStream:
        def __init__(self, items, depth=2):
            self.items = items
            self.depth = depth
            self.issued = []
            self.pos = 0

        def _issue(self):
            k = len(self.issued)
            if k < len(self.items):
                src, ncols, _ = self.items[k]
                self.issued.append(wload(src, ncols))

        def get(self, tag):
            while len(self.issued) < min(len(self.items), self.pos + 1 + self.depth):
                self._issue()
            assert self.items[self.pos][2] == tag, (self.items[self.pos][2], tag)
            t = self.issued[self.pos]
            self.pos += 1
            return t

    passes = []
    for ci in range(n_chunks):
        for l in range(n_layers):
            passes.append(("p", ci, l))
    if do_sample:
        for l in range(n_layers):
            passes.append(("s", 0, l))

    items = []
    for l in range(n_layers):
        for g in range(48):
            items.append((wada_d[l, g], 4096, ("ada", l, g)))
    for (kind, ci, l) in passes:
        for g in range(22):
            items.append((win_d[l, g], 4096, ("in", l, g)))
        for g in range(2):
            items.append((wkvt_d[l, g], 4096, ("kv", l, g)))
        for g in range(8):
            items.append((wout_d[l, g], 4096, ("out", l, g)))
        for g in range(44):
            items.append((wffi_d[l, g], 4096, ("ffi", l, g)))
        for g in range(32):
            items.append((wffo_d[l, g], 2816, ("ffo", l, g)))
    WS = WStream(items)

    def checkpoint(name):
        if cfg.get("stop") == name:
            raise _Stop()

    def tap(name, ap, bufs, dt=F32):
        if name in taps:
            d = dout("tap_" + name, list(ap.shape), dt)
            S.dma("sp", d, ap, r=bufs, w=[dram_out_buf])

    S.dma("sp", cst[:], cst_d, w=[cstB])
    S.dma("sp", vecs[:], vecs_d, w=[vecB])
    S.dma("sp", sinks[:], sinks_d, w=[vecB])
    S.dma("sp", c5[:], c5T, w=[modB])
    S.dma("sp", gfin[:], gfin_d, w=[vecB])
    op("dve", "tensor_copy", cbf[:, 0:256], cst[:, C_ID:C_ID + 256], r=[cstB], w=[cstB])
    op("dve", "memset", cbf[:, 256:384], 1.0, w=[cstB])
    op("dve", "memset", smallc[:, 0:1], RMS_EPS, w=[cstB])
    op("dve", "memset", smallc[:, 1:2], GN_EPS, w=[cstB])
    op("dve", "memset", smallc[:, 2:3], 0.0, w=[cstB])
    op("dve", "memset", smallc[:, 3:4], 1.0, w=[cstB])
    op("dve", "memset", ones_f[:], 1.0, w=[cstB])
    for l in range(L):
        op("dve", "tensor_scalar", omk[:, l, :], vecs[:, l, V_KA:V_KA + 8], -1.0, 1.0, ALU.mult, ALU.add,
           r=[vecB], w=[vecB])
        op("dve", "memset", Sst[:, l], 0.0, w=[SstB[l]])
        op("dve", "memset", prevrow[:, l], 0.0, w=[prevrowB[l]])
        op("dve", "memset", prevK[:, l], 0.0, w=[prevKB[l]])
        op("dve", "memset", prevV[:, l], 0.0, w=[prevVB[l]])

    stopped = False
    try:
        tap("scT", c5[:], [modB])
        checkpoint("consts")
        op("act", "activation", scT[:], c5[:], AF.Silu, r=[modB], w=[modB])
        for l in range(n_layers):
            pm, pmB = PS1()
            for g2 in range(48):
                if cfg.get("stop") == "ada1" and g2 == cfg.get("ngrp", 1):
                    raise _Stop()
                wt, wb = WS.get(("ada", l, g2))
                wv = wt[:, :].rearrange("p (g k c) -> p g k c", g=2, k=KC)
                for g in range(2):
                    m = g2 * 2 + g
                    for kc in range(KC):
                        op("pe", "matmul", pm[:, m * 5:m * 5 + 5], wv[:, g, kc, :], scT[:, kc, :],
                           start=(kc == 0), stop=(kc == KC - 1), r=[wb, modB], w=[pmB])
            checkpoint("ada_mm")
            pm3 = pm[:, 0:480].rearrange("p (m s) -> p m s", s=5)
            op("dve", "tensor_tensor", modT[:, l], pm3, bc(vecs[:, l, V_BADA:V_BADA + 96].unsqueeze(2), [128, 96, 5]),
               ALU.add, r=[pmB, vecB], w=[modB])
            checkpoint("ada_tt")
            for (lo, voff) in ((16, V_GMIX), (64, V_GFFN)):
                op("dve", "tensor_scalar", modT[:, l, lo:lo + 16, :], modT[:, l, lo:lo + 16, :], 1.0, None, ALU.add,
                   r=[modB], w=[modB])
                op("dve", "tensor_tensor", modT[:, l, lo:lo + 16, :], modT[:, l, lo:lo + 16, :],
                   bc(vecs[:, l, voff:voff + 16].unsqueeze(2), [128, 16, 5]), ALU.mult, r=[vecB, modB], w=[modB])
    except _Stop:
        stopped = True
    tap("modT", modT[:, 0:n_layers], [modB])


    def modap(l, sec, kc, N, sample):
        if sample:
            return modT[:, l, sec * 16 + kc, 1:1 + NS]
        return bc(modT[:, l, sec * 16 + kc, 0:1], [128, N])

    def norm_phase(N, sample, A_of, B_of, out_of, outB_of, out_dt_bf16=True):
        pss, pssB = PS1()
        for kc in range(KC):
            i = kc % 2
            op("act", "activation", sqb[:, i, :N], xT[:, kc, :N], AF.Square, r=[xTb[kc]], w=[sqB[i]])
            op("pe", "matmul", pss[:, :N], ones_b, sqb[:, i, :N], start=(kc == 0), stop=(kc == KC - 1),
               r=[sqB[i], cstB], w=[pssB])
        op("act", "activation", rstd[:, :N], pss[:, :N], AF.Sqrt, bias=smallc[:, 0:1], scale=1.0 / D,
           r=[pssB, cstB], w=[rstdB])
        op("dve", "reciprocal", rstd[:, :N], rstd[:, :N], r=[rstdB], w=[rstdB])
        checkpoint("norm_a")
        for kc in range(KC):
            i = kc % 2
            op("dve", "tensor_tensor", tmpn[:, i, :N], xT[:, kc, :N], rstd[:, :N], ALU.mult,
               r=[xTb[kc], rstdB], w=[tmpnB[i]])
            A, B = A_of(kc), B_of(kc)
            if B is None:
                op("dve", "tensor_tensor", out_of(kc), tmpn[:, i, :N], A, ALU.mult,
                   r=[tmpnB[i], modB, vecB], w=[outB_of(kc)])
            else:
                op("pool", "tensor_tensor", tmpn[:, i, :N], tmpn[:, i, :N], A, ALU.mult,
                   r=[tmpnB[i], modB], w=[tmpnB[i]])
                op("dve", "tensor_tensor", out_of(kc), tmpn[:, i, :N], B, ALU.add,
                   r=[tmpnB[i], modB], w=[outB_of(kc)])

    def resid_add(l, sec, N, sample, mo, psap, pB):
        G = modap(l, sec, mo, N, sample)
        i = mo % 2
        op("dve", "tensor_tensor", ftmp[:, i, :N], psap, G, ALU.mult, r=[pB, modB], w=[ftmpB[i]])
        op("pool", "tensor_tensor", xT[:, mo, :N], xT[:, mo, :N], ftmp[:, i, :N], ALU.add,
           r=[ftmpB[i], xTb[mo]], w=[xTb[mo]])

    def attn_group(l, g, Q, nk, q_of, k_of, dm, pv_terms, out_ap, out_bufs, rbufs):
        ai = attn_state["i"]
        attn_state["i"] = 1 - ai
        pbt, pbB, pTt, pTt_f, pTB = pbt2[ai], pbB2[ai], pTt2[ai], pTtf2[ai], pTB2[ai]
        p2, p2B = PS2()
        lgv = p2[:, :].rearrange("p (r k) -> p r k", r=4)
        for r in range(4):
            h = 4 * g + r
            op("pe", "matmul", lgv[0:Q, r, 0:nk], q_of(h), k_of(h), start=True, stop=True, r=rbufs, w=p2B)
        for r in range(4):
            h = 4 * g + r
            op("dve", "tensor_scalar", lg[0:Q, r, 0:nk], dm, -SLOPES[h], None, ALU.mult, r=[cstB], w=[lgB])
        checkpoint("att_mm")
        op("dve", "tensor_tensor", lg[0:Q, :, 0:nk], lg[0:Q, :, 0:nk], lgv[0:Q, :, 0:nk], ALU.add, r=p2B + [lgB], w=[lgB])
        checkpoint("att_lg")
        mx, rs, sd, den = ast[0:Q, 0, :], ast[0:Q, 1, :], ast[0:Q, 2, :], ast[0:Q, 3, :]
        sk = sinks[0:Q, l, 4 * g:4 * g + 4]
        op("dve", "tensor_reduce", mx, lg[0:Q, :, 0:nk], AX.X, ALU.max, r=[lgB], w=[astB])
        op("dve", "tensor_tensor", mx, mx, sk, ALU.max, r=[astB, vecB], w=[astB])
        op("dve", "tensor_tensor", lg[0:Q, :, 0:nk], lg[0:Q, :, 0:nk], bc(mx.unsqueeze(2), [Q, 4, nk]), ALU.subtract,
           r=[lgB, astB], w=[lgB])
        op("act", "activation", lg[0:Q, :, 0:nk], lg[0:Q, :, 0:nk], AF.Exp, r=[lgB], w=[lgB])
        op("dve", "tensor_reduce", rs, lg[0:Q, :, 0:nk], AX.X, ALU.add, r=[lgB], w=[astB])
        op("dve", "tensor_tensor", sd, sk, mx, ALU.subtract, r=[astB, vecB], w=[astB])
        op("act", "activation", sd, sd, AF.Exp, r=[astB], w=[astB])
        op("dve", "tensor_tensor", den, rs, sd, ALU.add, r=[astB], w=[astB])
        op("dve", "reciprocal", den, den, r=[astB], w=[astB])
        op("dve", "tensor_tensor", pbt[0:Q, :, 0:nk], lg[0:Q, :, 0:nk], bc(den.unsqueeze(2), [Q, 4, nk]), ALU.mult,
           r=[lgB, astB], w=[pbB])
        checkpoint("att_sm")
        pt1, pt1B = PS1()
        ptb = pt1.bitcast(BF16)
        if Q == 128:
            ptv = ptb.rearrange("p (a q) -> p a q", q=128)
            for r in range(4):
                for kb in range(2):
                    op("pe", "transpose", ptv[:, r * 2 + kb, :], pbt[:, r, kb * 128:(kb + 1) * 128], ident_b,
                       r=[pbB, cstB], w=[pt1B])
            op("dve", "tensor_copy", pTt_f[:, :], pt1[:, :], r=[pt1B], w=[pTB])
        else:
            for r in range(4):
                op("pe", "transpose", ptb[:, 2 * r:2 * r + 1], pbt[0:1, r, 0:128], ident_b[0:1, 0:1],
                   r=[pbB, cstB], w=[pt1B])
            op("dve", "tensor_copy", pTt_f[:, 0:4], pt1[:, 0:4], r=[pt1B], w=[pTB])
        checkpoint("att_T")
        po, poB = PS1()
        pov = po[:, 0:2 * Q].rearrange("p (c q) -> p c q", c=2)
        for r in range(4):
            h = 4 * g + r
            half = h % 2
            c = (h // 2) - 2 * g
            terms = pv_terms(r, pTt, pTB, pbt, pbB)
            for ti, (lt, rh, rb) in enumerate(terms):
                op("pe", "matmul", pov[half * 64:half * 64 + 64, c, :], lt, rh, start=(ti == 0),
                   stop=(ti == len(terms) - 1), r=rb, w=[poB])
        op("act", "activation", out_ap, pov, AF.Copy, r=[poB], w=out_bufs)

    def rwkv_prep(l, nt, prev_ap, cur_ap, rbufs):
        P3 = [128, NPC, nt]
        mixb = bc(vecs[:, l, V_MIX:V_MIX + NPC].unsqueeze(2), P3)
        op("dve", "tensor_tensor", psb[:, :, 0:nt], prev_ap, cur_ap, ALU.subtract, r=rbufs, w=[psbB])
        op("dve", "tensor_tensor", psb[:, :, 0:nt], psb[:, :, 0:nt], mixb, ALU.mult, r=[psbB, vecB], w=[psbB])
        op("dve", "tensor_tensor", psb[:, :, 0:nt], psb[:, :, 0:nt], cur_ap, ALU.add, r=[psbB] + rbufs, w=[psbB])
        r_, k_, v_ = psb[:, 0:8, 0:nt], psb[:, 8:16, 0:nt], psb[:, 16:24, 0:nt]
        op("act", "activation", tl[0:64, 0:nt], psb[0:64, 24, 0:nt], AF.Tanh, r=[psbB], w=[tlB])
        op("act", "activation", tl[64:128, 0:nt], psb[64:128, 24, 0:nt], AF.Copy, r=[psbB], w=[tlB])
        op("act", "activation", sg[:, :, 0:nt], psb[:, 25:27, 0:nt], AF.Sigmoid, r=[psbB], w=[sgB])
        up1w = lupb[:, 0:1024]
        up1a = lupb[:, 1024:2048]
        gup = lupb[:, 2048:4096].rearrange("p (k c) -> p k c", k=2)
        T = [t[:, :, 0:nt] for t in TT]
        B8 = [128, 8, nt]
        pw, pwB = PS1()
        pa, paB = PS1()
        pg, pgB = PS1()
        pwv = pw[:, 0:8 * nt].rearrange("p (m t) -> p m t", m=8)
        pav = pa[:, 0:8 * nt].rearrange("p (m t) -> p m t", m=8)
        pgv = pg[:, 0:8 * nt].rearrange("p (m t) -> p m t", m=8)
        for m in range(8):
            cs_ = slice(m * 128, (m + 1) * 128)
            op("pe", "matmul", pwv[:, m, :], up1w[:, cs_], tl[:, 0:nt], start=True, stop=True, r=[tlB, lupB], w=[pwB])
            op("pe", "matmul", pav[:, m, :], up1a[:, cs_], tl[:, 0:nt], start=True, stop=True, r=[tlB, lupB], w=[paB])
            op("pe", "matmul", pgv[:, m, :], gup[:, 0, cs_], sg[:, 0, 0:nt], start=True, stop=False, r=[sgB, lupB], w=[pgB])
            op("pe", "matmul", pgv[:, m, :], gup[:, 1, cs_], sg[:, 1, 0:nt], start=False, stop=True, r=[sgB, lupB], w=[pgB])

        def vb(off):
            return bc(vecs[:, l, off:off + 8].unsqueeze(2), B8)

        op("dve", "tensor_tensor", T[0], pwv, vb(V_W0), ALU.add, r=[pwB, vecB], w=[TTB[0]])
        op("act", "activation", T[0], T[0], AF.Sigmoid, r=[TTB[0]], w=[TTB[0]])
        op("dve", "tensor_tensor", T[1], pav, vb(V_A0), ALU.add, r=[paB, vecB], w=[TTB[1]])
        op("act", "activation", T[1], T[1], AF.Sigmoid, r=[TTB[1]], w=[TTB[1]])
        op("act", "activation", gg[:, :, 0:nt], pgv, AF.Copy, r=[pgB], w=[ggB])
        if nt == 64:
            for m in range(8):
                op("dve", "tensor_tensor_scan", TT[3][:, m, :], ones_f[:, :], TT[0][:, m, :], 0.0,
                   ALU.mult, ALU.add, r=[TTB[0], cstB], w=[TTB[3]])
            lastb = bc(TT[3][:, :, 63:64], B8)
        else:
            op("dve", "tensor_copy", T[3], T[0], r=[TTB[0]], w=[TTB[3]])
            lastb = T[3]
        op("act", "activation", T[4], T[3], AF.Exp, scale=-C0, r=[TTB[3]], w=[TTB[4]])
        op("dve", "tensor_tensor", T[0], T[3], T[0], ALU.subtract, r=[TTB[3], TTB[0]], w=[TTB[0]])
        op("act", "activation", T[0], T[0], AF.Exp, scale=-C0, r=[TTB[0]], w=[TTB[0]])
        op("act", "activation", T[5], T[3], AF.Exp, scale=C0, r=[TTB[3]], w=[TTB[5]])
        if nt == 64:
            op("dve", "tensor_copy", pcol[:, :], TT[4][:, :, 63], r=[TTB[4]], w=[pcolB])
            op("dve", "tensor_tensor", T[3], lastb, T[3], ALU.subtract, r=[TTB[3]], w=[TTB[3]])
            op("act", "activation", T[3], T[3], AF.Exp, scale=-C0, r=[TTB[3]], w=[TTB[3]])
        else:
            op("dve", "tensor_copy", pcolS[:, :, 0:nt], T[4], r=[TTB[4]], w=[pcolB])
            op("dve", "memset", T[3], 1.0, w=[TTB[3]])
        op("dve", "tensor_tensor", T[6], k_, vb(V_KK), ALU.mult, r=[psbB, vecB], w=[TTB[6]])
        op("dve", "tensor_tensor", tmpb[:, :, 0:nt], T[6], T[6], ALU.mult, r=[TTB[6]], w=[tmpbB])
        pq, pqB = PS1()
        pqv = pq[:, 0:8 * nt].rearrange("p (m t) -> p m t", m=8)
        op("pe", "matmul", pqv, blk_b, tmpb[:, :, 0:nt], start=True, stop=True, r=[tmpbB, cstB], w=[pqB])
        op("act", "activation", T[7], pqv, AF.Sqrt, r=[pqB], w=[TTB[7]])
        op("dve", "tensor_scalar", T[7], T[7], 1e-12, None, ALU.max, r=[TTB[7]], w=[TTB[7]])
        op("dve", "reciprocal", T[7], T[7], r=[TTB[7]], w=[TTB[7]])
        op("dve", "tensor_tensor", T[6], T[6], T[7], ALU.mult, r=[TTB[6], TTB[7]], w=[TTB[6]])
        op("dve", "tensor_tensor", T[7], T[1], vb(V_KA), ALU.mult, r=[TTB[1], vecB], w=[TTB[7]])
        op("dve", "tensor_tensor", T[7], T[7], bc(omk[:, l, :].unsqueeze(2), B8), ALU.add, r=[TTB[7], vecB], w=[TTB[7]])
        op("dve", "tensor_tensor", T[7], T[7], k_, ALU.mult, r=[TTB[7], psbB], w=[TTB[7]])
        op("dve", "tensor_tensor", k_, r_, T[7], ALU.mult, r=[psbB, TTB[7]], w=[psbB])
        op("dve", "tensor_tensor", tmpb[:, :, 0:nt], k_, vb(V_RK), ALU.mult, r=[psbB, vecB], w=[tmpbB])
        pq2, pq2B = PS1()
        pq2v = pq2[:, 0:8 * nt].rearrange("p (m t) -> p m t", m=8)
        op("pe", "matmul", pq2v, blk_b, tmpb[:, :, 0:nt], start=True, stop=True, r=[tmpbB, cstB], w=[pq2B])
        op("dve", "tensor_tensor", bon[:, :, 0:nt], pq2v, v_, ALU.mult, r=[pq2B, psbB], w=[bonB])
        O = {n: opb[n][:, :, 0:nt] for n in OPN}
        for nm_, lo in (("rte", 0), ("rto", 64)):
            op("dve", "tensor_tensor", O[nm_][lo:lo + 64], r_[lo:lo + 64], T[4][lo:lo + 64], ALU.mult,
               r=[psbB, TTB[4]], w=[opB[nm_]])
        op("dve", "tensor_tensor", T[0], T[6], T[0], ALU.mult, r=[TTB[6], TTB[0]], w=[TTB[0]])
        for nm_, lo in (("ate", 0), ("ato", 64)):
            op("dve", "tensor_scalar", O[nm_][lo:lo + 64], T[0][lo:lo + 64], -1.0, None, ALU.mult, r=[TTB[0]], w=[opB[nm_]])
        op("dve", "tensor_tensor", T[1], T[6], T[1], ALU.mult, r=[TTB[6], TTB[1]], w=[TTB[1]])
        op("dve", "tensor_tensor", O["bt"], T[1], T[5], ALU.mult, r=[TTB[1], TTB[5]], w=[opB["bt"]])
        op("dve", "tensor_tensor", O["bh"], T[1], T[3], ALU.mult, r=[TTB[1], TTB[3]], w=[opB["bh"]])
        op("dve", "tensor_tensor", O["kt"], T[7], T[5], ALU.mult, r=[TTB[7], TTB[5]], w=[opB["kt"]])
        op("dve", "tensor_tensor", O["kh"], T[7], T[3], ALU.mult, r=[TTB[7], TTB[3]], w=[opB["kh"]])
        op("act", "activation", O["vb"], v_, AF.Copy, r=[psbB], w=[opB["vb"]])

    def rwkv_tokmajor(t0, C):
        for n, src in (("v", "vb"), ("bh", "bh"), ("kh", "kh")):
            p1, p1B = PS1()
            pb16 = p1.bitcast(BF16).rearrange("p (m f) -> p m f", f=128)
            for m in range(8):
                op("pe", "transpose", pb16[0:C, m, :], opb[src][:, m, t0:t0 + C], ident_b, r=[opB[src], cstB], w=[p1B])
            op("dve", "tensor_copy", tkm_f[n][0:C, :], p1[0:C, :], r=[p1B], w=[tkmB[n]])

    def rwkv_chunk(l, t0, C, pcol_ap, y_tok_out):
        mlt = cst[0:C, C_MLT:C_MLT + C]
        mut = cst[0:C, C_MUT:C_MUT + C]
        mue = cst[0:C, C_MUE:C_MUE + C]
        idb = ident_f[0:C, 0:C]

        def fm(name, h):
            if name in ("at", "rt"):
                name = name + ("e" if h % 2 == 0 else "o")
            return opb[name][:, h // 2, t0:t0 + C]

        def fmB(name, h):
            if name in ("at", "rt"):
                name = name + ("e" if h % 2 == 0 else "o")
            return opB[name]

        def tk(name, h):
            return tkm[name][0:C, h // 2, (h % 2) * 64:(h % 2) * 64 + 64]

        def sbh(h):
            return Sbf[:, h // 2, :]

        def hb(ap):
            return bc(ap.unsqueeze(1), [C, NH, C])

        allop = [opB[n_] for n_ in ("rte", "rto", "ate", "ato", "bt", "kt")]

        def pair(nameL, nameR, outn, mask, rb):
            p2, p2B = PS2()
            pv = p2[:, :].rearrange("p (h s) -> p h s", h=NH)
            for h in range(NH):
                op("pe", "matmul", pv[0:C, h, 0:C], fm(nameL, h), fm(nameR, h), start=True, stop=True, r=rb, w=p2B)
            op("dve", "tensor_tensor", mat[outn][0:C, :, 0:C], pv[0:C, :, 0:C], hb(mask), ALU.mult, r=p2B + [cstB], w=[matB[outn]])

        def pair2(nameL, outA, outB):
            for half in range(2):
                p2, p2B = PS2()
                pv = p2[:, :].rearrange("p (h w s) -> p h w s", h=8, w=2)
                for hh in range(8):
                    h = half * 8 + hh
                    op("pe", "matmul", pv[0:C, hh, :, :], fm(nameL, h), ar2[:, h // 2, h % 2, :, t0:t0 + C],
                       start=True, stop=True, r=allop, w=p2B)
                hs = slice(half * 8, half * 8 + 8)
                op("dve", "tensor_tensor", mat[outA][0:C, hs, 0:C], pv[0:C, :, 0, 0:C], bc(mut.unsqueeze(1), [C, 8, C]),
                   ALU.mult, r=p2B + [cstB], w=[matB[outA]])
                op("dve", "tensor_tensor", mat[outB][0:C, hs, 0:C], pv[0:C, :, 1, 0:C], bc(mue.unsqueeze(1), [C, 8, C]),
                   ALU.mult, r=p2B + [cstB], w=[matB[outB]])

        if C > 1:
            pair("at", "bt", "M0", mlt, allop)
            pair2("bt", "N0", "Rb")
            pair2("kt", "Ak", "Rk")
        else:
            pair("bt", "rt", "Rb", mue, allop)
            pair("kt", "rt", "Rk", mue, allop)
        HB = {}

        def hbuf(name, g):
            k = (name, g)
            if k not in HB:
                HB[k] = Buf("h_%s%d" % (name, g))
            return HB[k]

        hsl = [slice(0, 8), slice(8, 16)]
        if C == 1:
            p2, p2B = PS2()
            pv = p2[:, :].rearrange("p (h s) -> p h s", h=NH)
            for h in range(NH):
                op("pe", "matmul", pv[0:C, h, :], fm("at", h), sbh(h), start=True, stop=True, r=allop + [SbfB], w=p2B)
            op("act", "activation", mat["RH"][0:C], pv[0:C], AF.Copy, r=p2B, w=[matB["RH"]])
        else:
            for g in range(2):
                p1, p1B = PS1()
                pv = p1[:, :].rearrange("p (h s) -> p h s", h=8)
                for hh in range(8):
                    h = g * 8 + hh
                    op("pe", "matmul", pv[0:C, hh, :], fm("at", h), sbh(h), start=True, stop=False, r=allop + [SbfB], w=[p1B])
                    op("pe", "matmul", pv[0:C, hh, :], mat["Ak"][0:C, h, 0:C], tk("v", h), start=False, stop=True,
                       r=[matB["Ak"], tkmB["v"]], w=[p1B])
                op("act", "activation", mat["RH"][0:C, hsl[g]], pv[0:C], AF.Copy, r=[p1B], w=[hbuf("RH", g)])
        if C > 1:
            hb8 = bc(idb.unsqueeze(1), [C, 8, C])
            for g in range(2):
                op("dve", "tensor_tensor", mat["Pa"][0:C, hsl[g], 0:C], mat["N0"][0:C, hsl[g], 0:C], hb8, ALU.add,
                   r=[matB["N0"], cstB], w=[hbuf("Pa", g)])
            curM, curN, curP = "M0", "N0", "Pa"
            for lev in range(1, 6):
                nM = "M1" if curM == "M0" else "M0"
                nN = "N1" if curN == "N0" else "N0"
                nP = "Pb" if curP == "Pa" else "Pa"
                first = [matB["M0"], matB["N0"]] if lev == 1 else []
                for g in range(2):
                    p1, p1B = PS1()
                    pm_ = p1[:, :].rearrange("p (h s) -> p h s", h=8)
                    for hh in range(8):
                        h = g * 8 + hh
                        op("pe", "matmul", pm_[0:C, hh, 0:C], mat[curN][0:C, h, 0:C], mat[curM][0:C, h, 0:C], start=True, stop=True,
                           r=[hbuf(curN, g), hbuf(curM, g)] + first, w=[p1B])
                    op("dve", "tensor_tensor", mat["IM"][0:C, hsl[g], 0:C], pm_[0:C, :, 0:C], hb8, ALU.add, r=[p1B, cstB],
                       w=[hbuf("IM", g)])
                    if lev < 5:
                        op("act", "activation", mat[nM][0:C, hsl[g], 0:C], pm_[0:C, :, 0:C], AF.Copy, r=[p1B], w=[hbuf(nM, g)])
                if lev < 5:
                    for g in range(2):
                        p1, p1B = PS1()
                        pn_ = p1[:, :].rearrange("p (h s) -> p h s", h=8)
                        for hh in range(8):
                            h = g * 8 + hh
                            op("pe", "matmul", pn_[0:C, hh, 0:C], mat[curM][0:C, h, 0:C], mat[curN][0:C, h, 0:C], start=True, stop=True,
                               r=[hbuf(curN, g), hbuf(curM, g)] + first, w=[p1B])
                        op("act", "activation", mat[nN][0:C, hsl[g], 0:C], pn_[0:C, :, 0:C], AF.Copy, r=[p1B], w=[hbuf(nN, g)])
                for g in range(2):
                    p1, p1B = PS1()
                    pp_ = p1[:, :].rearrange("p (h s) -> p h s", h=8)
                    for hh in range(8):
                        h = g * 8 + hh
                        op("pe", "matmul", pp_[0:C, hh, 0:C], mat["IM"][0:C, h, 0:C], mat[curP][0:C, h, 0:C], start=True, stop=True,
                           r=[hbuf("IM", g), hbuf(curP, g)], w=[p1B])
                    wl = [hbuf(nP, g)] + ([matB[nP]] if lev == 5 else [])
                    op("dve", "tensor_copy", mat[nP][0:C, hsl[g], 0:C], pp_[0:C, :, 0:C], r=[p1B], w=wl)
                curM, curN, curP = nM, nN, nP
            for g in range(2):
                p1, p1B = PS1()
                pu = p1[:, :].rearrange("p (h s) -> p h s", h=8)
                for hh in range(8):
                    h = g * 8 + hh
                    op("pe", "matmul", pu[0:C, hh, :], mat[curP][0:C, h, 0:C], mat["RH"][0:C, h, :], start=True, stop=True,
                       r=[hbuf(curP, g), hbuf("RH", g)], w=[p1B])
                op("act", "activation", mat["U"][0:C, hsl[g]], pu[0:C], AF.Copy, r=[p1B], w=[hbuf("U", g), matB["Ak"]])
            Un = "U"
            UnB = [hbuf("U", 0), hbuf("U", 1)]
        else:
            Un = "RH"
            UnB = [matB["RH"]]
        if C == 1:
            p2, p2B = PS2()
            py = p2[:, :].rearrange("p (h s) -> p h s", h=NH)
            for h in range(NH):
                op("pe", "matmul", py[0:C, h, :], fm("rt", h), sbh(h), start=True, stop=False, r=allop + [SbfB], w=p2B)
                op("pe", "matmul", py[0:C, h, :], mat["Rb"][0:C, h, 0:C], mat[Un][0:C, h, :], start=False, stop=False,
                   r=[matB["Rb"]] + UnB, w=p2B)
                op("pe", "matmul", py[0:C, h, :], mat["Rk"][0:C, h, 0:C], tk("v", h), start=False, stop=True,
                   r=[matB["Rk"], tkmB["v"]], w=p2B)
            op("act", "activation", ysb[0:C], py[0:C], AF.Copy, r=p2B, w=[ysbB])
        else:
            for g in range(2):
                p1, p1B = PS1()
                py = p1[:, :].rearrange("p (h s) -> p h s", h=8)
                for hh in range(8):
                    h = g * 8 + hh
                    op("pe", "matmul", py[0:C, hh, :], fm("rt", h), sbh(h), start=True, stop=False, r=allop + [SbfB], w=[p1B])
                    op("pe", "matmul", py[0:C, hh, :], mat["Rb"][0:C, h, 0:C], mat[Un][0:C, h, :], start=False, stop=False,
                       r=[matB["Rb"], UnB[g]], w=[p1B])
                    op("pe", "matmul", py[0:C, hh, :], mat["Rk"][0:C, h, 0:C], tk("v", h), start=False, stop=True,
                       r=[matB["Rk"], tkmB["v"]], w=[p1B])
                op("act", "activation", ysb[0:C, hsl[g]], py[0:C], AF.Copy, r=[p1B], w=[ysbB])
        p1, p1B = PS1()
        pS = p1[:, :].rearrange("p (m i) -> p m i", m=8)
        for h in range(NH):
            hp = (h % 2) * 64
            op("pe", "matmul", pS[hp:hp + 64, h // 2, :], tk("bh", h), mat[Un][0:C, h, :], start=True, stop=False,
               r=[tkmB["bh"]] + UnB, w=[p1B])
            op("pe", "matmul", pS[hp:hp + 64, h // 2, :], tk("kh", h), tk("v", h), start=False, stop=True,
               r=[tkmB["kh"], tkmB["v"]], w=[p1B])
        op("pool", "tensor_tensor", Scur[:], Scur[:], bc(pcol_ap.unsqueeze(2), [128, 8, 64]), ALU.mult,
           r=[ScurB, pcolB], w=[ScurB])
        op("dve", "tensor_tensor", Scur[:], Scur[:], pS, ALU.add, r=[ScurB, p1B], w=[ScurB])
        op("act", "activation", Sbf[:], Scur[:], AF.Copy, r=[ScurB], w=[SbfB])
        mu, var = yst[0:C, 0, :], yst[0:C, 1, :]
        op("dve", "tensor_reduce", mu, ysb[0:C], AX.X, ALU.add, r=[ysbB], w=[ystB])
        op("dve", "tensor_scalar", mu, mu, 1.0 / 64, None, ALU.mult, r=[ystB], w=[ystB])
        op("dve", "tensor_tensor", ysb[0:C], ysb[0:C], bc(mu.unsqueeze(2), [C, NH, 64]), ALU.subtract, r=[ysbB, ystB], w=[ysbB])
        op("pool", "tensor_tensor", ytmp[0:C], ysb[0:C], ysb[0:C], ALU.mult, r=[ysbB], w=[ytmpB])
        op("dve", "tensor_reduce", var, ytmp[0:C], AX.X, ALU.add, r=[ytmpB], w=[ystB])
        op("act", "activation", var, var, AF.Sqrt, bias=smallc[0:C, 1:2], scale=1.0 / 64, r=[ystB, cstB], w=[ystB])
        op("dve", "reciprocal", var, var, r=[ystB], w=[ystB])
        op("dve", "tensor_tensor", ytmp[0:C], ysb[0:C], bc(var.unsqueeze(2), [C, NH, 64]), ALU.mult, r=[ysbB, ystB], w=[ytmpB])

    def rwkv_post(l, nt, pz, pzB, out_ap, out_bufs):
        B8 = [128, 8, nt]
        z = zt[:, :, 0:nt]
        op("dve", "tensor_tensor", z, pz, bc(vecs[:, l, V_LNW:V_LNW + 8].unsqueeze(2), B8), ALU.mult, r=[pzB, vecB], w=[ztB])
        op("dve", "tensor_tensor", z, z, bc(vecs[:, l, V_LNB:V_LNB + 8].unsqueeze(2), B8), ALU.add, r=[ztB, vecB], w=[ztB])
        op("dve", "tensor_tensor", z, z, bon[:, :, 0:nt], ALU.add, r=[ztB, bonB], w=[ztB])
        op("dve", "tensor_tensor", out_ap, z, gg[:, :, 0:nt], ALU.mult, r=[ztB, ggB], w=out_bufs)

    def state_out(dst):
        p2, p2B = PS2()
        pv = p2[:, :].rearrange("p (m f) -> p m f", m=8)
        for m in range(8):
            op("pe", "transpose", pv[0:64, m, :], Scur[:, m, :], ident_f, r=[ScurB, cstB], w=p2B)
        op("act", "activation", stout[:].rearrange("p h j -> p (h j)"), p2[0:64, :], AF.Copy, r=p2B, w=[stoutB])
        S.dma("sp", dst.rearrange("h i j -> i h j"), stout[:], r=[stoutB], w=[dram_out_buf])

    def run_pass(kind, ci, l):
        sample = (kind == "s")
        N = NS if sample else NT
        first_chunk = (ci == 0)
        last_chunk = sample or (ci == n_chunks - 1)
        if l == 0:
            if sample:
                S.dma("sp", xT[:, :, 0:NS], xsT, w=xTb)
            else:
                for kc in range(KC):
                    S.dma("sp", xT[:, kc, :], xpT[kc, :, ci * NT:(ci + 1) * NT], w=[xTb[kc]])
        S.barrier()
        S.dma("pool", lupb[:], lup_d[l], w=[lupB], max_dma_last_dim=8192)
        norm_phase(N, sample, lambda kc: modap(l, 1, kc, N, sample), lambda kc: modap(l, 0, kc, N, sample),
                   lambda kc: hT[:, kc, :N], lambda kc: hTb[kc])
        checkpoint("norm")
        if not sample:
            op("dve", "tensor_copy", pc[:, :, 0], prevrow[:, l, :], r=[prevrowB[l]], w=pcB)
            op("dve", "tensor_copy", kdT[:, :, 0:128], prevK[:, l], r=[prevKB[l]], w=kdTb)
            op("dve", "tensor_copy", vtok[:, 0, :], prevV[:, l, :], r=[prevVB[l]], w=[vtokB[0]])
        checkpoint("prevcopy")
        ev = 0
        for g2 in range(22):
            wt, wb = WS.get(("in", l, g2))
            wv = wt[:, :].rearrange("p (g k c) -> p g k c", g=2, k=KC)
            for g in range(2):
                m = g2 * 2 + g
                if m >= 43:
                    continue
                if cfg.get("stop") == "projn" and m == cfg.get("nm", 1):
                    raise _Stop()
                p1, p1B = PS1()
                for kc in range(KC):
                    op("pe", "matmul", p1[:, :N], wv[:, g, kc, :], hT[:, kc, :N], start=(kc == 0), stop=(kc == KC - 1),
                       r=[wb, hTb[kc]], w=[p1B])
                if m < 8:
                    op("act", "activation", qT[:, m, :N], p1[:, :N], AF.Copy, scale=0.125, r=[p1B], w=[qTb[m]])
                elif m < 16:
                    gq = m - 8
                    if sample:
                        op("dve", "tensor_copy", kdTs[:, gq, :], p1[:, :N], r=[p1B], w=[kdTsB])
                    else:
                        op("dve", "tensor_copy", kdT[:, gq, 128:128 + N], p1[:, :N], r=[p1B], w=[kdTb[gq]])
                else:
                    c = m - 16
                    off = 0 if sample else 1
                    if ev % 2 == 0:
                        op("act", "activation", pc[:, c, off:off + N], p1[:, :N], AF.Copy, r=[p1B], w=[pcB[c]])
                    else:
                        op("dve", "tensor_copy", pc[:, c, off:off + N], p1[:, :N], r=[p1B], w=[pcB[c]])
                    ev += 1
                    if last_chunk:
                        if sample:
                            op("dve", "tensor_copy", lastrow[:, c, :], p1[:, 0:NS], r=[p1B], w=[lastrowB])
                        else:
                            op("dve", "tensor_copy", lastrow[:, c, 0:1], p1[:, N - 1:N], r=[p1B], w=[lastrowB])
        checkpoint("projmm")
        wkt, wkb = WS.get(("kv", l, 0))
        wk3 = wkt[:, :].rearrange("p (k c) -> p k c", k=KC)
        if not sample:
            if last_chunk:
                p1, p1B = PS1()
                for kc in range(KC):
                    op("pe", "matmul", p1[:, 0:256], hT[:, kc, NT - 128:NT], wk3[:, kc, :], start=(kc == 0),
                       stop=(kc == KC - 1), r=[wkb, hTb[kc]], w=[p1B])
                op("act", "activation", kvout[:, 0, :], p1[:, 0:256], AF.Copy, r=[p1B], w=[kvoutB])
                S.dma("sp", nk_p[l], kvout[:, 0, :], r=[kvoutB], w=[dram_out_buf])
        else:
            p1, p1B = PS1()
            p1b_, p1bB = PS1()
            for s in range(NS):
                pso = (p1 if s < 2 else p1b_)[:, :].rearrange("p (s c) -> p s c", c=256)
                psoB = p1B if s < 2 else p1bB
                for kc in range(KC):
                    op("pe", "matmul", pso[0:1, s % 2, :], hT[:, kc, s:s + 1], wk3[:, kc, :], start=(kc == 0),
                       stop=(kc == KC - 1), r=[wkb, hTb[kc]], w=[psoB])
                op("act", "activation", kvnew_f[0:1, 0, s, :], pso[0:1, s % 2, :], AF.Copy, r=[psoB], w=[kvnewB])
            S.dma("sp", nk_s[l, :, 127:128, :].rearrange("s o d -> o s d"), kvnew_f[0:1, 0], r=[kvnewB], w=[dram_out_buf])
        wvt, wvb = WS.get(("kv", l, 1))
        wv3 = wvt[:, :].rearrange("p (k c) -> p k c", k=KC)
        if not sample:
            for tb in range(NT // 128):
                p1, p1B = PS1()
                for kc in range(KC):
                    op("pe", "matmul", p1[:, 0:256], hT[:, kc, tb * 128:(tb + 1) * 128], wv3[:, kc, :], start=(kc == 0),
                       stop=(kc == KC - 1), r=[wvb, hTb[kc]], w=[p1B])
                op("dve", "tensor_copy", vtok[:, 1 + tb, :], p1[:, 0:256], r=[p1B], w=[vtokB[1 + tb]])
                if last_chunk and tb == NT // 128 - 1:
                    op("act", "activation", kvout[:, 1, :], p1[:, 0:256], AF.Copy, r=[p1B], w=[kvoutB])
            if last_chunk:
                S.dma("sp", nv_p[l], kvout[:, 1, :], r=[kvoutB], w=[dram_out_buf])
        else:
            p1, p1B = PS1()
            p1b_, p1bB = PS1()
            for s in range(NS):
                pso = (p1 if s < 2 else p1b_)[:, :].rearrange("p (s c) -> p s c", c=256)
                psoB = p1B if s < 2 else p1bB
                for kc in range(KC):
                    op("pe", "matmul", pso[0:1, s % 2, :], hT[:, kc, s:s + 1], wv3[:, kc, :], start=(kc == 0),
                       stop=(kc == KC - 1), r=[wvb, hTb[kc]], w=[psoB])
                op("dve", "tensor_copy", vnew[0:1, s, :], pso[0:1, s % 2, :], r=[psoB], w=[vnewB])
                op("act", "activation", kvnew_f[0:1, 1, s, :], pso[0:1, s % 2, :], AF.Copy, r=[psoB], w=[kvnewB])
            S.dma("sp", nv_s[l, :, 127:128, :].rearrange("s o d -> o s d"), kvnew_f[0:1, 1], r=[kvnewB], w=[dram_out_buf])
            S.dma("sp", nk_s[l, :, 0:127, :], ck_d[l, :, 1:128, :], w=[dram_out_buf])
            S.dma("sp", nv_s[l, :, 0:127, :], cv_d[l, :, 1:128, :], w=[dram_out_buf])
        checkpoint("proj")
        tap("pc_%s%d_%d" % (kind, ci, l), pc[:, :, 0:NT + 1], pcB, BF16)
        tap("qT_%s%d_%d" % (kind, ci, l), qT[:], qTb, BF16)
        if not sample:
            for qb in range(NT // 128):
                dm = cst[:, C_DM0:C_DM0 + 256] if (first_chunk and qb == 0) else cst[:, C_DM:C_DM + 256]
                for g in range(4):
                    def q_of(h, qb=qb):
                        return qT[:, h // 2, qb * 128:(qb + 1) * 128]

                    def k_of(h, qb=qb, g=g):
                        return kdT[:, g * 2 + h % 2, qb * 128:qb * 128 + 256]

                    def pv_terms(r, pTt, pTB, pbt, pbB, qb=qb, g=g):
                        return [(vtok[:, qb + kb, g * 64:(g + 1) * 64], pTt[:, r * 2 + kb, :], [vtokB[qb + kb], pTB])
                                for kb in range(2)]

                    attn_group(l, g, 128, 256, q_of, k_of, dm, pv_terms,
                               yT[:, 2 * g:2 * g + 2, qb * 128:(qb + 1) * 128], [yTb[2 * g], yTb[2 * g + 1]],
                               qTb + [kdTb[2 * g], kdTb[2 * g + 1]])
            op("pool", "tensor_copy", prevK[:, l], kdT[:, :, NT:NT + 128], r=kdTb, w=[prevKB[l]])
            op("pool", "tensor_copy", prevV[:, l, :], vtok[:, NT // 128, :], r=[vtokB[NT // 128]], w=[prevVB[l]])
        else:
            for s in range(NS):
                S.dma("pool", ksT[:, :, 0:128], ckT_d[l, s], w=[ksTB])
                S.dma("pool", cvb[:], cv_d[l, s], w=[cvbB])
                op("dve", "tensor_copy", ksT[:, :, 128:129], kdTs[:, :, s:s + 1], r=[kdTsB], w=[ksTB])
                for g in range(4):
                    def q_of(h, s=s):
                        return qT[:, h // 2, s:s + 1]

                    def k_of(h, g=g):
                        return ksT[:, g * 2 + h % 2, 0:129]

                    def pv_terms(r, pTt, pTB, pbt, pbB, s=s, g=g):
                        return [(cvb[:, g * 64:(g + 1) * 64], pTt[:, 0, 2 * r:2 * r + 1], [cvbB, pTB]),
                                (vnew[0:1, s, g * 64:(g + 1) * 64], pbt[0:1, r, 128:129], [vnewB, pbB])]

                    attn_group(l, g, 1, 129, q_of, k_of, cst[0:1, C_DS:C_DS + 129], pv_terms,
                               yT[:, 2 * g:2 * g + 2, s:s + 1], [yTb[2 * g], yTb[2 * g + 1]], qTb + [ksTB])
        checkpoint("attn")
        tap("yTa_%s%d_%d" % (kind, ci, l), yT[:, 0:8, :], yTb[0:8], BF16)
        S.barrier()
        for nm_, lo in (("rte", 64), ("rto", 0), ("ate", 64), ("ato", 0)):
            op("pool", "memset", opb[nm_][lo:lo + 64], 0.0, w=[opB[nm_]])
        if not sample:
            op("dve", "tensor_copy", Scur[:], Sst[:, l], r=[SstB[l]], w=[ScurB])
            op("act", "activation", Sbf[:], Scur[:], AF.Copy, r=[ScurB], w=[SbfB])
            for cb in range(NT // 64):
                t0 = cb * 64
                rwkv_prep(l, 64, pc[:, :, t0:t0 + 64], pc[:, :, t0 + 1:t0 + 65], pcB)
                rwkv_tokmajor(0, 64)
                S.barrier()
                rwkv_chunk(l, 0, 64, pcol[:, :], None)
                pz, pzB = PS1()
                pzv = pz[:, :].rearrange("p (m t) -> p m t", m=8)
                for m in range(8):
                    op("pe", "transpose", pzv[:, m, :], ytmp[:, 2 * m:2 * m + 2, :].rearrange("p a b -> p (a b)"),
                       ident_f[0:64, 0:64], r=[ytmpB, cstB], w=[pzB])
                rwkv_post(l, 64, pzv, pzB, yT[:, 8:16, t0:t0 + 64], yTb[8:16])
                S.barrier()
            op("dve", "tensor_copy", Sst[:, l], Scur[:], r=[ScurB], w=[SstB[l]])
            op("dve", "tensor_copy", prevrow[:, l, :], pc[:, :, NT], r=pcB, w=[prevrowB[l]])
            if last_chunk:
                state_out(nwkv_p[l])
        else:
            S.dma("sp", shs[:], shT_d[l], w=[shsB])
            rwkv_prep(l, NS, shs[:, :, :], pc[:, :, 0:NS], pcB + [shsB])
            S.barrier()
            for s in range(NS):
                S.dma("sp", Scur[:], stT_d[l, s], w=[ScurB])
                op("act", "activation", Sbf[:], Scur[:], AF.Copy, r=[ScurB], w=[SbfB])
                rwkv_tokmajor(s, 1)
                rwkv_chunk(l, s, 1, pcolS[:, :, s], None)
                pz, pzB = PS1()
                for m in range(8):
                    op("pe", "transpose", pz[:, 2 * m:2 * m + 1], ytmp[0:1, 2 * m:2 * m + 2, :].rearrange("p a b -> p (a b)"),
                       ident_f[0:1, 0:1], r=[ytmpB, cstB], w=[pzB])
                op("dve", "tensor_copy", zt[:, :, s:s + 1], pz[:, 0:16].rearrange("p (m t) -> p m t", t=2)[:, :, 0:1],
                   r=[pzB], w=[ztB])
                state_out(nwkv_s[l, s])
            rwkv_post(l, NS, zt[:, :, 0:NS], ztB, yT[:, 8:16, 0:NS], yTb[8:16])
        checkpoint("rwkv")
        tap("yT_%s%d_%d" % (kind, ci, l), yT[:], yTb, BF16)
        if last_chunk:
            ncol = NS if sample else 1
            for gi, c0 in enumerate(range(0, NPC, 4)):
                p1, p1B = PS1()
                pv = p1[:, :].rearrange("p (c f) -> p c f", f=128)
                nn = min(4, NPC - c0)
                for c in range(nn):
                    op("pe", "transpose", pv[0:ncol, c, :], lastrow[:, c0 + c, 0:ncol], ident_f, r=[lastrowB, cstB], w=[p1B])
                so = (gi % 4) * 512
                op("act", "activation", otok[0:ncol, so:so + nn * 128], p1[0:ncol, 0:nn * 128], AF.Copy, r=[p1B], w=[otokB])
                if sample:
                    S.dma("sp", nsh_s[l].rearrange("s c f -> s (c f)")[:, c0 * 128:(c0 + nn) * 128], otok[0:NS, so:so + nn * 128],
                          r=[otokB], w=[dram_out_buf])
                else:
                    S.dma("sp", nsh_p[l:l + 1].rearrange("o c f -> o (c f)")[:, c0 * 128:(c0 + nn) * 128],
                          otok[0:1, so:so + nn * 128], r=[otokB], w=[dram_out_buf])
        S.barrier()
        for g2 in range(8):
            wt, wb = WS.get(("out", l, g2))
            wv = wt[:, :].rearrange("p (g k c) -> p g k c", g=2, k=KC)
            for g in range(2):
                mo = g2 * 2 + g
                p1, p1B = PS1()
                for kc in range(KC):
                    op("pe", "matmul", p1[:, :N], wv[:, g, kc, :], yT[:, kc, :N], start=(kc == 0), stop=(kc == KC - 1),
                       r=[wb, yTb[kc]], w=[p1B])
                resid_add(l, 2, N, sample, mo, p1[:, :N], p1B)
        checkpoint("outproj")
        tap("x1_%s%d_%d" % (kind, ci, l), xT[:], xTb)
        S.barrier()
        norm_phase(N, sample, lambda kc: modap(l, 4, kc, N, sample), lambda kc: modap(l, 3, kc, N, sample),
                   lambda kc: hT[:, kc, :N], lambda kc: hTb[kc])
        for j in range(NJ):
            wt, wb = WS.get(("ffi", l, j))
            wv = wt[:, :].rearrange("p (g k c) -> p g k c", g=2, k=KC)
            pg, pgB = PS1()
            pu, puB = PS1()
            for kc in range(KC):
                op("pe", "matmul", pg[:, :N], wv[:, 0, kc, :], hT[:, kc, :N], start=(kc == 0), stop=(kc == KC - 1),
                   r=[wb, hTb[kc]], w=[pgB])
            for kc in range(KC):
                op("pe", "matmul", pu[:, :N], wv[:, 1, kc, :], hT[:, kc, :N], start=(kc == 0), stop=(kc == KC - 1),
                   r=[wb, hTb[kc]], w=[puB])
            i = j % 2
            op("act", "activation", ftmp[:, i, :N], pg[:, :N], AF.Silu, r=[pgB], w=[ftmpB[i]])
            op("dve", "tensor_tensor", aT[:, j, :N], ftmp[:, i, :N], pu[:, :N], ALU.mult, r=[ftmpB[i], puB], w=[aTb[j]])
        for mo in range(KC):
            p1, p1B = PS1()
            for hf in range(2):
                wt, wb = WS.get(("ffo", l, mo * 2 + hf))
                wv = wt[:, 0:2816].rearrange("p (j c) -> p j c", c=128)
                for jj in range(22):
                    j = hf * 22 + jj
                    op("pe", "matmul", p1[:, :N], wv[:, jj, :], aT[:, j, :N], start=(j == 0), stop=(j == NJ - 1),
                       r=[wb, aTb[j]], w=[p1B])
            resid_add(l, 5, N, sample, mo, p1[:, :N], p1B)
        checkpoint("ffn")
        tap("x2_%s%d_%d" % (kind, ci, l), xT[:], xTb)
        if l == n_layers - 1:
            pss, pssB = PS1()
            for kc in range(KC):
                i = kc % 2
                op("act", "activation", sqb[:, i, :N], xT[:, kc, :N], AF.Square, r=[xTb[kc]], w=[sqB[i]])
                op("pe", "matmul", pss[:, :N], ones_b, sqb[:, i, :N], start=(kc == 0), stop=(kc == KC - 1),
                   r=[sqB[i], cstB], w=[pssB])
            op("act", "activation", rstd[:, :N], pss[:, :N], AF.Sqrt, bias=smallc[:, 0:1], scale=1.0 / D, r=[pssB, cstB], w=[rstdB])
            op("dve", "reciprocal", rstd[:, :N], rstd[:, :N], r=[rstdB], w=[rstdB])
            for kc in range(KC):
                op("dve", "scalar_tensor_tensor", xT[:, kc, :N], xT[:, kc, :N], gfin[:, kc:kc + 1], rstd[:, :N], ALU.mult, ALU.mult,
                   r=[xTb[kc], rstdB, vecB], w=[xTb[kc]])
            nblk = 1 if sample else NT // 128
            for tb in range(nblk):
                nq = NS if sample else 128
                for k0 in range(0, KC, 4):
                    p1, p1B = PS1()
                    pv = p1[:, :].rearrange("p (c f) -> p c f", f=128)
                    for c in range(4):
                        op("pe", "transpose", pv[0:nq, c, :], xT[:, k0 + c, tb * 128:tb * 128 + nq], ident_f,
                           r=[xTb[k0 + c], cstB], w=[p1B])
                    dst_ = otok[0:nq, k0 * 128:(k0 + 4) * 128].rearrange("p (c f) -> p c f", f=128)
                    if (k0 // 4) % 2 == 0:
                        op("act", "activation", dst_, pv[0:nq], AF.Copy, r=[p1B], w=[otokB])
                    else:
                        op("dve", "tensor_copy", dst_, pv[0:nq], r=[p1B], w=[otokB])
                if sample:
                    S.dma("sp", y_s, otok[0:NS, :], r=[otokB], w=[dram_out_buf])
                else:
                    S.dma("sp", y_p[ci * NT + tb * 128:ci * NT + (tb + 1) * 128, :], otok[:, :], r=[otokB], w=[dram_out_buf])

    try:
        if stopped:
            raise _Stop()
        checkpoint("mods")
        for (kind, ci, l) in passes:
            run_pass(kind, ci, l)
    except _Stop:
        pass

    S.emit(es)
    es.close()
    return nc


def _consts():
    cst = np.zeros((128, NCST), np.float32)
    cst[:, C_ID:C_ID + 128] = np.eye(128, dtype=np.float32)
    blk = np.zeros((128, 128), np.float32)
    blk[0:64, 0:64] = 1.0
    blk[64:128, 64:128] = 1.0
    cst[:, C_BLK:C_BLK + 128] = blk
    p = np.arange(128)[:, None] % 64
    f = np.arange(64)[None, :]
    cst[:, C_MLT:C_MLT + 64] = (f < p)
    cst[:, C_MUT:C_MUT + 64] = (p < f)
    cst[:, C_MUE:C_MUE + 64] = (p <= f)
    i = np.arange(128)[:, None]
    j = np.arange(256)[None, :]
    dist = (128 + i - j).astype(np.float32)
    valid = (dist >= 0) & (dist <= 128)
    dm = np.where(valid, dist, BIG).astype(np.float32)
    cst[:, C_DM:C_DM + 256] = dm
    dm0 = dm.copy()
    dm0[:, 0:128] = BIG
    cst[:, C_DM0:C_DM0 + 256] = dm0
    ds = np.concatenate([128.0 - np.arange(128), [0.0]]).astype(np.float32)
    cst[:, C_DS:C_DS + 129] = ds[None, :]
    return cst


def _fm(v, ncol):
    return np.ascontiguousarray(v.reshape(ncol, 128).T)


def _pc_layout(a):
    out = np.zeros(a.shape[:-1] + (NPC * 128,), np.float32)
    out[..., 0:3072] = a[..., 0:3072]
    out[..., 3072:3136] = a[..., 3072:3136]
    out[..., 3136:3200] = a[..., 3136:3200]
    out[..., 3200:3360] = a[..., 3200:3360]
    return out


def _prep_weights(inp):
    w = {}
    L_ = L
    wada = inp["w_ada"].reshape(L_, KC, 128, 48, 2, 128).transpose(0, 3, 2, 4, 1, 5)
    w["wada"] = np.ascontiguousarray(wada).reshape(L_, 48, 128, 4096)
    w_in = inp["w_in"]
    wperm = np.zeros((L_, D, 44 * 128), np.float32)
    wperm[:, :, 0:1024] = w_in[:, :, 0:1024]
    for g in range(4):
        wkg = w_in[:, :, 1024 + g * 64:1024 + (g + 1) * 64]
        base = 1024 + g * 256
        wperm[:, :, base:base + 64] = wkg
        wperm[:, :, base + 128 + 64:base + 256] = wkg
    wperm[:, :, 2048:2048 + 3360] = w_in[:, :, 1536:1536 + 3360]
    win = wperm.reshape(L_, KC, 128, 22, 2, 128).transpose(0, 3, 2, 4, 1, 5)
    w["win"] = np.ascontiguousarray(win).reshape(L_, 22, 128, 4096)
    wk = w_in[:, :, 1024:1280].reshape(L_, KC, 128, 256).transpose(0, 2, 1, 3).reshape(L_, 128, 4096)
    wv = w_in[:, :, 1280:1536].reshape(L_, KC, 128, 256).transpose(0, 2, 1, 3).reshape(L_, 128, 4096)
    w["wkvt"] = np.ascontiguousarray(np.stack([wk, wv], axis=1))
    wout = inp["w_out"].reshape(L_, KC, 128, 8, 2, 128).transpose(0, 3, 2, 4, 1, 5)
    w["wout"] = np.ascontiguousarray(wout).reshape(L_, 8, 128, 4096)
    wfi = inp["w_ffn_in"].reshape(L_, KC, 128, 2, NJ, 128).transpose(0, 4, 2, 3, 1, 5)
    w["wffi"] = np.ascontiguousarray(wfi).reshape(L_, NJ, 128, 4096)
    wfo = inp["w_ffn_out"].reshape(L_, 2, 22, 128, KC, 128).transpose(0, 4, 1, 3, 2, 5)
    w["wffo"] = np.ascontiguousarray(wfo).reshape(L_, 32, 128, 2816)
    lup = np.zeros((L_, 128, 4096), np.float32)
    lup[:, 0:64, 0:1024] = inp["decay_up"]
    lup[:, 64:128, 1024:2048] = inp["iclr_up"]
    lup[:, :, 2048:3072] = inp["gate_up"][:, 0:128]
    lup[:, 0:32, 3072:4096] = inp["gate_up"][:, 128:160]
    w["lup"] = lup
    vecs = np.zeros((128, L_, NV), np.float32)
    for l in range(L_):
        vecs[:, l, V_GMIX:V_GMIX + 16] = _fm(inp["g_norm_mix"][l], 16)
        vecs[:, l, V_GFFN:V_GFFN + 16] = _fm(inp["g_norm_ffn"][l], 16)
        vecs[:, l, V_MIX:V_MIX + NPC] = _fm(_pc_layout(inp["mix_shift"][l]), NPC)
        vecs[:, l, V_W0:V_W0 + 8] = _fm(inp["decay_w0"][l], 8)
        vecs[:, l, V_A0:V_A0 + 8] = _fm(inp["iclr_a0"][l], 8)
        vecs[:, l, V_KK:V_KK + 8] = _fm(inp["k_k"][l], 8)
        vecs[:, l, V_KA:V_KA + 8] = _fm(inp["k_a"][l], 8)
        vecs[:, l, V_RK:V_RK + 8] = _fm(inp["r_k"][l].reshape(-1), 8)
        vecs[:, l, V_LNW:V_LNW + 8] = _fm(inp["ln_x_w"][l], 8)
        vecs[:, l, V_LNB:V_LNB + 8] = _fm(inp["ln_x_b"][l], 8)
        vecs[:, l, V_BADA:V_BADA + 96] = _fm(inp["b_ada"][l], 96)
    w["vecs"] = vecs
    w["sinks"] = np.ascontiguousarray(np.broadcast_to(inp["attn_sinks"][None], (128, L_, NH))).astype(np.float32)
    w["cst"] = _consts()
    w["gfin"] = _fm(inp["g_norm_final"], KC)
    return w


def _core_inputs(inp, shared, core):
    b = core % 4
    ss = slice(core * NS, (core + 1) * NS)
    m = dict(shared)
    m["xpT"] = np.ascontiguousarray(inp["x_prompt"][b].T).reshape(KC, 128, SEQ)
    xs = inp["x_sample"][ss, 0, :]
    m["xsT"] = np.ascontiguousarray(xs.reshape(NS, KC, 128).transpose(2, 1, 0))
    c5 = np.concatenate([inp["c_prompt"][b:b + 1], inp["c_sample"][ss]], axis=0)
    m["c5T"] = np.ascontiguousarray(c5.reshape(5, KC, 128).transpose(2, 1, 0))
    ck = inp["cache_k"][:, ss]
    ckT = ck.transpose(0, 1, 4, 3, 2)
    ckz = np.zeros((L, NS, 128, 4, 2, 128), np.float32)
    ckz[:, :, 0:64, :, 0, :] = ckT
    ckz[:, :, 64:128, :, 1, :] = ckT
    m["ckT"] = ckz.reshape(L, NS, 128, 8, 128)
    m["ck"] = np.ascontiguousarray(ck.reshape(L, NS, 128, 256))
    m["cv"] = np.ascontiguousarray(inp["cache_v"][:, ss].reshape(L, NS, 128, 256))
    stw = inp["state_wkv"][:, ss]
    stT = stw.reshape(L, NS, 8, 2, 64, 64).transpose(0, 1, 3, 5, 2, 4)
    m["stT"] = np.ascontiguousarray(stT).reshape(L, NS, 128, 8, 64)
    sh = _pc_layout(inp["state_shift"][:, ss])
    m["shT"] = np.ascontiguousarray(sh.reshape(L, NS, NPC, 128).transpose(0, 3, 2, 1))
    return m


def _unpc(a):
    return np.concatenate([a[..., 0:3200], a[..., 3200:3360]], axis=-1)


_NC_CACHE = {}


def kernel(**inputs):
    inp = {k: np.asarray(v) for k, v in inputs.items()}
    shared = _prep_weights(inp)
    in_maps = [_core_inputs(inp, shared, c) for c in range(8)]
    if "nc" not in _NC_CACHE:
        _NC_CACHE["nc"] = build()
    nc = _NC_CACHE["nc"]
    res = run_bass_kernel_spmd(nc, in_maps, core_ids=list(range(8)))
    R = res.results
    y_prompt = np.stack([R[b]["y_p"] for b in range(4)], 0).astype(np.float32)
    y_sample = np.concatenate([R[c]["y_s"] for c in range(8)], 0).reshape(32, 1, D).astype(np.float32)
    nkp = np.stack([R[b]["nk_p"] for b in range(4)], 1).reshape(L, 4, 128, 4, 64)
    nvp = np.stack([R[b]["nv_p"] for b in range(4)], 1).reshape(L, 4, 128, 4, 64)
    nwp = np.stack([R[b]["nwkv_p"] for b in range(4)], 1)
    nsp = _unpc(np.stack([R[b]["nsh_p"] for b in range(4)], 1).reshape(L, 4, NPC * 128))
    nks = np.concatenate([R[c]["nk_s"] for c in range(8)], 1).reshape(L, 32, 128, 4, 64)
    nvs = np.concatenate([R[c]["nv_s"] for c in range(8)], 1).reshape(L, 32, 128, 4, 64)
    nws = np.concatenate([R[c]["nwkv_s"] for c in range(8)], 1)
    nss = _unpc(np.concatenate([R[c]["nsh_s"] for c in range(8)], 1).reshape(L, 32, NPC * 128))
    f = lambda a: np.ascontiguousarray(a, dtype=np.float32)
    return (f(y_prompt), f(y_sample), f(nkp), f(nvp), f(nwp), f(nsp), f(nks), f(nvs), f(nws), f(nss))
```
